# Optimizing a Trainium2 kernel written in Bass

```python
import jax
import jax.numpy as jnp
from jax import lax
import numpy as np

D_MODEL = 1024
BATCH = 16
SEQ = 256
DEPTH = 4
DEC_BATCH = 8
DEC_SEQ = 4096
PAST_LEN = 256

GRID_W = 64
EPS = 1e-6
ROPE_BASE = 10000.0
BLOCK = 128
NEG_INF = -1e30
D_RNN = 1024
RNN_BLOCKS = 8
RNN_BW = D_RNN // RNN_BLOCKS
CONV_W = 4
CONV_LEFT = (CONV_W - 1) // 2
LRU_C = 8.0
MLA_HEADS = 8
MLA_NOPE = 64
MLA_ROPE = 32
MLA_V = 64
Q_LORA = 384
KV_LORA = 256
MLA_WIDTH = MLA_HEADS * MLA_V
SWA_HEADS = 8
SWA_KV_HEADS = 2
SWA_GROUPS = SWA_HEADS // SWA_KV_HEADS
SWA_HD = 64
WINDOW = 128
SWA_WIDTH = SWA_HEADS * SWA_HD
N_BRANCH = 3
D_IN = (2 * D_RNN + Q_LORA + KV_LORA + MLA_ROPE + MLA_WIDTH
        + SWA_HEADS * SWA_HD + 2 * SWA_KV_HEADS * SWA_HD + SWA_WIDTH + N_BRANCH * D_MODEL)

kernel_name = "hybrid_flow_backbone_step"


def _in_split_points():
    sizes = (D_RNN, D_RNN, Q_LORA, KV_LORA, MLA_ROPE, MLA_WIDTH, SWA_HEADS * SWA_HD,
             SWA_KV_HEADS * SWA_HD, SWA_KV_HEADS * SWA_HD, SWA_WIDTH, N_BRANCH * D_MODEL)
    pts, acc = [], 0
    for s in sizes[:-1]:
        acc += s
        pts.append(acc)
    return pts


def rmsnorm(x, g):
    xf = x.astype(jnp.float32)
    y = xf * lax.rsqrt(jnp.mean(xf * xf, axis=-1, keepdims=True) + EPS)
    return (y * g.astype(jnp.float32)).astype(x.dtype)


def axial_rope_tables(n_tok, rot_dim):
    rows = n_tok // GRID_W
    row = jnp.repeat(jnp.arange(rows, dtype=jnp.float32), GRID_W)
    col = jnp.tile(jnp.arange(GRID_W, dtype=jnp.float32), rows)
    half = rot_dim // 2
    freqs = ROPE_BASE ** (-jnp.arange(0, half, 2, dtype=jnp.float32) / half)
    ang_r = row[:, None] * freqs[None, :]
    ang_c = col[:, None] * freqs[None, :]
    ang = jnp.concatenate([ang_r, ang_r, ang_c, ang_c], axis=-1)
    return jnp.cos(ang), jnp.sin(ang)


def apply_rope(x, cos, sin):
    r = x.shape[-1]
    q = r // 4
    rot = jnp.concatenate([-x[..., q:2 * q], x[..., :q], -x[..., 3 * q:], x[..., 2 * q:3 * q]], axis=-1)
    shape = (1, x.shape[1]) + (1,) * (x.ndim - 3) + (r,)
    return x * cos.reshape(shape).astype(x.dtype) + rot * sin.reshape(shape).astype(x.dtype)


def centred_conv(x, w, b):
    L = x.shape[1]
    xp = jnp.pad(x, ((0, 0), (CONV_LEFT, CONV_W - 1 - CONV_LEFT), (0, 0)))
    acc = xp[:, 0:L] * w[0]
    for k in range(1, CONV_W):
        acc = acc + xp[:, k:k + L] * w[k]
    return acc + b


def rglru_scan(x, h0, w_a, b_a, w_i, b_i, lam, reverse):
    B, L, _ = x.shape
    xb = x.reshape(B, L, RNN_BLOCKS, RNN_BW)
    r = jax.nn.sigmoid((jnp.einsum("blnc,ncd->blnd", xb, w_a).reshape(B, L, D_RNN) + b_a).astype(jnp.float32))
    i = jax.nn.sigmoid((jnp.einsum("blnc,ncd->blnd", xb, w_i).reshape(B, L, D_RNN) + b_i).astype(jnp.float32))
    log_a = LRU_C * r * jax.nn.log_sigmoid(lam.astype(jnp.float32))
    a = jnp.exp(log_a)
    u = jnp.sqrt(-jnp.expm1(2.0 * log_a)) * i * x.astype(jnp.float32)

    def combine(e1, e2):
        return (e1[0] * e2[0], e2[0] * e1[1] + e2[1])

    a_cum, h = lax.associative_scan(combine, (a, u), reverse=reverse, axis=1)
    return (h + a_cum * h0.astype(jnp.float32)[:, None, :]).astype(x.dtype)


def rglru_branch(x_rnn, h0_f, h0_b, p):
    xc = centred_conv(x_rnn, p["conv_w"], p["conv_b"])
    h_f = rglru_scan(xc, h0_f, p["lru_wa"][0], p["lru_ba"][0], p["lru_wi"][0], p["lru_bi"][0], p["lru_lam"][0], False)
    h_b = rglru_scan(xc, h0_b, p["lru_wa"][1], p["lru_ba"][1], p["lru_wi"][1], p["lru_bi"][1], p["lru_lam"][1], True)
    return h_f, h_b


def mla_q(c_q, p):
    B, L = c_q.shape[:2]
    q = (rmsnorm(c_q, p["mla_q_norm"]) @ p["mla_w_uq"]).reshape(B, L, MLA_HEADS, MLA_NOPE + MLA_ROPE)
    return q[..., :MLA_NOPE], q[..., MLA_NOPE:]


def mla_kv_up(ckv, p):
    B, L = ckv.shape[:2]
    kv = (ckv @ p["mla_w_ukv"]).reshape(B, L, MLA_HEADS, MLA_NOPE + MLA_V)
    return kv[..., :MLA_NOPE], kv[..., MLA_NOPE:]


def mla_attend(q_nope, q_rope, k_nope, k_rope, v):
    B, Lq = q_nope.shape[:2]
    nb = Lq // BLOCK
    scale = (MLA_NOPE + MLA_ROPE) ** -0.5
    qn = q_nope.reshape(B, nb, BLOCK, MLA_HEADS, MLA_NOPE).swapaxes(0, 1)
    qr = q_rope.reshape(B, nb, BLOCK, MLA_HEADS, MLA_ROPE).swapaxes(0, 1)

    def block(args):
        qn_b, qr_b = args
        s = jnp.einsum("bqhd,bkhd->bhqk", qn_b, k_nope) + jnp.einsum("bqhr,bkr->bhqk", qr_b, k_rope)
        pr = jax.nn.softmax(s.astype(jnp.float32) * scale, axis=-1).astype(v.dtype)
        return jnp.einsum("bhqk,bkhd->bqhd", pr, v)

    o = lax.map(block, (qn, qr))
    return o.swapaxes(0, 1).reshape(B, Lq, MLA_WIDTH)


def sink_softmax(s, sink):
    sk = jnp.broadcast_to(sink.astype(jnp.float32).reshape(1, SWA_KV_HEADS, SWA_GROUPS, 1, 1), s.shape[:-1] + (1,))
    return jax.nn.softmax(jnp.concatenate([s, sk], axis=-1), axis=-1)[..., :-1]


def swa_context(q, k, v, sink):
    B, L = q.shape[:2]
    nb = L // BLOCK
    scale = SWA_HD ** -0.5
    qb = q.reshape(B, nb, BLOCK, SWA_KV_HEADS, SWA_GROUPS, SWA_HD).swapaxes(0, 1)

    def block(q_b):
        s = jnp.einsum("bqhgd,bkhd->bhgqk", q_b, k).astype(jnp.float32) * scale
        pr = sink_softmax(s, sink).astype(v.dtype)
        return jnp.einsum("bhgqk,bkhd->bqhgd", pr, v)

    o = lax.map(block, qb)
    return o.swapaxes(0, 1).reshape(B, L, SWA_WIDTH)


def swa_latent(q, k, v, k_ctx, v_ctx, sink):
    B, N = q.shape[:2]
    Lc = k_ctx.shape[1]
    nb = N // BLOCK
    scale = SWA_HD ** -0.5
    pad = ((0, 0), (BLOCK, BLOCK), (0, 0), (0, 0))
    kp = jnp.pad(k, pad)
    vp = jnp.pad(v, pad)
    qb = q.reshape(B, nb, BLOCK, SWA_KV_HEADS, SWA_GROUPS, SWA_HD).swapaxes(0, 1)
    offs = jnp.arange(3 * BLOCK)
    band = jnp.abs(jnp.arange(BLOCK)[:, None] + BLOCK - offs[None, :]) <= WINDOW
    ctx_ok = jnp.ones((BLOCK, Lc), dtype=bool)

    def block(args):
        n, q_b = args
        start = n * BLOCK
        k_b = lax.dynamic_slice_in_dim(kp, start, 3 * BLOCK, axis=1)
        v_b = lax.dynamic_slice_in_dim(vp, start, 3 * BLOCK, axis=1)
        kpos = start - BLOCK + offs
        valid = jnp.concatenate([band & ((kpos >= 0) & (kpos < N))[None, :], ctx_ok], axis=1)
        k_all = jnp.concatenate([k_b, k_ctx], axis=1)
        v_all = jnp.concatenate([v_b, v_ctx], axis=1)
        s = jnp.einsum("bqhgd,bkhd->bhgqk", q_b, k_all).astype(jnp.float32) * scale
        s = jnp.where(valid, s, NEG_INF)
        pr = sink_softmax(s, sink).astype(v.dtype)
        return jnp.einsum("bhgqk,bkhd->bqhgd", pr, v_all)

    o = lax.map(block, (jnp.arange(nb), qb))
    return o.swapaxes(0, 1).reshape(B, N, SWA_WIDTH)


def mix_inputs(x, cond, p):
    mod = jax.nn.silu(cond) @ p["w_mod"] + p["b_mod"]
    shift, scale, gate = jnp.split(mod, 3, axis=-1)
    h = rmsnorm(x, p["g_norm"]) * (1 + scale[:, None, :]) + shift[:, None, :]
    parts = jnp.split(h @ p["w_in"], _in_split_points(), axis=-1)
    return gate, parts


def merge_branches(y_rnn, y_mla, y_swa, g_rnn, g_mla, g_swa, merge_logits, p):
    B, L, _ = y_rnn.shape
    m = jax.nn.sigmoid(merge_logits.astype(jnp.float32)).astype(y_rnn.dtype).reshape(B, L, N_BRANCH, D_MODEL)
    u = (m[:, :, 0] * ((y_rnn * jax.nn.silu(g_rnn)) @ p["w_br_rnn"])
         + m[:, :, 1] * ((y_mla * jax.nn.silu(g_mla)) @ p["w_br_mla"])
         + m[:, :, 2] * ((y_swa * jax.nn.silu(g_swa)) @ p["w_br_swa"]))
    return u @ p["w_out"]


def context_layer(x, cond, p):
    B, L, _ = x.shape
    gate, (x_rnn, g_rnn, c_q, c_kv, k_rope, g_mla, q_s, k_s, v_s, g_swa, merge_logits) = mix_inputs(x, cond, p)
    zeros = jnp.zeros((B, D_RNN), x.dtype)
    h_f, h_b = rglru_branch(x_rnn, zeros, zeros, p)
    y_rnn = h_f + h_b
    ckv = rmsnorm(c_kv, p["mla_kv_norm"])
    q_nope, q_rope = mla_q(c_q, p)
    k_nope, v_m = mla_kv_up(ckv, p)
    y_mla = mla_attend(q_nope, q_rope, k_nope, k_rope, v_m)
    q = q_s.reshape(B, L, SWA_KV_HEADS, SWA_GROUPS, SWA_HD)
    k = k_s.reshape(B, L, SWA_KV_HEADS, SWA_HD)
    v = v_s.reshape(B, L, SWA_KV_HEADS, SWA_HD)
    y_swa = swa_context(q, k, v, p["swa_sink"])
    out = merge_branches(y_rnn, y_mla, y_swa, g_rnn, g_mla, g_swa, merge_logits, p)
    x = x + gate[:, None, :] * out
    h_state = jnp.stack([h_f[:, -1], h_b[:, 0]], axis=1)
    return x, ckv, k_rope, k, v, h_state


def latent_layer(x, cond, ckv_ctx, krope_ctx, k_ctx, v_ctx, h_ctx, cos_m, sin_m, cos_s, sin_s, p):
    B, N, _ = x.shape
    gate, (x_rnn, g_rnn, c_q, c_kv, k_rope, g_mla, q_s, k_s, v_s, g_swa, merge_logits) = mix_inputs(x, cond, p)
    h_f, h_b = rglru_branch(x_rnn, h_ctx[:, 0], h_ctx[:, 1], p)
    y_rnn = h_f + h_b
    ckv = rmsnorm(c_kv, p["mla_kv_norm"])
    q_nope, q_rope = mla_q(c_q, p)
    q_rope = apply_rope(q_rope, cos_m, sin_m)
    k_rope_l = apply_rope(k_rope, cos_m, sin_m)
    k_nope_l, v_l = mla_kv_up(ckv, p)
    k_nope_c, v_c = mla_kv_up(ckv_ctx, p)
    y_mla = mla_attend(q_nope, q_rope,
                       jnp.concatenate([k_nope_c, k_nope_l], axis=1),
                       jnp.concatenate([krope_ctx, k_rope_l], axis=1),
                       jnp.concatenate([v_c, v_l], axis=1))
    q = apply_rope(q_s.reshape(B, N, SWA_KV_HEADS, SWA_GROUPS, SWA_HD), cos_s, sin_s)
    k = apply_rope(k_s.reshape(B, N, SWA_KV_HEADS, SWA_HD), cos_s, sin_s)
    v = v_s.reshape(B, N, SWA_KV_HEADS, SWA_HD)
    y_swa = swa_latent(q, k, v, k_ctx, v_ctx, p["swa_sink"])
    out = merge_branches(y_rnn, y_mla, y_swa, g_rnn, g_mla, g_swa, merge_logits, p)
    return x + gate[:, None, :] * out


def setup_inputs(seed: int = 0) -> dict:
    key = jax.random.key(seed)
    ks = iter(jax.random.split(key, 40))

    def nrm(shape, scale=1.0):
        return jax.random.normal(next(ks), shape, jnp.float32) * scale

    def gain(shape):
        return 1.0 + nrm(shape, 0.02)

    a8 = jax.random.uniform(next(ks), (DEPTH, 2, D_RNN), jnp.float32, minval=0.9, maxval=0.999)
    a = a8 ** (1.0 / LRU_C)
    lam = jnp.log(a) - jnp.log1p(-a)
    return {
        "x_prompt": nrm((BATCH, SEQ, D_MODEL)),
        "x_sample": nrm((DEC_BATCH, DEC_SEQ, D_MODEL)),
        "cache_mla_ckv": nrm((DEC_BATCH, DEPTH, PAST_LEN, KV_LORA)),
        "cache_mla_krope": nrm((DEC_BATCH, DEPTH, PAST_LEN, MLA_ROPE)),
        "cache_swa_k": nrm((DEC_BATCH, DEPTH, PAST_LEN, SWA_KV_HEADS, SWA_HD)),
        "cache_swa_v": nrm((DEC_BATCH, DEPTH, PAST_LEN, SWA_KV_HEADS, SWA_HD)),
        "state_rglru": nrm((DEC_BATCH, DEPTH, 2, D_RNN), 0.5),
        "c": nrm((DEC_BATCH, D_MODEL)),
        "c_ctx": nrm((D_MODEL,)),
        "w_mod": nrm((DEPTH, D_MODEL, 3 * D_MODEL), 0.5 * D_MODEL ** -0.5),
        "b_mod": nrm((DEPTH, 3 * D_MODEL), 0.02),
        "g_norm": gain((DEPTH, D_MODEL)),
        "w_in": nrm((DEPTH, D_MODEL, D_IN), D_MODEL ** -0.5),
        "conv_w": nrm((DEPTH, CONV_W, D_RNN), CONV_W ** -0.5),
        "conv_b": nrm((DEPTH, D_RNN), 0.02),
        "lru_wa": nrm((DEPTH, 2, RNN_BLOCKS, RNN_BW, RNN_BW), RNN_BW ** -0.5),
        "lru_ba": nrm((DEPTH, 2, D_RNN), 0.02),
        "lru_wi": nrm((DEPTH, 2, RNN_BLOCKS, RNN_BW, RNN_BW), RNN_BW ** -0.5),
        "lru_bi": nrm((DEPTH, 2, D_RNN), 0.02),
        "lru_lam": lam,
        "mla_q_norm": gain((DEPTH, Q_LORA)),
        "mla_w_uq": nrm((DEPTH, Q_LORA, MLA_HEADS * (MLA_NOPE + MLA_ROPE)), Q_LORA ** -0.5),
        "mla_kv_norm": gain((DEPTH, KV_LORA)),
        "mla_w_ukv": nrm((DEPTH, KV_LORA, MLA_HEADS * (MLA_NOPE + MLA_V)), KV_LORA ** -0.5),
        "swa_sink": nrm((DEPTH, SWA_HEADS), 0.5),
        "w_br_rnn": nrm((DEPTH, D_RNN, D_MODEL), D_RNN ** -0.5),
        "w_br_mla": nrm((DEPTH, MLA_WIDTH, D_MODEL), MLA_WIDTH ** -0.5),
        "w_br_swa": nrm((DEPTH, SWA_WIDTH, D_MODEL), SWA_WIDTH ** -0.5),
        "w_out": nrm((DEPTH, D_MODEL, D_MODEL), D_MODEL ** -0.5),
        "final_norm": gain((D_MODEL,)),
    }


def reference(x_prompt, x_sample, cache_mla_ckv, cache_mla_krope, cache_swa_k, cache_swa_v, state_rglru,
              c, c_ctx, w_mod, b_mod, g_norm, w_in, conv_w, conv_b, lru_wa, lru_ba, lru_wi, lru_bi, lru_lam,
              mla_q_norm, mla_w_uq, mla_kv_norm, mla_w_ukv, swa_sink, w_br_rnn, w_br_mla, w_br_swa, w_out,
              final_norm):
    layers = [dict(w_mod=w_mod[l], b_mod=b_mod[l], g_norm=g_norm[l], w_in=w_in[l], conv_w=conv_w[l],
                   conv_b=conv_b[l], lru_wa=lru_wa[l], lru_ba=lru_ba[l], lru_wi=lru_wi[l], lru_bi=lru_bi[l],
                   lru_lam=lru_lam[l], mla_q_norm=mla_q_norm[l], mla_w_uq=mla_w_uq[l],
                   mla_kv_norm=mla_kv_norm[l], mla_w_ukv=mla_w_ukv[l], swa_sink=swa_sink[l],
                   w_br_rnn=w_br_rnn[l], w_br_mla=w_br_mla[l], w_br_swa=w_br_swa[l], w_out=w_out[l])
              for l in range(DEPTH)]

    xp = x_prompt
    cond_p = jnp.broadcast_to(c_ctx, (x_prompt.shape[0], D_MODEL))
    ckvs, krs, sks, svs, hs = [], [], [], [], []
    for l in range(DEPTH):
        xp, ckv, kr, sk, sv, hst = context_layer(xp, cond_p, layers[l])
        ckvs.append(ckv)
        krs.append(kr)
        sks.append(sk)
        svs.append(sv)
        hs.append(hst)
    y_prompt = rmsnorm(xp, final_norm)
    new_mla_ckv = jnp.stack(ckvs, axis=1)
    new_mla_krope = jnp.stack(krs, axis=1)
    new_swa_k = jnp.stack(sks, axis=1)
    new_swa_v = jnp.stack(svs, axis=1)
    new_rglru = jnp.stack(hs, axis=1)

    n_lat = x_sample.shape[1]
    cos_m, sin_m = axial_rope_tables(n_lat, MLA_ROPE)
    cos_s, sin_s = axial_rope_tables(n_lat, SWA_HD)
    xs = x_sample
    for l in range(DEPTH):
        xs = latent_layer(xs, c, cache_mla_ckv[:, l], cache_mla_krope[:, l], cache_swa_k[:, l], cache_swa_v[:, l],
                          state_rglru[:, l], cos_m, sin_m, cos_s, sin_s, layers[l])
    y_sample = rmsnorm(xs, final_norm)
    return (y_prompt, y_sample, new_mla_ckv, new_mla_krope, new_swa_k, new_swa_v, new_rglru)
```

```python
import contextlib
import numpy as np
import ml_dtypes
import concourse.bass as bass
import concourse.mybir as mybir
from concourse.bass_utils import run_bass_kernel_spmd

F32 = mybir.dt.float32
BF16 = mybir.dt.bfloat16
AF = mybir.ActivationFunctionType
ALU = mybir.AluOpType

D = 1024
DEPTH = 4
NS = 4096
NPR = 256
T = NS + 2 * NPR
TS = 512
NT = T // TS
KOFF = 256
NKEY = T + KOFF
EPS = 1e-6
D_IN = 7584
C_XR, C_GR, C_CQ, C_CKV, C_KR, C_GM, C_QS, C_KS, C_VS, C_GS, C_MG = (
    0, 1024, 2048, 2432, 2688, 2720, 3232, 3744, 3872, 4000, 4512)
MLA_SCALE = 96 ** -0.5
SWA_SCALE = 64 ** -0.5

COMPUTE = ("pe", "act", "dve", "pool")
ENGS = ("pe", "act", "dve", "pool", "sp")


class Region:
    __slots__ = ("name", "writers", "readers", "prev_readers")

    def __init__(self, name):
        self.name = name
        self.writers = []
        self.readers = []
        self.prev_readers = []


class Op:
    __slots__ = ("id", "eng", "fn", "is_dma", "signal", "token", "slot", "idx", "waits", "barrier")


class Sched:
    def __init__(self, nslots=90):
        self.ops = []
        self.per_eng = {e: [] for e in ENGS}
        self.known = {e: {} for e in ENGS}
        self.regions = {}
        self.nslots = nslots
        self.slot_count = [0] * nslots
        self.slot_of = {}
        self.nsw = 12
        self.free_slots = list(range(self.nsw, nslots))
        self.free_sw = list(range(self.nsw))
        self.last = {e: None for e in COMPUTE}
        self.max_slots_used = 0
        self.dj_names = set()

    def region(self, name):
        r = self.regions.get(name)
        if r is None:
            r = Region(name)
            self.regions[name] = r
        return r

    def add(self, eng, fn, reads=(), writes=(), dma=False):
        op = Op()
        op.id = len(self.ops)
        op.eng = eng
        op.fn = fn
        op.is_dma = dma
        op.signal = False
        op.token = None
        op.slot = None
        op.waits = []
        op.barrier = None
        reads = [self.region(r) for r in dict.fromkeys(reads)]
        writes = [self.region(r) for r in dict.fromkeys(writes)]
        if dma:
            assert len(writes) == 1, "DMA op must write exactly one region"
            nm = writes[0].name
            if nm.startswith("d_") and reads and not reads[0].name.startswith("d_"):
                nm = "src:" + reads[0].name
            if nm not in self.slot_of:
                fl = self.free_sw if eng == "pool" else self.free_slots
                assert fl, "out of DMA semaphore slots"
                self.slot_of[nm] = fl.pop(0)
                self.max_slots_used = max(self.max_slots_used, len(self.slot_of))
            op.slot = self.slot_of[nm]
            self.slot_count[op.slot] += 1
            op.idx = self.slot_count[op.slot]
        else:
            op.idx = len(self.per_eng[eng]) + 1
        deps = set()
        raw = set()
        for r in reads:
            deps.update(r.writers)
            raw.update(r.writers)
        for r in writes:
            if r.name not in self.dj_names:
                deps.update(r.writers)
            deps.update(r.readers)
            deps.update(r.prev_readers)
        kn = self.known[eng]
        for d in sorted(deps, reverse=True):
            dop = self.ops[d]
            if (not dop.is_dma) and (not dma) and dop.eng == eng:
                if eng == "pe" or d not in raw:
                    continue
            if dop.is_dma:
                key = ("slot", dop.slot)
                idx = self.slot_count[dop.slot]
                if dma and op.slot == dop.slot:
                    idx -= 1
                if kn.get(key, 0) >= idx:
                    continue
                kn[key] = idx
                op.waits.append(("slot", dop.slot, idx))
                continue
            key = ("eng", dop.eng)
            if kn.get(key, 0) >= dop.idx:
                continue
            kn[key] = dop.idx
            dop.signal = True
            op.waits.append(dop)
        for r in reads:
            r.readers.append(op.id)
        for r in writes:
            if r.name in self.dj_names:
                if r.readers:
                    r.prev_readers = r.readers
                    r.readers = []
                    r.writers = [op.id]
                else:
                    r.writers.append(op.id)
            else:
                r.writers = [op.id]
                r.readers = []
                r.prev_readers = []
        self.ops.append(op)
        self.per_eng[eng].append(op)
        if not dma:
            self.last[eng] = op
        return op

    def barrier(self):
        b = Op()
        b.id = -1
        b.barrier = ([self.last[e] for e in COMPUTE if self.last[e] is not None], list(self.slot_count))
        for o in b.barrier[0]:
            o.signal = True
        for e in ENGS:
            self.per_eng[e].append(b)
            kn = self.known[e]
            for o in b.barrier[0]:
                kn[("eng", o.eng)] = o.idx
            for k, c in enumerate(self.slot_count):
                kn[("slot", k)] = c
        self.regions = {}
        self.slot_of = {}
        self.free_slots = list(range(self.nsw, self.nslots))
        self.free_sw = list(range(self.nsw))

    def emit(self, nc, es, final_slots=True):
        eng_sem = {e: es.enter_context(nc.semaphore(f"sem_{e}")) for e in COMPUTE}
        slot_sem = [es.enter_context(nc.semaphore(f"dsem{k}")) for k in range(self.nslots)]
        cnt = {e: 0 for e in COMPUTE}
        for op in self.ops:
            if op.is_dma:
                op.token = (slot_sem[op.slot], 16 * op.idx)
            elif op.signal:
                cnt[op.eng] += 1
                op.token = (eng_sem[op.eng], cnt[op.eng])
        self.stats = dict(signals=dict(cnt), nops={e: len(v) for e, v in self.per_eng.items()},
                          max_slots=self.max_slots_used, max_dma_cnt=max(self.slot_count))
        engobj = {"pe": nc.tensor, "act": nc.scalar, "dve": nc.vector, "pool": nc.gpsimd, "sp": nc.sync}
        final_counts = list(self.slot_count)
        nw = {e: 0 for e in ENGS}

        def run(engname, e):
            ek = {}

            def wait(s, v):
                if ek.get(s.num, 0) >= v:
                    return
                ek[s.num] = v
                e.wait_ge(s, v)
                nw[engname] += 1

            for op in self.per_eng[engname]:
                if op.barrier is not None:
                    lasts, slots = op.barrier
                    for o in lasts:
                        wait(*o.token)
                    for k, c in enumerate(slots):
                        if c:
                            wait(slot_sem[k], 16 * c)
                    continue
                for d in op.waits:
                    if isinstance(d, tuple):
                        wait(slot_sem[d[1]], 16 * d[2])
                    else:
                        wait(*d.token)
                ins = op.fn(e)
                if op.is_dma:
                    ins.then_inc(op.token[0], 16)
                elif op.signal:
                    ins.then_inc(op.token[0], 1)
            if engname == "sp":
                for k, c in enumerate(final_counts):
                    if c:
                        wait(slot_sem[k], 16 * c)

        with nc.Block() as block:
            @block.sync
            def _(e):
                run("sp", e)

            @block.scalar
            def _(e):
                run("act", e)

            @block.vector
            def _(e):
                run("dve", e)

            @block.gpsimd
            def _(e):
                run("pool", e)

            @block.tensor
            def _(e):
                run("pe", e)
        self.stats["nwaits"] = nw


class V:
    __slots__ = ("ap", "reg")

    def __init__(self, ap, reg):
        self.ap = ap
        self.reg = reg

    def __getitem__(self, k):
        return V(self.ap[k], self.reg)

    def rr(self, s, **kw):
        return V(self.ap.rearrange(s, **kw), self.reg)

    def sub(self, reg):
        return V(self.ap, reg)


def _esz(dt):
    return 4 if dt == F32 else 2


def _rope_tables(rot_dim, nrep):
    rows = NS // 64
    row = np.repeat(np.arange(rows, dtype=np.float32), 64)
    col = np.tile(np.arange(64, dtype=np.float32), rows)
    half = rot_dim // 2
    freqs = (np.float32(10000.0) ** (-np.arange(0, half, 2, dtype=np.float32) / np.float32(half))).astype(np.float32)
    ang_r = row[:, None] * freqs[None, :]
    ang_c = col[:, None] * freqs[None, :]
    ang = np.concatenate([ang_r, ang_r, ang_c, ang_c], axis=-1).astype(np.float32)
    cos = np.cos(ang).astype(np.float32)
    sin = np.sin(ang).astype(np.float32)
    q = rot_dim // 4
    sign = np.ones(rot_dim, np.float32)
    sign[0:q] = -1.0
    sign[2 * q:3 * q] = -1.0
    sin_s = sin * sign[None, :]
    cosT = np.tile(cos.T, (nrep, 1))
    sinT = np.tile(sin_s.T, (nrep, 1))
    perm = np.zeros(rot_dim, np.int64)
    for m in range(rot_dim):
        blk = m // q
        perm[m] = m + q if blk % 2 == 0 else m - q
    pm = np.zeros((128, 128), np.float32)
    for rep in range(nrep):
        for m in range(rot_dim):
            pm[rep * rot_dim + perm[m], rep * rot_dim + m] = 1.0
    return np.ascontiguousarray(cosT), np.ascontiguousarray(sinT), pm


def _consts():
    cos_s, sin_s, pm_s = _rope_tables(64, 2)
    cos_m, sin_m, pm_m = _rope_tables(32, 4)
    j = np.arange(128)[:, None]
    i = np.arange(128)[None, :]
    m_prev = np.tile((j >= i).astype(np.float32), (1, 4)).astype(ml_dtypes.bfloat16)
    m_next = np.tile((j <= i).astype(np.float32), (1, 4)).astype(ml_dtypes.bfloat16)
    sel = np.zeros((128, 64), np.float32)
    sel[64, :] = 1.0
    return dict(c_cos_s=cos_s, c_sin_s=sin_s, c_pm_s=pm_s, c_cos_m=cos_m, c_sin_m=sin_m, c_pm_m=pm_m,
                c_mprev=m_prev, c_mnext=m_next, c_ident=np.eye(128, dtype=np.float32), c_sel=sel)


class Builder:
    def __init__(self, dbg=None):
        self.dbg = dbg or {}
        self.nc = bass.Bass("TRN2", target_bir_lowering=False)
        self.S = Sched()
        self.es = contextlib.ExitStack()
        self.dram_in = {}
        self.dram_out = {}
        self.uid = 0

    def din(self, name, shape, dt=F32):
        t = self.nc.dram_tensor(name, list(shape), dt, kind="ExternalInput")
        self.dram_in[name] = t
        return V(t.ap(), "d_" + name)

    def dout(self, name, shape, dt=F32):
        t = self.nc.dram_tensor(name, list(shape), dt, kind="ExternalOutput")
        self.dram_out[name] = t
        self.S.dj_names.add("d_" + name)
        return V(t.ap(), "d_" + name)

    def dscr(self, name, shape, dt):
        if name in self.dbg.get("dump", ()):
            return self.dout(name, shape, dt)
        t = self.nc.dram_tensor(name, list(shape), dt, kind="Internal")
        self.S.dj_names.add("d_" + name)
        return V(t.ap(), "d_" + name)

    def arena_init(self, nbytes):
        self.arena_elems = nbytes // 2
        self.arena = self.es.enter_context(self.nc.sbuf_tensor("arena", [128, self.arena_elems], BF16))
        self.aoff = 0

    def tile(self, name, shape, dt=F32, dj=False):
        n = int(np.prod(shape))
        ne = n * (_esz(dt) // 2)
        ne = (ne + 15) // 16 * 16
        assert self.aoff + ne <= self.arena_elems, f"arena overflow at {name}: {self.aoff}+{ne}>{self.arena_elems}"
        ap = self.arena[:, self.aoff:self.aoff + n * (_esz(dt) // 2)]
        self.aoff += ne
        if dt == F32:
            ap = ap.bitcast(F32)
        if len(shape) == 2:
            ap = ap.rearrange("p (a b) -> p a b", a=shape[0])
        elif len(shape) == 3:
            ap = ap.rearrange("p (a b c) -> p a b c", a=shape[0], b=shape[1])
        self.uid += 1
        if dj:
            self.S.dj_names.add(f"{name}#{self.uid}")
        return V(ap, f"{name}#{self.uid}")

    def mark(self):
        return self.aoff

    def reset(self, m):
        self.aoff = m

    def _regs(self, *vs):
        return [v.reg for v in vs if isinstance(v, V)]

    def dma(self, out, in_, q="sp"):
        self.S.add(q, lambda e: e.dma_start(out=out.ap, in_=in_.ap), reads=[in_.reg], writes=[out.reg], dma=True)

    def mm(self, out, lhsT, rhs, start=True, stop=True):
        self.S.add("pe", lambda e: e.matmul(out.ap, lhsT=lhsT.ap, rhs=rhs.ap, start=start, stop=stop),
                   reads=[lhsT.reg, rhs.reg], writes=[out.reg])

    def transpose(self, out, in_, ident):
        self.S.add("pe", lambda e: e.transpose(out=out.ap, in_=in_.ap, identity=ident.ap),
                   reads=[in_.reg, ident.reg], writes=[out.reg])

    def act(self, out, in_, func, bias=None, scale=None):
        kw = {}
        rd = [in_.reg]
        if bias is not None:
            kw["bias"] = bias.ap if isinstance(bias, V) else bias
            rd += self._regs(bias)
        if scale is not None:
            kw["scale"] = scale.ap if isinstance(scale, V) else scale
            rd += self._regs(scale)
        self.S.add("act", lambda e: e.activation(out=out.ap, in_=in_.ap, func=func, **kw), reads=rd, writes=[out.reg])

    def _e(self, eng):
        if eng == "POOL":
            return "pool"
        if eng == "pool" and self.dbg.get("nopool", True):
            return "dve"
        return eng

    def tt(self, eng, out, in0, in1, op):
        eng = self._e(eng)
        self.S.add(eng, lambda e: e.tensor_tensor(out=out.ap, in0=in0.ap, in1=in1.ap, op=op),
                   reads=[in0.reg, in1.reg], writes=[out.reg])

    def ts(self, eng, out, in0, s1, s2, op0, op1=None):
        eng = self._e(eng)
        rd = [in0.reg] + self._regs(s1, s2)
        a1 = s1.ap if isinstance(s1, V) else s1
        a2 = s2.ap if isinstance(s2, V) else s2
        if op1 is None:
            self.S.add(eng, lambda e: e.tensor_scalar(out=out.ap, in0=in0.ap, scalar1=a1, scalar2=None, op0=op0),
                       reads=rd, writes=[out.reg])
        else:
            self.S.add(eng, lambda e: e.tensor_scalar(out=out.ap, in0=in0.ap, scalar1=a1, scalar2=a2, op0=op0, op1=op1),
                       reads=rd, writes=[out.reg])

    def stt(self, eng, out, in0, scalar, in1, op0, op1):
        eng = self._e(eng)
        rd = [in0.reg, in1.reg] + self._regs(scalar)
        sa = scalar.ap if isinstance(scalar, V) else scalar
        self.S.add(eng, lambda e: e.scalar_tensor_tensor(out=out.ap, in0=in0.ap, scalar=sa, in1=in1.ap, op0=op0, op1=op1),
                   reads=rd, writes=[out.reg])

    def copy(self, eng, out, in_):
        eng = self._e(eng)
        if eng == "act":
            self.S.add("act", lambda e: e.activation(out=out.ap, in_=in_.ap, func=AF.Copy), reads=[in_.reg], writes=[out.reg])
        else:
            self.S.add(eng, lambda e: e.tensor_copy(out=out.ap, in_=in_.ap), reads=[in_.reg], writes=[out.reg])

    def recip(self, out, in_):
        self.S.add("dve", lambda e: e.reciprocal(out=out.ap, in_=in_.ap), reads=[in_.reg], writes=[out.reg])

    def memset(self, eng, out, val):
        if eng == "act_ms":
            self.S.add("dve", lambda e: e.memset(out.ap, val), reads=[], writes=[out.reg])
            self.S.add("act", lambda e: e.activation(out=out.ap, in_=out.ap, func=AF.Copy), reads=[out.reg], writes=[out.reg])
            return
        self.S.add(eng, lambda e: e.memset(out.ap, val), reads=[], writes=[out.reg])

    def scan(self, out, d0, d1, init):
        rd = [d0.reg, d1.reg] + self._regs(init)
        ia = init.ap if isinstance(init, V) else init
        self.S.add("dve", lambda e: e.tensor_tensor_scan(out=out.ap, data0=d0.ap, data1=d1.ap, initial=ia,
                                                         op0=ALU.mult, op1=ALU.add), reads=rd, writes=[out.reg])

    def barrier(self):
        self.S.barrier()

    def build(self):
        nc = self.nc
        dbg = self.dbg
        nlayers = dbg.get("nlayers", DEPTH)
        stop_after = dbg.get("stop_after", None)
        x_all = self.din("x_all", [T, D])
        cond = self.din("cond", [2, D])
        cache_ckv = self.din("cache_ckv", [DEPTH, 256, 256])
        cache_kr = self.din("cache_kr", [DEPTH, 256, 32])
        cache_k = self.din("cache_k", [DEPTH, 256, 128])
        cache_v = self.din("cache_v", [DEPTH, 256, 128])
        state = self.din("state", [DEPTH, 2, D])
        w_mod = self.din("w_mod", [DEPTH, D, 3 * D])
        b_mod = self.din("b_mod", [DEPTH, 3 * D])
        g_norm = self.din("g_norm", [DEPTH, D])
        w_in = self.din("w_in", [DEPTH, D, D_IN])
        conv_w = self.din("conv_w", [DEPTH, 4, D])
        conv_b = self.din("conv_b", [DEPTH, D])
        lru_wa = self.din("lru_wa", [DEPTH, 2, 8, 128, 128])
        lru_ba = self.din("lru_ba", [DEPTH, 2, D])
        lru_wi = self.din("lru_wi", [DEPTH, 2, 8, 128, 128])
        lru_bi = self.din("lru_bi", [DEPTH, 2, D])
        lru_lam = self.din("lru_lam", [DEPTH, 2, D])
        mla_q_norm = self.din("mla_q_norm", [DEPTH, 384])
        mla_w_uq = self.din("mla_w_uq", [DEPTH, 384, 768])
        mla_kv_norm = self.din("mla_kv_norm", [DEPTH, 256])
        mla_w_ukv = self.din("mla_w_ukv", [DEPTH, 256, 1024])
        swa_sink = self.din("swa_sink", [DEPTH, 8])
        w_br_rnn = self.din("w_br_rnn", [DEPTH, D, D])
        w_br_mla = self.din("w_br_mla", [DEPTH, 512, D])
        w_br_swa = self.din("w_br_swa", [DEPTH, 512, D])
        w_out = self.din("w_out", [DEPTH, D, D])
        final_norm = self.din("final_norm", [D])
        c_cos_s = self.din("c_cos_s", [128, NS])
        c_sin_s = self.din("c_sin_s", [128, NS])
        c_pm_s = self.din("c_pm_s", [128, 128])
        c_cos_m = self.din("c_cos_m", [128, NS])
        c_sin_m = self.din("c_sin_m", [128, NS])
        c_pm_m = self.din("c_pm_m", [128, 128])
        c_mprev = self.din("c_mprev", [128, 512], BF16)
        c_mnext = self.din("c_mnext", [128, 512], BF16)
        c_ident = self.din("c_ident", [128, 128])
        c_sel = self.din("c_sel", [128, 64])
        y_out = self.dout("y", [T, D])
        o_ckv = self.dout("o_ckv", [2, DEPTH, 256, 256])
        o_kr = self.dout("o_kr", [2, DEPTH, 256, 32])
        o_k = self.dout("o_k", [2, DEPTH, 256, 128])
        o_v = self.dout("o_v", [2, DEPTH, 256, 128])
        o_h = self.dout("o_h", [2, DEPTH, 2, D])
        XT = self.dscr("XT", [D, T], F32)
        A_xr = self.dscr("A_xr", [1024, T], F32)
        A_gr = self.dscr("A_gr", [1024, T], BF16)
        A_cq = self.dscr("A_cq", [384, T], F32)
        A_ckv = self.dscr("A_ckv", [256, T], F32)
        KR = self.dscr("KR", [32, NKEY], BF16)
        A_gm = self.dscr("A_gm", [512, T], BF16)
        A_qs = self.dscr("A_qs", [512, T], BF16)
        A_ks = self.dscr("A_ks", [128, NKEY], BF16)
        A_vs = self.dscr("A_vs", [NKEY, 256], BF16)
        A_gs = self.dscr("A_gs", [512, T], BF16)
        A_mg = self.dscr("A_mg", [3072, T], BF16)
        Y_rnn = self.dscr("Y_rnn", [1024, T], BF16)
        Y_mla = self.dscr("Y_mla", [512, T], BF16)
        Y_swa = self.dscr("Y_swa", [512, T], BF16)
        QTN = self.dscr("QTN", [512, T], BF16)
        QTR = self.dscr("QTR", [256, T], BF16)
        KTN = self.dscr("KTN", [512, NKEY], BF16)
        VM = self.dscr("VM", [NKEY, 1024], BF16)

        self.arena_init(self.dbg.get("arena_bytes", 204 * 1024))
        ps = [V(self.es.enter_context(nc.psum_tensor(f"psum{i}", [128, 512], F32))[:], f"ps{i}") for i in range(8)]

        ident = self.tile("ident", [128], F32)
        ones = self.tile("ones", [128], F32)
        pm_s = self.tile("pm_s", [128], F32)
        pm_m = self.tile("pm_m", [128], F32)
        sel = self.tile("sel", [64], F32)
        mprev = self.tile("mprev", [512], BF16)
        mnext = self.tile("mnext", [512], BF16)
        NPT = 640
        PT = self.tile("PT", [NPT], F32)
        MOD = self.tile("MOD", [DEPTH, 2, 24], F32)
        MA = self.tile("MA", [DEPTH, 2, 8], F32)
        C1 = self.tile("C1", [64], F32)
        ES = self.tile("ES", [DEPTH * 8], F32)
        GKV = self.tile("GKV", [DEPTH, 256], F32)
        SK = self.tile("SK", [2, 512], F32)
        ZR = self.tile("ZR", [128], F32)
        CST = self.tile("CST", [4], F32)
        self.memset("pool", CST[:, 0:1], EPS)
        self.memset("pool", CST[:, 1:2], 1.0)
        self.dma(ident, c_ident)
        self.dma(pm_s, c_pm_s)
        self.dma(pm_m, c_pm_m)
        self.dma(sel, c_sel)
        self.dma(mprev, c_mprev)
        self.dma(mnext, c_mnext)
        self.memset("pool", ones, 1.0)
        self.memset("pool", ZR, 0.0)
        self.memset("pool", SK, 0.0)
        for l in range(DEPTH):
            self.S.add("sp", lambda e, l=l: e.dma_start(out=GKV.ap[:, l, :], in_=mla_kv_norm.ap[l].partition_broadcast(128)),
                       reads=[mla_kv_norm.reg], writes=[GKV.reg], dma=True)
        self.S.add("sp", lambda e: e.dma_start(out=ES.ap, in_=swa_sink.ap.rearrange("l h -> (l h)").partition_broadcast(128)),
                   reads=[swa_sink.reg], writes=[ES.reg], dma=True)
        self.act(ES, ES, AF.Exp)

        rows = []
        rows.append(("b_mod", b_mod.rr("l (n p) -> (l n) p", p=128)))
        rows.append(("g_norm", g_norm.rr("l (n p) -> (l n) p", p=128)))
        rows.append(("conv_w", conv_w.rr("l k (n p) -> (l k n) p", p=128)))
        rows.append(("conv_b", conv_b.rr("l (n p) -> (l n) p", p=128)))
        rows.append(("lru_ba", lru_ba.rr("l d (n p) -> (l d n) p", p=128)))
        rows.append(("q_norm", mla_q_norm.rr("l (n p) -> (l n) p", p=128)))
        rows.append(("kv_norm", mla_kv_norm.rr("l (n p) -> (l n) p", p=128)))
        rows.append(("final", final_norm.rr("(n p) -> n p", p=128)))
        rows.append(("lru_bi", lru_bi.rr("l d (n p) -> (l d n) p", p=128)))
        rows.append(("lru_lam", lru_lam.rr("l d (n p) -> (l d n) p", p=128)))
        rows.append(("state", state.rr("l d (n p) -> (l d n) p", p=128)))
        rows.append(("cond", cond.rr("c (n p) -> (c n) p", p=128)))
        self.pcol = {}
        col = 0
        m0 = self.mark()
        stg = [self.tile(f"pstg{i}", [128], F32) for i in range(2)]
        blocks = []
        cur = []
        curn = 0
        for name, view in rows:
            R = view.ap.shape[0]
            if curn + R > 128:
                blocks.append(cur)
                col += 128 - curn
                cur, curn = [], 0
            self.pcol[name] = col
            cur.append((curn, R, view))
            curn += R
            col += R
            if curn == 128:
                blocks.append(cur)
                cur, curn = [], 0
        if cur:
            blocks.append(cur)
        assert col <= NPT, col
        for bi, blk in enumerate(blocks):
            st = stg[bi % 2]
            nr = 0
            for (r0, n, view) in blk:
                self.dma(st[r0:r0 + n, :], view)
                nr = r0 + n
            pt = ps[bi % 2]
            self.transpose(pt[:, 0:128], st, ident)
            self.copy("dve", PT[:, bi * 128:bi * 128 + nr], pt[:, 0:nr])

        def pc(name, idx):
            c = self.pcol[name] + idx
            return PT[:, c:c + 1]

        lam0 = self.pcol["lru_lam"]
        self.act(C1, PT[:, lam0:lam0 + 64], AF.Exp, scale=-1.0)
        self.act(C1, C1, AF.Ln, bias=1.0)
        self.ts("dve", C1, C1, -8.0, None, ALU.mult)

        m0 = self.mark()
        sc = self.tile("sc", [8, 2], F32)
        c0 = self.pcol["cond"]
        for c in range(2):
            self.act(sc[:, :, c], PT[:, c0 + c * 8:c0 + c * 8 + 8], AF.Silu)
        wmb = [self.tile(f"wm{i}", [8, 512], F32) for i in range(2)]
        modrow = self.tile("modrow", [3 * D], F32)
        gi = 0
        for l in range(nlayers):
            for cg in range(6):
                wb = wmb[gi % 2]
                self.dma(wb, w_mod[l].rr("(k p) c -> p k c", p=128)[:, :, cg * 512:(cg + 1) * 512])
                pb = ps[2 + gi % 2]
                for k in range(8):
                    self.mm(pb[0:2, :], sc[:, k, :], wb[:, k, :], start=(k == 0), stop=(k == 7))
                self.copy("act", modrow[0:2, cg * 512:(cg + 1) * 512], pb[0:2, :])
                gi += 1
            pmod = ps[4 + l % 2]
            for j in range(24):
                self.transpose(pmod[:, 2 * j:2 * j + 2], modrow[0:2, j * 128:(j + 1) * 128], ident[0:2, 0:2])
            bm0 = self.pcol["b_mod"] + l * 24
            for c in range(2):
                self.tt("dve", MOD[:, l, c, :], pmod[:, 0:48].rr("p (j c) -> p j c", c=2)[:, :, c], PT[:, bm0:bm0 + 24], ALU.add)
                g0 = self.pcol["g_norm"] + l * 8
                self.stt("dve", MA[:, l, c, :], MOD[:, l, c, 8:16], 1.0, PT[:, g0:g0 + 8], ALU.add, ALU.mult)
        self.barrier()
        self.reset(m0)
        layer_mark = self.mark()
        if dbg.get("dump_pre"):
            dpre = self.dout("dbg_pre", [128, NPT + 192 + 64 + 64 + 32])
            self.dma(dpre[:, 0:NPT], PT)
            self.dma(dpre[:, NPT:NPT + 192], MOD.rr("p l c j -> p (l c j)"))
            self.dma(dpre[:, NPT + 192:NPT + 256], MA.rr("p l c j -> p (l c j)"))
            self.dma(dpre[:, NPT + 256:NPT + 320], C1)
            self.dma(dpre[:, NPT + 320:NPT + 352], ES)

        xin = [self.tile(f"xin{i}", [D], F32) for i in range(2)]
        xtt = [self.tile(f"xtt{i}", [8, TS], F32, dj=True) for i in range(2)]
        XTv = XT.rr("(n p) t -> p n t", p=128)
        for t in range(NT):
            xo = xtt[t % 2]
            for b in range(4):
                xi = xin[(t * 4 + b) % 2]
                r0 = t * TS + b * 128
                self.dma(xi, x_all[r0:r0 + 128, :])
                for half in range(2):
                    pt = ps[(b * 2 + half) % 4]
                    for n4 in range(4):
                        n = half * 4 + n4
                        self.transpose(pt[:, n4 * 128:(n4 + 1) * 128], xi[:, n * 128:(n + 1) * 128], ident)
                    eng = "act" if half == 0 else "dve"
                    self.copy(eng, xo[:, half * 4:half * 4 + 4, b * 128:(b + 1) * 128],
                              pt.rr("p (n t) -> p n t", n=4))
            self.dma(XTv[:, :, t * TS:(t + 1) * TS], xo)
        self.barrier()
        self.reset(layer_mark)
        if stop_after == "init":
            return self.finish()

        self.__dict__.update({k: v for k, v in locals().items() if k != "self"})
        for l in range(nlayers):
            self.layer(l)
            if self.stopped:
                return self.finish()
            self.barrier()

        self.reset(layer_mark)
        xt2 = [self.tile(f"fx{i}", [8, TS], F32) for i in range(2)]
        sq = self.tile("fsq", [8, TS], F32)
        rstd = [self.tile(f"frs{i}", [TS], F32) for i in range(2)]
        xn = [self.tile(f"fxn{i}", [8, TS], F32) for i in range(2)]
        yo = [self.tile(f"fyo{i}", [D], F32, dj=True) for i in range(2)]
        f0 = self.pcol["final"]
        for t in range(NT):
            x = xt2[t % 2]
            self.dma(x, XTv[:, :, t * TS:(t + 1) * TS])
            self.act(sq, x, AF.Square)
            pss = ps[t % 2]
            for n in range(8):
                self.mm(pss, ones, sq[:, n, :], start=(n == 0), stop=(n == 7))
            rs = rstd[t % 2]
            self.act(rs, pss, AF.Ln, bias=CST[:, 0:1], scale=1.0 / D)
            self.act(rs, rs, AF.Exp, scale=-0.5)
            xo = xn[t % 2]
            for n in range(8):
                self.stt("dve" if n % 2 == 0 else "pool", xo[:, n, :], x[:, n, :], PT[:, f0 + n:f0 + n + 1], rs, ALU.mult, ALU.mult)
            for b in range(4):
                y = yo[(t * 4 + b) % 2]
                for half in range(2):
                    pt = ps[2 + (b * 2 + half) % 4]
                    for n4 in range(4):
                        n = half * 4 + n4
                        self.transpose(pt[:, n4 * 128:(n4 + 1) * 128], xo[:, n, b * 128:(b + 1) * 128], ident)
                    self.copy("act" if half == 0 else "dve", y[:, half * 512:(half + 1) * 512], pt)
                r0 = t * TS + b * 128
                self.dma(y_out[r0:r0 + 128, :], y)
        return self.finish()

    def finish(self):
        self.S.emit(self.nc, self.es)
        self.es.close()
        return self.nc

    stopped = False

    def stop(self, l, name):
        sa = self.dbg.get("stop_after", None)
        if sa == (l, name):
            self.stopped = True
        return self.stopped

    def layer(self, l):
        self.reset(self.layer_mark)
        self.p0(l)
        if self.stop(l, "p0"):
            dh = self.dout("dbg_hT", [D, T], BF16)
            self.dma(dh.rr("(n p) t -> p n t", p=128), self.hT)
            return
        self.barrier()
        self.p1(l)
        if self.stopped or self.stop(l, "p1"):
            return
        self.barrier()
        self.reset(self.layer_mark)
        self.p2(l)
        if self.stop(l, "p2"):
            return
        self.barrier()
        self.reset(self.layer_mark)
        self.p3(l)
        if self.stopped or self.stop(l, "p3"):
            return
        self.barrier()
        self.reset(self.layer_mark)
        self.p4(l)
        if self.stop(l, "p4"):
            return
        self.barrier()
        self.reset(self.layer_mark)
        self.p5(l)
        if self.stop(l, "p5"):
            return
        self.barrier()
        self.reset(self.layer_mark)
        self.p6(l)
        if self.stop(l, "p6"):
            return

    def substop(self):
        self._sub = getattr(self, "_sub", 0) + 1
        if self.dbg.get("p1_n") == self._sub:
            self.stopped = True
        return self.stopped

    def tcols(self, t):
        return slice(t * TS, (t + 1) * TS)

    def p0(self, l):
        ps, ones, MOD, MA, XTv, CST = self.ps, self.ones, self.MOD, self.MA, self.XTv, self.CST
        self.hT = hT = self.tile("hT", [8, T], BF16, dj=True)
        self.p1_mark = self.mark()
        xt = [self.tile(f"p0x{i}", [8, TS], F32) for i in range(2)]
        sq = self.tile("p0sq", [8, TS], F32)
        rs = [self.tile(f"p0rs{i}", [TS], F32) for i in range(2)]
        tmp = [self.tile(f"p0tm{i}", [TS], F32) for i in range(4)]
        for t in range(NT):
            ci = 0 if t < 8 else 1
            x = xt[t % 2]
            self.dma(x, XTv[:, :, self.tcols(t)])
            self.act(sq, x, AF.Square)
            pss = ps[t % 2]
            for n in range(8):
                self.mm(pss, ones, sq[:, n, :], start=(n == 0), stop=(n == 7))
            r = rs[t % 2]
            self.act(r, pss, AF.Ln, bias=CST[:, 0:1], scale=1.0 / D)
            self.act(r, r, AF.Exp, scale=-0.5)
            for n in range(8):
                tm = tmp[n % 4]
                self.tt("dve" if n % 2 == 0 else "pool", tm, x[:, n, :], r, ALU.mult)
                self.act(hT[:, n, self.tcols(t)], tm, AF.Identity, bias=MOD[:, l, ci, n:n + 1], scale=MA[:, l, ci, n:n + 1])

    def p1(self, l):
        ps, hT = self.ps, self.hT
        self.reset(self.p1_mark)
        wb = [self.tile(f"p1w{i}", [8, 512], BF16) for i in range(2)]
        sf = [self.tile(f"p1sf{i}", [TS], F32) for i in range(4)]
        sb = [self.tile(f"p1sb{i}", [T], BF16, dj=True) for i in range(2)]
        cs = [self.tile(f"p1cs{i}", [TS], F32) for i in range(2)]
        sn = [self.tile(f"p1sn{i}", [TS], F32) for i in range(2)]
        qf = [self.tile(f"p1qf{i}", [TS], F32) for i in range(2)]
        t1 = [self.tile(f"p1t1{i}", [TS], F32) for i in range(2)]
        t2 = [self.tile(f"p1t2{i}", [TS], F32) for i in range(2)]
        vst = [self.tile(f"p1vs{i}", [4, 256], BF16) for i in range(2)]
        tmo = [self.tile(f"p1tm{i}", [288], F32) for i in range(2)]
        tsq = self.tile("p1tsq", [256], F32)
        tss = self.tile("p1tss", [2], F32)
        self.memset("act_ms", vst[0], 1.0)
        self.memset("dve", vst[1], 1.0)
        w_l = self.w_in[l].rr("(k p) c -> p k c", p=128)
        st = dict(pi=0, ev=0, sfi=0, sbi=0, ri=0, gi=0)

        def load_w(c0, ncol):
            w = wb[st["gi"] % 2]
            st["gi"] += 1
            self.dma(w[:, :, 0:ncol], w_l[:, :, c0:c0 + ncol], q="pool")
            return w

        def evac_eng():
            st["ev"] += 1
            return "act" if st["ev"] % 2 == 0 else "dve"

        def fm_chunk(w, wc0, M, kind, dst, dcol0=0, tables=None, pm=None):
            if kind != "f32":
                s_b = sb[st["sbi"] % 2]
                st["sbi"] += 1
            for t in range(NT):
                pb = ps[st["pi"] % 4]
                st["pi"] += 1
                for k in range(8):
                    self.mm(pb[0:M, :], w[:, k, wc0:wc0 + M], hT[:, k, self.tcols(t)], start=(k == 0), stop=(k == 7))
                if kind == "f32":
                    s = sf[st["sfi"] % 4]
                    st["sfi"] += 1
                    self.copy(evac_eng(), s[0:M, :], pb[0:M, :])
                    self.dma(dst[:, dcol0 + t * TS:dcol0 + (t + 1) * TS], s[0:M, :])
                elif kind == "bf16":
                    self.copy(evac_eng(), s_b[0:M, self.tcols(t)], pb[0:M, :])
                elif kind == "silu":
                    self.act(s_b[0:M, self.tcols(t)], pb[0:M, :], AF.Silu)
                elif kind == "sigmoid":
                    self.act(s_b[0:M, self.tcols(t)], pb[0:M, :], AF.Sigmoid)
                elif kind == "rope":
                    if t == 8:
                        self.copy(evac_eng(), s_b[0:M, self.tcols(t)], pb[0:M, :])
                    else:
                        i = st["ri"] % 2
                        st["ri"] += 1
                        self.dma(cs[i][0:M, :], tables[0][0:M, self.tcols(t)])
                        self.dma(sn[i][0:M, :], tables[1][0:M, self.tcols(t)])
                        self.copy("act", qf[i][0:M, :], pb[0:M, :])
                        pr = ps[4 + i]
                        self.mm(pr[0:M, :], pm[0:M, 0:M], qf[i][0:M, :])
                        self.tt("dve", t1[i][0:M, :], qf[i][0:M, :], cs[i][0:M, :], ALU.mult)
                        self.tt("dve", t2[i][0:M, :], pr[0:M, :], sn[i][0:M, :], ALU.mult)
                        self.tt("pool", s_b[0:M, self.tcols(t)], t1[i][0:M, :], t2[i][0:M, :], ALU.add)
            if kind != "f32":
                self.dma(dst[:, dcol0:dcol0 + T], s_b[0:M, :])

        def rows(v, r0, n=128):
            return v[r0:r0 + n, :]

        for g in range(2):
            w = load_w(C_XR + g * 512, 512)
            for j in range(4):
                n = g * 4 + j
                fm_chunk(w, j * 128, 128, "f32", rows(self.A_xr, n * 128))
        if self.substop():
            return
        for g in range(2):
            w = load_w(C_GR + g * 512, 512)
            for j in range(4):
                n = g * 4 + j
                fm_chunk(w, j * 128, 128, "silu", rows(self.A_gr, n * 128))
        if self.substop():
            return
        w = load_w(C_CQ, 384)
        for j in range(3):
            fm_chunk(w, j * 128, 128, "f32", rows(self.A_cq, j * 128))
        if self.substop():
            return
        w = load_w(C_CKV, 288)
        for j in range(2):
            fm_chunk(w, j * 128, 128, "f32", rows(self.A_ckv, j * 128))
        if self.substop():
            return
        fm_chunk(w, 256, 32, "rope", self.KR, dcol0=KOFF, tables=(self.c_cos_m, self.c_sin_m), pm=self.pm_m)
        if self.substop():
            return
        for pbk in range(4):
            seq, pos0 = pbk // 2, (pbk % 2) * 128
            tok0 = NS + pbk * 128
            pt = ps[6 + pbk % 2]
            for k in range(8):
                self.mm(pt[:, 0:288], hT[:, k, tok0:tok0 + 128], w[:, k, 0:288], start=(k == 0), stop=(k == 7))
            o = tmo[pbk % 2]
            self.act(tsq, pt[:, 0:256], AF.Square)
            self.S.add("dve", lambda e: e.reduce_sum(out=tss.ap[:, 0:1], in_=tsq.ap, axis=mybir.AxisListType.X),
                       reads=[tsq.reg], writes=[tss.reg])
            self.act(tss[:, 1:2], tss[:, 0:1], AF.Sqrt, bias=self.CST[:, 0:1], scale=1.0 / 256)
            self.recip(tss[:, 1:2], tss[:, 1:2])
            self.stt("dve", o[:, 0:256], pt[:, 0:256], tss[:, 1:2], self.GKV[:, l, :], ALU.mult, ALU.mult)
            self.copy("act", o[:, 256:288], pt[:, 256:288])
            self.dma(self.o_ckv[seq, l, pos0:pos0 + 128, :], o[:, 0:256])
            self.dma(self.o_kr[seq, l, pos0:pos0 + 128, :], o[:, 256:288])
        if self.substop():
            return
        w = load_w(C_GM, 512)
        for j in range(4):
            fm_chunk(w, j * 128, 128, "silu", rows(self.A_gm, j * 128))
        if self.substop():
            return
        w = load_w(C_QS, 512)
        for j in range(4):
            fm_chunk(w, j * 128, 128, "rope", rows(self.A_qs, j * 128), tables=(self.c_cos_s, self.c_sin_s), pm=self.pm_s)
        if self.substop():
            return
        w = load_w(C_KS, 256)
        fm_chunk(w, 0, 128, "rope", self.A_ks, dcol0=KOFF, tables=(self.c_cos_s, self.c_sin_s), pm=self.pm_s)
        if self.substop():
            return
        for t in range(NT):
            v = vst[t % 2]
            for b in range(4):
                tok0 = t * TS + b * 128
                pt = ps[6 + b % 2]
                prompt = (t == 8)
                c0 = 0 if prompt else 128
                for k in range(8):
                    self.mm(pt[:, c0:256], hT[:, k, tok0:tok0 + 128], w[:, k, c0:256], start=(k == 0), stop=(k == 7))
                if not self.dbg.get("skip_vcopy"):
                    for h2 in range(2):
                        self.copy("act" if t % 2 == 0 else "dve", v[:, b, h2 * 128:h2 * 128 + 64], pt[:, 128 + h2 * 64:192 + h2 * 64])
                if prompt:
                    seq, pos0 = b // 2, (b % 2) * 128
                    o = tmo[b % 2]
                    self.copy("act", o[:, 0:256], pt[:, 0:256])
                    self.dma(self.o_k[seq, l, pos0:pos0 + 128, :], o[:, 0:128])
                    self.dma(self.o_v[seq, l, pos0:pos0 + 128, :], o[:, 128:256])
            if not self.dbg.get("skip_vdma"):
                self.dma(self.A_vs[KOFF + t * TS:KOFF + (t + 1) * TS, :].rr("(b p) c -> p b c", p=128), v)
        if self.substop():
            return
        w = load_w(C_GS, 512)
        for j in range(4):
            fm_chunk(w, j * 128, 128, "silu", rows(self.A_gs, j * 128))
        if self.substop():
            return
        for g in range(6):
            w = load_w(C_MG + g * 512, 512)
            for j in range(4):
                fm_chunk(w, j * 128, 128, "sigmoid", rows(self.A_mg, (g * 4 + j) * 128))

    def p2(self, l):
        ps, PT = self.ps, self.PT
        segs = [(0, 0, NS, True), (NS, 1, NPR, False), (NS + NPR, 2, NPR, False)]
        XPs = [self.tile(f"p2xp{i}", [L_ + 3], F32) for i, L_ in enumerate((NS, NPR, NPR))]
        xcbs = [self.tile(f"p2xcb{i}", [T], BF16) for i in range(2)]
        RAs = [self.tile(f"p2ra{i}", [T], F32, dj=True) for i in range(2)]
        IIs = [self.tile(f"p2ii{i}", [T], BF16, dj=True) for i in range(2)]
        S2s = [self.tile(f"p2s2{i}", [T], F32) for i in range(2)]
        hf = self.tile("p2hf", [T], F32)
        hb = self.tile("p2hb", [T], F32)
        gss = [self.tile(f"p2gs{i}", [T], BF16) for i in range(2)]
        wl = [self.tile(f"p2w{i}", [4, 128], BF16, dj=True) for i in range(2)]
        HS = self.tile("p2hs", [32], F32)
        hso = self.tile("p2hso", [128], F32)
        for xp_ in XPs:
            self.memset("pool", xp_, 0.0)
        cw0, cb0 = self.pcol["conv_w"], self.pcol["conv_b"]
        ba0, bi0, st0 = self.pcol["lru_ba"], self.pcol["lru_bi"], self.pcol["state"]
        st = dict(pi=0)

        def p2loads(n):
            w_ = wl[n % 2]
            for d in range(2):
                self.dma(w_[:, d * 2 + 0, :], self.lru_wa[l, d, n], q="pool")
                self.dma(w_[:, d * 2 + 1, :], self.lru_wi[l, d, n], q="pool")
            for (tok0, xi, L, smp) in segs:
                self.dma(XPs[xi][:, 1:1 + L], self.A_xr[n * 128:(n + 1) * 128, tok0:tok0 + L])

        def p2loads_g(n):
            self.dma(gss[n % 2], self.A_gr[n * 128:(n + 1) * 128, :])

        def conv(n):
            xcb = xcbs[n % 2]
            for (tok0, xi, L, smp) in segs:
                XP = XPs[xi]
                o = xcb[:, tok0:tok0 + L]
                self.ts("dve", o, XP[:, 0:L], PT[:, cw0 + (l * 4 + 0) * 8 + n:cw0 + (l * 4 + 0) * 8 + n + 1],
                        PT[:, cb0 + l * 8 + n:cb0 + l * 8 + n + 1], ALU.mult, ALU.add)
                for k in range(1, 4):
                    c = cw0 + (l * 4 + k) * 8 + n
                    self.stt("dve", o, XP[:, k:k + L], PT[:, c:c + 1], o, ALU.mult, ALU.add)

        def gates(n, d):
            w = wl[n % 2]
            xcb = xcbs[n % 2]
            cc = (l * 2 + d) * 8 + n
            RA, II = RAs[d], IIs[d]
            for t in range(NT):
                pa = ps[st["pi"] % 4]
                pi_ = ps[4 + st["pi"] % 4]
                st["pi"] += 1
                self.mm(pa, w[:, d * 2 + 0, :], xcb[:, self.tcols(t)])
                self.mm(pi_, w[:, d * 2 + 1, :], xcb[:, self.tcols(t)])
                self.act(RA[:, self.tcols(t)], pa, AF.Sigmoid, bias=PT[:, ba0 + cc:ba0 + cc + 1])
                self.act(II[:, self.tcols(t)], pi_, AF.Sigmoid, bias=PT[:, bi0 + cc:bi0 + cc + 1])

        def act_part(n, d):
            cc = (l * 2 + d) * 8 + n
            RA, S2 = RAs[d], S2s[d]
            self.act(RA, RA, AF.Exp, scale=self.C1[:, cc:cc + 1])
            self.act(S2, RA, AF.Square)
            self.act(S2, S2, AF.Sqrt, bias=self.CST[:, 1:2], scale=-1.0)

        def dve_part(n, d):
            cc = (l * 2 + d) * 8 + n
            RA, II, S2, xcb = RAs[d], IIs[d], S2s[d], xcbs[n % 2]
            self.tt("dve", S2, S2, II, ALU.mult)
            self.tt("dve", S2, S2, xcb, ALU.mult)
            h = hf if d == 0 else hb
            for si, (tok0, pb, L, smp) in enumerate(segs):
                init = PT[:, st0 + cc:st0 + cc + 1] if smp else 0.0
                sl = slice(tok0, tok0 + L)
                if d == 0:
                    self.scan(h[:, sl], RA[:, sl], S2[:, sl], init)
                else:
                    self.scan(h[:, sl][:, ::-1], RA[:, sl][:, ::-1], S2[:, sl][:, ::-1], init)
                if not smp:
                    col = (si - 1) * 16 + d * 8 + n
                    src = tok0 + L - 1 if d == 0 else tok0
                    self.copy("dve", HS[:, col:col + 1], h[:, src:src + 1])

        p2loads(0)
        p2loads_g(0)
        p2loads_g(1)
        conv(0)
        p2loads(1)
        gates(0, 0)
        act_part(0, 0)
        gates(0, 1)
        for n in range(8):
            gs = gss[n % 2]
            act_part(n, 1)
            dve_part(n, 0)
            if n + 1 < 8:
                conv(n + 1)
                if n + 2 < 8:
                    p2loads(n + 2)
                gates(n + 1, 0)
                act_part(n + 1, 0)
            dve_part(n, 1)
            if n + 1 < 8:
                gates(n + 1, 1)
            self.tt("dve", hf, hf, hb, ALU.add)
            self.tt("dve", gs, hf, gs, ALU.mult)
            self.dma(self.Y_rnn[n * 128:(n + 1) * 128, :], gs)
            if n + 2 < 8:
                p2loads_g(n + 2)
        pt = ps[0]
        self.transpose(pt[0:32, 0:128], HS, self.ident)
        self.copy("dve", hso[0:32, :], pt[0:32, 0:128])
        for s_ in range(2):
            self.dma(self.o_h[s_, l].rr("d (n p) -> (d n) p", p=128), hso[s_ * 16:(s_ + 1) * 16, :])

    def p3(self, l):
        ps, PT, ident, ones = self.ps, self.PT, self.ident, self.ones
        wqn = self.tile("p3wqn", [3, 8, 64], BF16, dj=True)
        wqr = self.tile("p3wqr", [3, 8, 32], BF16, dj=True)
        wkn = self.tile("p3wkn", [2, 8, 64], BF16, dj=True)
        wv = self.tile("p3wv", [2, 8, 64], BF16, dj=True)
        wqf = self.tile("p3wqf", [3, 768], BF16)
        wkf = self.tile("p3wkf", [2, 1024], BF16)
        self.dma(wqf, self.mla_w_uq[l].rr("(k p) c -> p k c", p=128), q="pool")
        self.dma(wkf, self.mla_w_ukv[l].rr("(k p) c -> p k c", p=128), q="pool")
        ci_ = 0
        for h in range(8):
            for k in range(3):
                self.copy("dve" if ci_ % 2 else "act", wqn[:, k, h, :], wqf[:, k, h * 96:h * 96 + 64])
                self.copy("act" if ci_ % 2 else "dve", wqr[:, k, h, :], wqf[:, k, h * 96 + 64:h * 96 + 96])
                ci_ += 1
            for k in range(2):
                self.copy("dve" if ci_ % 2 else "act", wkn[:, k, h, :], wkf[:, k, h * 128:h * 128 + 64])
                self.copy("act" if ci_ % 2 else "dve", wv[:, k, h, :], wkf[:, k, h * 128 + 64:h * 128 + 128])
                ci_ += 1
        wqn2 = wqn.rr("p k h r -> p k (h r)")
        wqr2 = wqr.rr("p k h r -> p k (h r)")
        wkn2 = wkn.rr("p k h r -> p k (h r)")
        wv2 = wv.rr("p k h r -> p k (h r)")
        if self.substop():
            return
        cin = self.tile("p3cin", [2, 256], F32)
        cT = self.tile("p3cT", [2, 256], BF16)
        self.dma(cin, self.cache_ckv[l].rr("(b p) f -> p b f", p=128))
        for kf in range(2):
            pt = ps[kf]
            for b in range(2):
                self.transpose(pt[:, b * 128:(b + 1) * 128], cin[:, b, kf * 128:(kf + 1) * 128], ident)
            self.copy("dve", cT[:, kf, :], pt[:, 0:256])
        stg = [self.tile(f"p3st{i}", [TS], BF16, dj=True) for i in range(4)]
        vst = [self.tile(f"p3vs{i}", [4, 8, 128], BF16) for i in range(2)]
        self.memset("act_ms", vst[0], 1.0)
        self.memset("dve", vst[1], 1.0)
        sti = 0
        for j in range(4):
            pk = ps[2 + j % 2]
            for k in range(2):
                self.mm(pk[:, 0:256], wkn2[:, k, j * 128:(j + 1) * 128], cT[:, k, :], start=(k == 0), stop=(k == 1))
            s = stg[sti % 4]
            sti += 1
            self.copy("act", s[:, 0:256], pk[:, 0:256])
            self.dma(self.KTN[j * 128:(j + 1) * 128, 0:256], s[:, 0:256])
        v = vst[0]
        for b in range(2):
            pv = ps[4 + b]
            for k in range(2):
                self.mm(pv, cT[:, k, b * 128:(b + 1) * 128], wv2[:, k, :], start=(k == 0), stop=(k == 1))
            for h8 in range(8):
                self.copy("act", v[:, b, h8, 0:64], pv[:, h8 * 64:(h8 + 1) * 64])
        self.dma(self.VM[0:256, :].rr("(b p) c -> p b c", p=128), v[:, 0:2].rr("p b h c -> p b (h c)"))
        if self.substop():
            return
        krin = self.tile("p3kri", [2, 32], F32)
        krs = self.tile("p3krs", [256], BF16)
        self.dma(krin, self.cache_kr[l].rr("(b p) f -> p b f", p=128))
        pt = ps[6]
        for b in range(2):
            self.transpose(pt[0:32, b * 128:(b + 1) * 128], krin[:, b, :], ident)
        self.copy("act", krs[0:32, :], pt[0:32, 0:256])
        self.dma(self.KR[:, 0:256], krs[0:32, :])
        if self.substop():
            return
        skin = self.tile("p3ski", [2, 128], F32)
        sks = self.tile("p3sks", [256], BF16)
        self.dma(skin, self.cache_k[l].rr("(b p) f -> p b f", p=128))
        pt = ps[7]
        for b in range(2):
            self.transpose(pt[:, b * 128:(b + 1) * 128], skin[:, b, :], ident)
        self.copy("dve", sks, pt[:, 0:256])
        self.dma(self.A_ks[:, 0:256], sks)
        svin = self.tile("p3svi", [2, 128], F32)
        svs = self.tile("p3svs", [2, 2, 128], BF16)
        self.memset("pool", svs, 1.0)
        self.dma(svin, self.cache_v[l].rr("(b p) f -> p b f", p=128))
        for b2 in range(2):
            for h2 in range(2):
                self.copy("dve", svs[:, b2, h2, 0:64], svin[:, b2, h2 * 64:(h2 + 1) * 64])
        self.dma(self.A_vs[0:256, :].rr("(b p) c -> p b c", p=128), svs.rr("p b h c -> p b (h c)"))
        if self.substop():
            return
        cq = [self.tile(f"p3cq{i}", [3, TS], F32) for i in range(2)]
        ck = [self.tile(f"p3ck{i}", [2, TS], F32) for i in range(2)]
        sq = self.tile("p3sq", [3, TS], F32)
        rs = [self.tile(f"p3rs{i}", [TS], F32) for i in range(2)]
        cqn = [self.tile(f"p3cqn{i}", [3, TS], BF16) for i in range(2)]
        ckn = [self.tile(f"p3ckn{i}", [2, TS], BF16) for i in range(2)]
        cs = [self.tile(f"p3cs{i}", [TS], F32) for i in range(2)]
        sn = [self.tile(f"p3sn{i}", [TS], F32) for i in range(2)]
        qf = [self.tile(f"p3qf{i}", [TS], F32) for i in range(2)]
        t1 = [self.tile(f"p3t1{i}", [TS], F32) for i in range(2)]
        t2 = [self.tile(f"p3t2{i}", [TS], F32) for i in range(2)]
        A_cq3 = self.A_cq.rr("(k p) t -> p k t", p=128)
        A_ckv3 = self.A_ckv.rr("(k p) t -> p k t", p=128)
        qn0, kvn0 = self.pcol["q_norm"] + l * 3, self.pcol["kv_norm"] + l * 2
        pi = 0
        ri = 0
        def p3loads(t):
            self.dma(cq[t % 2], A_cq3[:, :, self.tcols(t)])
            self.dma(ck[t % 2], A_ckv3[:, :, self.tcols(t)])

        p3loads(0)
        for t in range(NT):
            tc_ = self.tcols(t)
            q_ = cq[t % 2]
            c_ = ck[t % 2]
            if t + 1 < NT:
                p3loads(t + 1)
            self.act(sq, q_, AF.Square)
            pss = ps[6]
            for k in range(3):
                self.mm(pss, ones, sq[:, k, :], start=(k == 0), stop=(k == 2))
            r = rs[0]
            self.act(r, pss, AF.Ln, bias=self.CST[:, 0:1], scale=1.0 / 384)
            self.act(r, r, AF.Exp, scale=-0.5)
            qn = cqn[t % 2]
            for k in range(3):
                self.stt("dve" if k != 1 else "pool", qn[:, k, :], q_[:, k, :], PT[:, qn0 + k:qn0 + k + 1], r, ALU.mult, ALU.mult)
            self.act(sq[:, 0:2, :], c_, AF.Square)
            pss = ps[7]
            for k in range(2):
                self.mm(pss, ones, sq[:, k, :], start=(k == 0), stop=(k == 1))
            r = rs[1]
            self.act(r, pss, AF.Ln, bias=self.CST[:, 0:1], scale=1.0 / 256)
            self.act(r, r, AF.Exp, scale=-0.5)
            kn = ckn[t % 2]
            for k in range(2):
                self.stt("dve" if k == 0 else "pool", kn[:, k, :], c_[:, k, :], PT[:, kvn0 + k:kvn0 + k + 1], r, ALU.mult, ALU.mult)
            for j in range(4):
                pb = ps[pi % 4]
                pi += 1
                for k in range(3):
                    self.mm(pb, wqn2[:, k, j * 128:(j + 1) * 128], qn[:, k, :], start=(k == 0), stop=(k == 2))
                s = stg[sti % 4]
                sti += 1
                self.copy("act" if j % 2 == 0 else "dve", s, pb)
                self.dma(self.QTN[j * 128:(j + 1) * 128, tc_], s)
            for j in range(2):
                pb = ps[pi % 4]
                pi += 1
                for k in range(3):
                    self.mm(pb, wqr2[:, k, j * 128:(j + 1) * 128], qn[:, k, :], start=(k == 0), stop=(k == 2))
                s = stg[sti % 4]
                sti += 1
                if t == 8:
                    self.copy("act", s, pb)
                else:
                    i = ri % 2
                    ri += 1
                    if j == 0:
                        self.dma(cs[i], self.c_cos_m[:, tc_])
                        self.dma(sn[i], self.c_sin_m[:, tc_])
                        csn = (cs[i], sn[i])
                    self.copy("act", qf[i], pb)
                    pr = ps[4 + i]
                    self.mm(pr, self.pm_m, qf[i])
                    self.tt("dve", t1[i], qf[i], csn[0], ALU.mult)
                    self.tt("dve", t2[i], pr, csn[1], ALU.mult)
                    self.tt("pool", s, t1[i], t2[i], ALU.add)
                self.dma(self.QTR[j * 128:(j + 1) * 128, tc_], s)
            for j in range(4):
                pb = ps[pi % 4]
                pi += 1
                for k in range(2):
                    self.mm(pb, wkn2[:, k, j * 128:(j + 1) * 128], kn[:, k, :], start=(k == 0), stop=(k == 1))
                s = stg[sti % 4]
                sti += 1
                self.copy("act" if j % 2 == 0 else "dve", s, pb)
                self.dma(self.KTN[j * 128:(j + 1) * 128, KOFF + t * TS:KOFF + (t + 1) * TS], s)
            v = vst[t % 2]
            for b in range(4):
                pv = ps[pi % 4]
                pi += 1
                for k in range(2):
                    self.mm(pv, kn[:, k, b * 128:(b + 1) * 128], wv2[:, k, :], start=(k == 0), stop=(k == 1))
                for h8 in range(8):
                    self.copy("act" if t % 2 == 0 else "dve", v[:, b, h8, 0:64], pv[:, h8 * 64:(h8 + 1) * 64])
            self.dma(self.VM[KOFF + t * TS:KOFF + (t + 1) * TS, :].rr("(b p) c -> p b c", p=128), v.rr("p b h c -> p b (h c)"))

    def run_pipe(self, items, la=3, pd=2):
        n = len(items)
        pend = []
        for i in range(n + la):
            if i < n:
                items[i]["score"](i)
            j = i - la
            if j >= 0:
                items[j]["pv"](j)
                if items[j].get("post"):
                    pend.append([pd, items[j]["post"]])
            npend = []
            for p in pend:
                p[0] -= 1
                if p[0] <= 0:
                    p[1]()
                else:
                    npend.append(p)
            pend = npend
        for p in pend:
            p[1]()

    def p4(self, l):
        ps = self.ps
        seqs = [(0, NS, 0, NS + KOFF), (NS, NPR, NS + KOFF, NPR), (NS + NPR, NPR, NS + NPR + KOFF, NPR)]
        Vt = self.tile("p4vt", [(NS + KOFF) // 128, 1024], BF16)
        Kt = [self.tile(f"p4kt{i}", [NS + KOFF], BF16) for i in range(2)]
        Qt = [self.tile(f"p4qt{i}", [TS], BF16) for i in range(3)]
        gm = [self.tile(f"p4gm{i}", [NS], BF16) for i in range(4)]
        yst = [self.tile(f"p4ys{i}", [NS], BF16, dj=True) for i in range(2)]
        pt = [self.tile(f"p4pt{i}", [TS], BF16) for i in range(4)]
        osb = [self.tile(f"p4os{i}", [TS], F32) for i in range(2)]
        y32 = [self.tile(f"p4y{i}", [TS], F32) for i in range(2)]
        items = []
        load_fns = []
        hq = 0
        VtP = [self.tile(f"p4vtp{i}", [2, 1024], BF16) for i in range(2)]
        for si_, (tok0, L, kb, nk) in enumerate(seqs):
            nkc = nk // 128
            vt = Vt[:, 0:nkc, :] if si_ == 0 else VtP[si_ - 1]
            for h in range(8):
                kt = Kt[h % 2]
                g_ = gm[h % 4]
                ys = yst[h % 2]
                nqt = max(1, L // TS)
                nq = min(TS, L)
                for qt in range(nqt):
                    q_ = Qt[hq % 3]
                    po = ps[4 + hq % 2]
                    ob = osb[hq % 2]
                    yb = y32[hq % 2]
                    hq += 1
                    q0 = tok0 + qt * TS

                    def loads(first_h=(h == 0 and qt == 0), first_q=(qt == 0), vt=vt, kt=kt, g_=g_, q_=q_, h=h, kb=kb, nk=nk,
                              tok0=tok0, L=L, q0=q0, nq=nq):
                        if first_h:
                            self.dma(vt, self.VM[kb:kb + nk, :].rr("(c p) f -> p c f", p=128))
                        if first_q:
                            self.dma(kt[0:64, 0:nk].sub(kt.reg + "n"), self.KTN[h * 64:(h + 1) * 64, kb:kb + nk])
                            self.dma(kt[64:96, 0:nk].sub(kt.reg + "r"), self.KR[:, kb:kb + nk])
                            self.dma(g_[0:64, 0:L], self.A_gm[h * 64:(h + 1) * 64, tok0:tok0 + L])
                        self.dma(q_[0:64, 0:nq].sub(q_.reg + "n"), self.QTN[h * 64:(h + 1) * 64, q0:q0 + nq])
                        self.dma(q_[64:96, 0:nq].sub(q_.reg + "r"), self.QTR[h * 32:(h + 1) * 32, q0:q0 + nq])

                    load_fns.append(loads)
                    gi = len(load_fns) - 1
                    for c in range(nkc):
                        def score(i, c=c, kt=kt, q_=q_, nq=nq, gi=gi):
                            if c == 0:
                                if gi == 0:
                                    load_fns[0]()
                                if gi + 1 < len(load_fns):
                                    load_fns[gi + 1]()
                            pb = ps[i % 4]
                            self.S.add("pe", lambda e: e.matmul(pb.ap[:, 0:nq], lhsT=kt.ap[0:96, c * 128:(c + 1) * 128],
                                                                rhs=q_.ap[0:96, 0:nq], start=True, stop=True),
                                       reads=[kt.reg + "n", kt.reg + "r", q_.reg + "n", q_.reg + "r"], writes=[pb.reg])
                            self.act(pt[i % 4][:, 0:nq], pb[:, 0:nq], AF.Exp, scale=MLA_SCALE)

                        def pv(i, c=c, vt=vt, h=h, po=po, nq=nq, nkc=nkc):
                            self.mm(po[:, 0:nq], vt[:, c, h * 128:(h + 1) * 128], pt[i % 4][:, 0:nq], start=(c == 0), stop=(c == nkc - 1))

                        it = dict(score=score, pv=pv)
                        if c == nkc - 1:
                            def post(po=po, ob=ob, yb=yb, ys=ys, g_=g_, nq=nq, qt=qt, h=h, tok0=tok0, L=L, last=(qt == nqt - 1)):
                                self.copy("dve", ob[0:64, 0:nq], po[64:128, 0:nq])
                                self.recip(ob[0:64, 0:nq], ob[0:64, 0:nq])
                                self.tt("dve", yb[0:64, 0:nq], po[0:64, 0:nq], ob[0:64, 0:nq], ALU.mult)
                                self.tt("pool", ys[0:64, qt * TS:qt * TS + nq], yb[0:64, 0:nq], g_[0:64, qt * TS:qt * TS + nq], ALU.mult)
                                if last:
                                    self.dma(self.Y_mla[h * 64:(h + 1) * 64, tok0:tok0 + L], ys[0:64, 0:L])
                            it["post"] = post
                        items.append(it)
        self.run_pipe(items)

    def p5(self, l):
        ps = self.ps
        SK, ES, ZR = self.SK, self.ES, self.ZR
        for h in range(8):
            kvh, g = h // 4, h % 4
            c = l * 8 + h
            self.act(SK[64:128, kvh, g * 128:(g + 1) * 128], ZR[64:128, 0:128], AF.Identity, bias=ES[64:128, c:c + 1])
        seqs = [(0, NS, 0, NS + KOFF, True), (NS, NPR, NS + KOFF, NPR, False), (NS + NPR, NPR, NS + NPR + KOFF, NPR, False)]
        Ks = self.tile("p5ks", [NS + KOFF], BF16)
        Vs = self.tile("p5vs", [(NS + KOFF) // 128, 256], BF16)
        Qt = [self.tile(f"p5qt{i}", [4, TS], BF16, dj=True) for i in range(2)]
        gsb = [self.tile(f"p5gs{i}", [2, 4, TS], BF16, dj=True) for i in range(3)]
        KsP = self.tile("p5ksp", [2 * NPR], BF16)
        VsP = self.tile("p5vsp", [(2 * NPR) // 128, 256], BF16)
        yst = [self.tile(f"p5ys{i}", [2, 4, TS], BF16, dj=True) for i in range(2)]
        pt = [self.tile(f"p5pt{i}", [TS], BF16) for i in range(4)]
        osb = [self.tile(f"p5os{i}", [TS], F32) for i in range(2)]
        rsb = [self.tile(f"p5rs{i}", [TS], F32) for i in range(2)]
        y32 = [self.tile(f"p5y{i}", [TS], F32) for i in range(2)]
        items = []
        load_fns = []
        grp = 0
        for t in range(NT):
            q_ = Qt[t % 2]
            gs_ = gsb[t % 3]
            ys = yst[t % 2]
            Kc, Vc = (Ks, Vs) if t < 8 else (KsP, VsP)

            def loads(t=t, q_=q_, gs_=gs_):
                tc_ = self.tcols(t)
                if t == 0 or t == 8:
                    kb, nk = (0, NS + KOFF) if t == 0 else (NS + KOFF, 2 * NPR)
                    Kc_, Vc_ = (Ks, Vs) if t == 0 else (KsP, VsP)
                    self.dma(Kc_[:, 0:nk], self.A_ks[:, kb:kb + nk])
                    self.dma(Vc_[:, 0:nk // 128, :], self.A_vs[kb:kb + nk, :].rr("(c p) f -> p c f", p=128))
                for kvh in range(2):
                    self.dma(q_[kvh * 64:(kvh + 1) * 64], self.A_qs[kvh * 256:(kvh + 1) * 256, tc_].rr("(g d) t -> d g t", d=64))
                    self.dma(gs_[0:64, kvh], self.A_gs[kvh * 256:(kvh + 1) * 256, tc_].rr("(g d) t -> d g t", d=64))

            load_fns.append(loads)
            gi = len(load_fns) - 1
            first_of_tile = True
            for qb in range(4):
                if t < 8:
                    n = t * 4 + qb
                    chunks = [(0, None), (1, None)]
                    for b, mk in ((n - 1, "prev"), (n, None), (n + 1, "next")):
                        if 0 <= b < NS // 128:
                            chunks.append((2 + b, mk))
                else:
                    sq_ = qb // 2
                    chunks = [(sq_ * 2, None), (sq_ * 2 + 1, None)]
                for kvh in range(2):
                    po = ps[4 + grp % 2]
                    ob = osb[grp % 2]
                    rb = rsb[grp % 2]
                    yb = y32[grp % 2]
                    grp += 1
                    nch = len(chunks)
                    for ci, (kc, mk) in enumerate(chunks):
                        def score(i, kc=kc, mk=mk, kvh=kvh, qb=qb, q_=q_, gi=gi, Ks=Kc, first=(first_of_tile and ci == 0)):
                            if first:
                                if gi == 0:
                                    load_fns[0]()
                                if gi + 1 < len(load_fns):
                                    load_fns[gi + 1]()
                            pb = ps[i % 4]
                            p0_, p1_ = kvh * 64, (kvh + 1) * 64
                            self.S.add("pe", lambda e: e.matmul(pb.ap.rearrange("p (g i) -> p g i", g=4),
                                                                lhsT=Ks.ap[p0_:p1_, kc * 128:(kc + 1) * 128],
                                                                rhs=q_.ap[p0_:p1_, :, qb * 128:(qb + 1) * 128], start=True, stop=True),
                                       reads=[Ks.reg, q_.reg], writes=[pb.reg])
                            self.act(pt[i % 4], pb, AF.Exp, scale=SWA_SCALE)
                            if mk is not None:
                                self.tt("pool", pt[i % 4], pt[i % 4], self.mprev if mk == "prev" else self.mnext, ALU.mult)

                        def pv(i, ci=ci, kc=kc, kvh=kvh, po=po, nch=nch, Vs=Vc):
                            self.mm(po, Vs[:, kc, kvh * 128:(kvh + 1) * 128], pt[i % 4], start=(ci == 0), stop=(ci == nch - 1))

                        it = dict(score=score, pv=pv)
                        first_of_tile = False
                        if ci == nch - 1:
                            def post(po=po, ob=ob, rb=rb, yb=yb, ys=ys, gs_=gs_, kvh=kvh, qb=qb, t=t, last=(qb == 3 and kvh == 1)):
                                self.tt("dve", ob[64:128, :], po[64:128, :], SK[64:128, kvh, :], ALU.add)
                                self.act(ob[64:128, :], ob[64:128, :], AF.Ln)
                                self.act(rb[0:64, :], ob[64:128, :], AF.Exp, scale=-1.0)
                                self.tt("dve", yb[0:64, :], po[0:64, :], rb[0:64, :], ALU.mult)
                                self.tt("pool", ys[0:64, kvh, :, qb * 128:(qb + 1) * 128], yb[0:64, :].rr("p (g i) -> p g i", g=4),
                                        gs_[0:64, kvh, :, qb * 128:(qb + 1) * 128], ALU.mult)
                                if last:
                                    for kv2 in range(2):
                                        self.dma(self.Y_swa[kv2 * 256:(kv2 + 1) * 256, self.tcols(t)].rr("(g d) t -> d g t", d=64),
                                                 ys[0:64, kv2])
                            it["post"] = post
                        items.append(it)
        self.run_pipe(items)

    def p6(self, l):
        ps, MOD = self.ps, self.MOD
        wr = self.tile("p6wr", [8, D], BF16)
        wm = self.tile("p6wm", [4, D], BF16)
        ws = self.tile("p6ws", [4, D], BF16)
        wo = self.tile("p6wo", [8, D], BF16)
        self.dma(wr, self.w_br_rnn[l].rr("(k p) c -> p k c", p=128), q="pool")
        self.dma(wm, self.w_br_mla[l].rr("(k p) c -> p k c", p=128), q="pool")
        self.dma(ws, self.w_br_swa[l].rr("(k p) c -> p k c", p=128), q="pool")
        self.dma(wo, self.w_out[l].rr("(k p) c -> p k c", p=128), q="pool")
        yr = [self.tile(f"p6yr{i}", [8, TS], BF16) for i in range(2)]
        ym = [self.tile(f"p6ym{i}", [4, TS], BF16) for i in range(2)]
        ysw = [self.tile(f"p6ys{i}", [4, TS], BF16) for i in range(2)]
        xt = [self.tile(f"p6xt{i}", [8, TS], F32) for i in range(2)]
        mg = [self.tile(f"p6mg{i}", [3, TS], BF16) for i in range(3)]
        U = [self.tile(f"p6u{i}", [8, TS], BF16) for i in range(2)]
        u1 = [self.tile(f"p6a{i}", [TS], BF16) for i in range(2)]
        u2 = [self.tile(f"p6b{i}", [TS], BF16) for i in range(2)]
        u3 = [self.tile(f"p6c{i}", [TS], BF16) for i in range(2)]
        e1 = [self.tile(f"p6e{i}", [TS], BF16) for i in range(2)]
        e2 = [self.tile(f"p6f{i}", [TS], BF16) for i in range(2)]
        e3 = [self.tile(f"p6g{i}", [TS], BF16) for i in range(2)]
        Yr3 = self.Y_rnn.rr("(k p) t -> p k t", p=128)
        Ym3 = self.Y_mla.rr("(k p) t -> p k t", p=128)
        Ys3 = self.Y_swa.rr("(k p) t -> p k t", p=128)
        Mg4 = self.A_mg.rr("(i j p) t -> p j i t", p=128, i=3)
        pi = 0
        mi = 0
        def p6loads(t):
            tc2 = self.tcols(t)
            self.dma(yr[t % 2], Yr3[:, :, tc2])
            self.dma(ym[t % 2], Ym3[:, :, tc2])
            self.dma(ysw[t % 2], Ys3[:, :, tc2])
            self.dma(xt[t % 2], self.XTv[:, :, tc2])

        p6loads(0)
        for t in range(NT):
            ci = 0 if t < 8 else 1
            tc_ = self.tcols(t)
            a, b, c, x, u = yr[t % 2], ym[t % 2], ysw[t % 2], xt[t % 2], U[t % 2]
            if t + 1 < NT:
                p6loads(t + 1)
            for j in range(8):
                m = mg[mi % 3]
                mi += 1
                self.dma(m, Mg4[:, j, :, tc_])
                jc = slice(j * 128, (j + 1) * 128)
                p1_, p2_, p3_ = ps[pi % 6], ps[(pi + 1) % 6], ps[(pi + 2) % 6]
                pi += 3
                for k in range(8):
                    self.mm(p1_, wr[:, k, jc], a[:, k, :], start=(k == 0), stop=(k == 7))
                for k in range(4):
                    self.mm(p2_, wm[:, k, jc], b[:, k, :], start=(k == 0), stop=(k == 3))
                for k in range(4):
                    self.mm(p3_, ws[:, k, jc], c[:, k, :], start=(k == 0), stop=(k == 3))
                i2 = j % 2
                self.copy("act", e1[i2], p1_)
                self.copy("act", e2[i2], p2_)
                self.copy("act", e3[i2], p3_)
                self.tt("dve", u1[i2], e1[i2], m[:, 0, :], ALU.mult)
                self.tt("dve", u2[i2], e2[i2], m[:, 1, :], ALU.mult)
                self.tt("dve", u3[i2], e3[i2], m[:, 2, :], ALU.mult)
                self.tt("pool", u1[i2], u1[i2], u2[i2], ALU.add)
                self.tt("pool", u[:, j, :], u1[i2], u3[i2], ALU.add)
            for j in range(8):
                jc = slice(j * 128, (j + 1) * 128)
                po = ps[6 + j % 2]
                for k in range(8):
                    self.mm(po, wo[:, k, jc], u[:, k, :], start=(k == 0), stop=(k == 7))
                self.stt("dve", x[:, j, :], po, MOD[:, l, ci, 16 + j:17 + j], x[:, j, :], ALU.mult, ALU.add)
            self.dma(self.XTv[:, :, tc_], x)


_CACHE = {}


def _get_program(dbg=None):
    key = repr(sorted((dbg or {}).items()))
    if key not in _CACHE:
        b = Builder(dbg)
        b.build()
        _CACHE[key] = b
    return _CACHE[key]


W_NAMES = ["w_mod", "b_mod", "g_norm", "w_in", "conv_w", "conv_b", "lru_wa", "lru_ba", "lru_wi", "lru_bi", "lru_lam",
           "mla_q_norm", "mla_w_uq", "mla_kv_norm", "mla_w_ukv", "swa_sink", "w_br_rnn", "w_br_mla", "w_br_swa",
           "w_out", "final_norm"]


def make_in_maps(inp, cores):
    consts = _consts()
    shared = {k: np.ascontiguousarray(np.asarray(inp[k], dtype=np.float32)) for k in W_NAMES}
    shared.update(consts)
    maps = []
    for b in cores:
        m = dict(shared)
        m["x_all"] = np.ascontiguousarray(np.concatenate(
            [np.asarray(inp["x_sample"][b]), np.asarray(inp["x_prompt"][2 * b]), np.asarray(inp["x_prompt"][2 * b + 1])], axis=0),
            dtype=np.float32)
        m["cond"] = np.ascontiguousarray(np.stack([np.asarray(inp["c"][b]), np.asarray(inp["c_ctx"])], axis=0), dtype=np.float32)
        m["cache_ckv"] = np.ascontiguousarray(np.asarray(inp["cache_mla_ckv"][b]), dtype=np.float32)
        m["cache_kr"] = np.ascontiguousarray(np.asarray(inp["cache_mla_krope"][b]), dtype=np.float32)
        m["cache_k"] = np.ascontiguousarray(np.asarray(inp["cache_swa_k"][b]).reshape(DEPTH, 256, 128), dtype=np.float32)
        m["cache_v"] = np.ascontiguousarray(np.asarray(inp["cache_swa_v"][b]).reshape(DEPTH, 256, 128), dtype=np.float32)
        m["state"] = np.ascontiguousarray(np.asarray(inp["state_rglru"][b]), dtype=np.float32)
        maps.append(m)
    return maps


def kernel(**inputs):
    prog = _get_program()
    cores = list(range(8))
    in_maps = make_in_maps(inputs, cores)
    res = run_bass_kernel_spmd(prog.nc, in_maps, core_ids=cores)
    rs = res.results
    y_sample = np.stack([rs[b]["y"][0:NS] for b in cores], axis=0)
    y_prompt = np.concatenate([rs[b]["y"][NS:].reshape(2, NPR, D) for b in cores], axis=0)
    new_ckv = np.concatenate([rs[b]["o_ckv"] for b in cores], axis=0)
    new_kr = np.concatenate([rs[b]["o_kr"] for b in cores], axis=0)
    new_k = np.concatenate([rs[b]["o_k"] for b in cores], axis=0).reshape(16, DEPTH, 256, 2, 64)
    new_v = np.concatenate([rs[b]["o_v"] for b in cores], axis=0).reshape(16, DEPTH, 256, 2, 64)
    new_h = np.concatenate([rs[b]["o_h"] for b in cores], axis=0)
    f = np.float32
    return (y_prompt.astype(f), y_sample.astype(f), new_ckv.astype(f), new_kr.astype(f), new_k.astype(f),
            new_v.astype(f), new_h.astype(f))
```

```python
import contextlib
import numpy as np
import ml_dtypes
import concourse.bass as bass
import concourse.mybir as mybir
from concourse.bass_utils import run_bass_kernel_spmd

F32 = mybir.dt.float32
BF16 = mybir.dt.bfloat16
AF = mybir.ActivationFunctionType
ALU = mybir.AluOpType

D = 1024
DEPTH = 4
NS = 4096
NPR = 256
T = NS + 2 * NPR
TS = 512
NT = T // TS
KOFF = 256
NKEY = T + KOFF
EPS = 1e-6
D_IN = 7584
C_XR, C_GR, C_CQ, C_CKV, C_KR, C_GM, C_QS, C_KS, C_VS, C_GS, C_MG = (
    0, 1024, 2048, 2432, 2688, 2720, 3232, 3744, 3872, 4000, 4512)
MLA_SCALE = 96 ** -0.5
SWA_SCALE = 64 ** -0.5

COMPUTE = ("pe", "act", "dve", "pool")
ENGS = ("pe", "act", "dve", "pool", "sp")


class Region:
    __slots__ = ("name", "writers", "readers", "prev_readers")

    def __init__(self, name):
        self.name = name
        self.writers = []
        self.readers = []
        self.prev_readers = []


class Op:
    __slots__ = ("id", "eng", "fn", "is_dma", "signal", "token", "slot", "idx", "waits", "barrier")


class Sched:
    def __init__(self, nslots=90):
        self.ops = []
        self.per_eng = {e: [] for e in ENGS}
        self.known = {e: {} for e in ENGS}
        self.regions = {}
        self.nslots = nslots
        self.slot_count = [0] * nslots
        self.slot_of = {}
        self.nsw = 12
        self.free_slots = list(range(self.nsw, nslots))
        self.free_sw = list(range(self.nsw))
        self.last = {e: None for e in COMPUTE}
        self.max_slots_used = 0
        self.dj_names = set()

    def region(self, name):
        r = self.regions.get(name)
        if r is None:
            r = Region(name)
            self.regions[name] = r
        return r

    def add(self, eng, fn, reads=(), writes=(), dma=False):
        op = Op()
        op.id = len(self.ops)
        op.eng = eng
        op.fn = fn
        op.is_dma = dma
        op.signal = False
        op.token = None
        op.slot = None
        op.waits = []
        op.barrier = None
        reads = [self.region(r) for r in dict.fromkeys(reads)]
        writes = [self.region(r) for r in dict.fromkeys(writes)]
        if dma:
            assert len(writes) == 1, "DMA op must write exactly one region"
            nm = writes[0].name
            if nm.startswith("d_") and reads and not reads[0].name.startswith("d_"):
                nm = "src:" + reads[0].name
            if nm not in self.slot_of:
                fl = self.free_sw if eng == "pool" else self.free_slots
                assert fl, "out of DMA semaphore slots"
                self.slot_of[nm] = fl.pop(0)
                self.max_slots_used = max(self.max_slots_used, len(self.slot_of))
            op.slot = self.slot_of[nm]
            self.slot_count[op.slot] += 1
            op.idx = self.slot_count[op.slot]
        else:
            op.idx = len(self.per_eng[eng]) + 1
        deps = set()
        raw = set()
        for r in reads:
            deps.update(r.writers)
            raw.update(r.writers)
        for r in writes:
            if r.name not in self.dj_names:
                deps.update(r.writers)
            deps.update(r.readers)
            deps.update(r.prev_readers)
        kn = self.known[eng]
        for d in sorted(deps, reverse=True):
            dop = self.ops[d]
            if (not dop.is_dma) and (not dma) and dop.eng == eng:
                if eng == "pe" or d not in raw:
                    continue
            if dop.is_dma:
                key = ("slot", dop.slot)
                idx = self.slot_count[dop.slot]
                if dma and op.slot == dop.slot:
                    idx -= 1
                if kn.get(key, 0) >= idx:
                    continue
                kn[key] = idx
                op.waits.append(("slot", dop.slot, idx))
                continue
            key = ("eng", dop.eng)
            if kn.get(key, 0) >= dop.idx:
                continue
            kn[key] = dop.idx
            dop.signal = True
            op.waits.append(dop)
        for r in reads:
            r.readers.append(op.id)
        for r in writes:
            if r.name in self.dj_names:
                if r.readers:
                    r.prev_readers = r.readers
                    r.readers = []
                    r.writers = [op.id]
                else:
                    r.writers.append(op.id)
            else:
                r.writers = [op.id]
                r.readers = []
                r.prev_readers = []
        self.ops.append(op)
        self.per_eng[eng].append(op)
        if not dma:
            self.last[eng] = op
        return op

    def barrier(self):
        b = Op()
        b.id = -1
        b.barrier = ([self.last[e] for e in COMPUTE if self.last[e] is not None], list(self.slot_count))
        for o in b.barrier[0]:
            o.signal = True
        for e in ENGS:
            self.per_eng[e].append(b)
            kn = self.known[e]
            for o in b.barrier[0]:
                kn[("eng", o.eng)] = o.idx
            for k, c in enumerate(self.slot_count):
                kn[("slot", k)] = c
        self.regions = {}
        self.slot_of = {}
        self.free_slots = list(range(self.nsw, self.nslots))
        self.free_sw = list(range(self.nsw))

    def emit(self, nc, es, final_slots=True):
        eng_sem = {e: es.enter_context(nc.semaphore(f"sem_{e}")) for e in COMPUTE}
        slot_sem = [es.enter_context(nc.semaphore(f"dsem{k}")) for k in range(self.nslots)]
        cnt = {e: 0 for e in COMPUTE}
        for op in self.ops:
            if op.is_dma:
                op.token = (slot_sem[op.slot], 16 * op.idx)
            elif op.signal:
                cnt[op.eng] += 1
                op.token = (eng_sem[op.eng], cnt[op.eng])
        self.stats = dict(signals=dict(cnt), nops={e: len(v) for e, v in self.per_eng.items()},
                          max_slots=self.max_slots_used, max_dma_cnt=max(self.slot_count))
        engobj = {"pe": nc.tensor, "act": nc.scalar, "dve": nc.vector, "pool": nc.gpsimd, "sp": nc.sync}
        final_counts = list(self.slot_count)
        nw = {e: 0 for e in ENGS}

        def run(engname, e):
            ek = {}

            def wait(s, v):
                if ek.get(s.num, 0) >= v:
                    return
                ek[s.num] = v
                e.wait_ge(s, v)
                nw[engname] += 1

            for op in self.per_eng[engname]:
                if op.barrier is not None:
                    lasts, slots = op.barrier
                    for o in lasts:
                        wait(*o.token)
                    for k, c in enumerate(slots):
                        if c:
                            wait(slot_sem[k], 16 * c)
                    continue
                for d in op.waits:
                    if isinstance(d, tuple):
                        wait(slot_sem[d[1]], 16 * d[2])
                    else:
                        wait(*d.token)
                ins = op.fn(e)
                if op.is_dma:
                    ins.then_inc(op.token[0], 16)
                elif op.signal:
                    ins.then_inc(op.token[0], 1)
            if engname == "sp":
                for k, c in enumerate(final_counts):
                    if c:
                        wait(slot_sem[k], 16 * c)

        with nc.Block() as block:
            @block.sync
            def _(e):
                run("sp", e)

            @block.scalar
            def _(e):
                run("act", e)

            @block.vector
            def _(e):
                run("dve", e)

            @block.gpsimd
            def _(e):
                run("pool", e)

            @block.tensor
            def _(e):
                run("pe", e)
        self.stats["nwaits"] = nw


class V:
    __slots__ = ("ap", "reg")

    def __init__(self, ap, reg):
        self.ap = ap
        self.reg = reg

    def __getitem__(self, k):
        return V(self.ap[k], self.reg)

    def rr(self, s, **kw):
        return V(self.ap.rearrange(s, **kw), self.reg)

    def sub(self, reg):
        return V(self.ap, reg)


def _esz(dt):
    return 4 if dt == F32 else 2


def _rope_tables(rot_dim, nrep):
    rows = NS // 64
    row = np.repeat(np.arange(rows, dtype=np.float32), 64)
    col = np.tile(np.arange(64, dtype=np.float32), rows)
    half = rot_dim // 2
    freqs = (np.float32(10000.0) ** (-np.arange(0, half, 2, dtype=np.float32) / np.float32(half))).astype(np.float32)
    ang_r = row[:, None] * freqs[None, :]
    ang_c = col[:, None] * freqs[None, :]
    ang = np.concatenate([ang_r, ang_r, ang_c, ang_c], axis=-1).astype(np.float32)
    cos = np.cos(ang).astype(np.float32)
    sin = np.sin(ang).astype(np.float32)
    q = rot_dim // 4
    sign = np.ones(rot_dim, np.float32)
    sign[0:q] = -1.0
    sign[2 * q:3 * q] = -1.0
    sin_s = sin * sign[None, :]
    cosT = np.tile(cos.T, (nrep, 1))
    sinT = np.tile(sin_s.T, (nrep, 1))
    perm = np.zeros(rot_dim, np.int64)
    for m in range(rot_dim):
        blk = m // q
        perm[m] = m + q if blk % 2 == 0 else m - q
    pm = np.zeros((128, 128), np.float32)
    for rep in range(nrep):
        for m in range(rot_dim):
            pm[rep * rot_dim + perm[m], rep * rot_dim + m] = 1.0
    return np.ascontiguousarray(cosT), np.ascontiguousarray(sinT), pm


def _consts():
    cos_s, sin_s, pm_s = _rope_tables(64, 2)
    cos_m, sin_m, pm_m = _rope_tables(32, 4)
    j = np.arange(128)[:, None]
    i = np.arange(128)[None, :]
    m_prev = np.tile((j >= i).astype(np.float32), (1, 4)).astype(ml_dtypes.bfloat16)
    m_next = np.tile((j <= i).astype(np.float32), (1, 4)).astype(ml_dtypes.bfloat16)
    sel = np.zeros((128, 64), np.float32)
    sel[64, :] = 1.0
    return dict(c_cos_s=cos_s, c_sin_s=sin_s, c_pm_s=pm_s, c_cos_m=cos_m, c_sin_m=sin_m, c_pm_m=pm_m,
                c_mprev=m_prev, c_mnext=m_next, c_ident=np.eye(128, dtype=np.float32), c_sel=sel)


class Builder:
    def __init__(self, dbg=None):
        self.dbg = dbg or {}
        self.nc = bass.Bass("TRN2", target_bir_lowering=False)
        self.S = Sched()
        self.es = contextlib.ExitStack()
        self.dram_in = {}
        self.dram_out = {}
        self.uid = 0

    def din(self, name, shape, dt=F32):
        t = self.nc.dram_tensor(name, list(shape), dt, kind="ExternalInput")
        self.dram_in[name] = t
        return V(t.ap(), "d_" + name)

    def dout(self, name, shape, dt=F32):
        t = self.nc.dram_tensor(name, list(shape), dt, kind="ExternalOutput")
        self.dram_out[name] = t
        self.S.dj_names.add("d_" + name)
        return V(t.ap(), "d_" + name)

    def dscr(self, name, shape, dt):
        if name in self.dbg.get("dump", ()):
            return self.dout(name, shape, dt)
        t = self.nc.dram_tensor(name, list(shape), dt, kind="Internal")
        self.S.dj_names.add("d_" + name)
        return V(t.ap(), "d_" + name)

    def arena_init(self, nbytes):
        self.arena_elems = nbytes // 2
        self.arena = self.es.enter_context(self.nc.sbuf_tensor("arena", [128, self.arena_elems], BF16))
        self.aoff = 0

    def tile(self, name, shape, dt=F32, dj=False):
        n = int(np.prod(shape))
        ne = n * (_esz(dt) // 2)
        ne = (ne + 15) // 16 * 16
        assert self.aoff + ne <= self.arena_elems, f"arena overflow at {name}: {self.aoff}+{ne}>{self.arena_elems}"
        ap = self.arena[:, self.aoff:self.aoff + n * (_esz(dt) // 2)]
        self.aoff += ne
        if dt == F32:
            ap = ap.bitcast(F32)
        if len(shape) == 2:
            ap = ap.rearrange("p (a b) -> p a b", a=shape[0])
        elif len(shape) == 3:
            ap = ap.rearrange("p (a b c) -> p a b c", a=shape[0], b=shape[1])
        self.uid += 1
        if dj:
            self.S.dj_names.add(f"{name}#{self.uid}")
        return V(ap, f"{name}#{self.uid}")

    def mark(self):
        return self.aoff

    def reset(self, m):
        self.aoff = m

    def _regs(self, *vs):
        return [v.reg for v in vs if isinstance(v, V)]

    def dma(self, out, in_, q="sp"):
        self.S.add(q, lambda e: e.dma_start(out=out.ap, in_=in_.ap), reads=[in_.reg], writes=[out.reg], dma=True)

    def mm(self, out, lhsT, rhs, start=True, stop=True):
        self.S.add("pe", lambda e: e.matmul(out.ap, lhsT=lhsT.ap, rhs=rhs.ap, start=start, stop=stop),
                   reads=[lhsT.reg, rhs.reg], writes=[out.reg])

    def transpose(self, out, in_, ident):
        self.S.add("pe", lambda e: e.transpose(out=out.ap, in_=in_.ap, identity=ident.ap),
                   reads=[in_.reg, ident.reg], writes=[out.reg])

    def act(self, out, in_, func, bias=None, scale=None):
        kw = {}
        rd = [in_.reg]
        if bias is not None:
            kw["bias"] = bias.ap if isinstance(bias, V) else bias
            rd += self._regs(bias)
        if scale is not None:
            kw["scale"] = scale.ap if isinstance(scale, V) else scale
            rd += self._regs(scale)
        self.S.add("act", lambda e: e.activation(out=out.ap, in_=in_.ap, func=func, **kw), reads=rd, writes=[out.reg])

    def _e(self, eng):
        if eng == "POOL":
            return "pool"
        if eng == "pool" and self.dbg.get("nopool", True):
            return "dve"
        return eng

    def tt(self, eng, out, in0, in1, op):
        eng = self._e(eng)
        self.S.add(eng, lambda e: e.tensor_tensor(out=out.ap, in0=in0.ap, in1=in1.ap, op=op),
                   reads=[in0.reg, in1.reg], writes=[out.reg])

    def ts(self, eng, out, in0, s1, s2, op0, op1=None):
        eng = self._e(eng)
        rd = [in0.reg] + self._regs(s1, s2)
        a1 = s1.ap if isinstance(s1, V) else s1
        a2 = s2.ap if isinstance(s2, V) else s2
        if op1 is None:
            self.S.add(eng, lambda e: e.tensor_scalar(out=out.ap, in0=in0.ap, scalar1=a1, scalar2=None, op0=op0),
                       reads=rd, writes=[out.reg])
        else:
            self.S.add(eng, lambda e: e.tensor_scalar(out=out.ap, in0=in0.ap, scalar1=a1, scalar2=a2, op0=op0, op1=op1),
                       reads=rd, writes=[out.reg])

    def stt(self, eng, out, in0, scalar, in1, op0, op1):
        eng = self._e(eng)
        rd = [in0.reg, in1.reg] + self._regs(scalar)
        sa = scalar.ap if isinstance(scalar, V) else scalar
        self.S.add(eng, lambda e: e.scalar_tensor_tensor(out=out.ap, in0=in0.ap, scalar=sa, in1=in1.ap, op0=op0, op1=op1),
                   reads=rd, writes=[out.reg])

    def copy(self, eng, out, in_):
        eng = self._e(eng)
        if eng == "act":
            self.S.add("act", lambda e: e.activation(out=out.ap, in_=in_.ap, func=AF.Copy), reads=[in_.reg], writes=[out.reg])
        else:
            self.S.add(eng, lambda e: e.tensor_copy(out=out.ap, in_=in_.ap), reads=[in_.reg], writes=[out.reg])

    def recip(self, out, in_):
        self.S.add("dve", lambda e: e.reciprocal(out=out.ap, in_=in_.ap), reads=[in_.reg], writes=[out.reg])

    def memset(self, eng, out, val):
        if eng == "act_ms":
            self.S.add("dve", lambda e: e.memset(out.ap, val), reads=[], writes=[out.reg])
            self.S.add("act", lambda e: e.activation(out=out.ap, in_=out.ap, func=AF.Copy), reads=[out.reg], writes=[out.reg])
            return
        self.S.add(eng, lambda e: e.memset(out.ap, val), reads=[], writes=[out.reg])

    def scan(self, out, d0, d1, init):
        rd = [d0.reg, d1.reg] + self._regs(init)
        ia = init.ap if isinstance(init, V) else init
        self.S.add("dve", lambda e: e.tensor_tensor_scan(out=out.ap, data0=d0.ap, data1=d1.ap, initial=ia,
                                                         op0=ALU.mult, op1=ALU.add), reads=rd, writes=[out.reg])

    def barrier(self):
        self.S.barrier()

    def build(self):
        nc = self.nc
        dbg = self.dbg
        nlayers = dbg.get("nlayers", DEPTH)
        stop_after = dbg.get("stop_after", None)
        x_all = self.din("x_all", [T, D])
        cond = self.din("cond", [2, D])
        cache_ckv = self.din("cache_ckv", [DEPTH, 256, 256])
        cache_kr = self.din("cache_kr", [DEPTH, 256, 32])
        cache_k = self.din("cache_k", [DEPTH, 256, 128])
        cache_v = self.din("cache_v", [DEPTH, 256, 128])
        state = self.din("state", [DEPTH, 2, D])
        w_mod = self.din("w_mod", [DEPTH, D, 3 * D])
        b_mod = self.din("b_mod", [DEPTH, 3 * D])
        g_norm = self.din("g_norm", [DEPTH, D])
        w_in = self.din("w_in", [DEPTH, D, D_IN])
        conv_w = self.din("conv_w", [DEPTH, 4, D])
        conv_b = self.din("conv_b", [DEPTH, D])
        lru_wa = self.din("lru_wa", [DEPTH, 2, 8, 128, 128])
        lru_ba = self.din("lru_ba", [DEPTH, 2, D])
        lru_wi = self.din("lru_wi", [DEPTH, 2, 8, 128, 128])
        lru_bi = self.din("lru_bi", [DEPTH, 2, D])
        lru_lam = self.din("lru_lam", [DEPTH, 2, D])
        mla_q_norm = self.din("mla_q_norm", [DEPTH, 384])
        mla_w_uq = self.din("mla_w_uq", [DEPTH, 384, 768])
        mla_kv_norm = self.din("mla_kv_norm", [DEPTH, 256])
        mla_w_ukv = self.din("mla_w_ukv", [DEPTH, 256, 1024])
        swa_sink = self.din("swa_sink", [DEPTH, 8])
        w_br_rnn = self.din("w_br_rnn", [DEPTH, D, D])
        w_br_mla = self.din("w_br_mla", [DEPTH, 512, D])
        w_br_swa = self.din("w_br_swa", [DEPTH, 512, D])
        w_out = self.din("w_out", [DEPTH, D, D])
        final_norm = self.din("final_norm", [D])
        c_cos_s = self.din("c_cos_s", [128, NS])
        c_sin_s = self.din("c_sin_s", [128, NS])
        c_pm_s = self.din("c_pm_s", [128, 128])
        c_cos_m = self.din("c_cos_m", [128, NS])
        c_sin_m = self.din("c_sin_m", [128, NS])
        c_pm_m = self.din("c_pm_m", [128, 128])
        c_mprev = self.din("c_mprev", [128, 512], BF16)
        c_mnext = self.din("c_mnext", [128, 512], BF16)
        c_ident = self.din("c_ident", [128, 128])
        c_sel = self.din("c_sel", [128, 64])
        y_out = self.dout("y", [T, D])
        o_ckv = self.dout("o_ckv", [2, DEPTH, 256, 256])
        o_kr = self.dout("o_kr", [2, DEPTH, 256, 32])
        o_k = self.dout("o_k", [2, DEPTH, 256, 128])
        o_v = self.dout("o_v", [2, DEPTH, 256, 128])
        o_h = self.dout("o_h", [2, DEPTH, 2, D])
        XT = self.dscr("XT", [D, T], F32)
        A_xr = self.dscr("A_xr", [1024, T], F32)
        A_gr = self.dscr("A_gr", [1024, T], BF16)
        A_cq = self.dscr("A_cq", [384, T], F32)
        A_ckv = self.dscr("A_ckv", [256, T], F32)
        KR = self.dscr("KR", [32, NKEY], BF16)
        A_gm = self.dscr("A_gm", [512, T], BF16)
        A_qs = self.dscr("A_qs", [512, T], BF16)
        A_ks = self.dscr("A_ks", [128, NKEY], BF16)
        A_vs = self.dscr("A_vs", [NKEY, 256], BF16)
        A_gs = self.dscr("A_gs", [512, T], BF16)
        A_mg = self.dscr("A_mg", [3072, T], BF16)
        Y_rnn = self.dscr("Y_rnn", [1024, T], BF16)
        Y_mla = self.dscr("Y_mla", [512, T], BF16)
        Y_swa = self.dscr("Y_swa", [512, T], BF16)
        QTN = self.dscr("QTN", [512, T], BF16)
        QTR = self.dscr("QTR", [256, T], BF16)
        KTN = self.dscr("KTN", [512, NKEY], BF16)
        VM = self.dscr("VM", [NKEY, 1024], BF16)

        self.arena_init(self.dbg.get("arena_bytes", 204 * 1024))
        ps = [V(self.es.enter_context(nc.psum_tensor(f"psum{i}", [128, 512], F32))[:], f"ps{i}") for i in range(8)]

        ident = self.tile("ident", [128], F32)
        ones = self.tile("ones", [128], F32)
        pm_s = self.tile("pm_s", [128], F32)
        pm_m = self.tile("pm_m", [128], F32)
        sel = self.tile("sel", [64], F32)
        mprev = self.tile("mprev", [512], BF16)
        mnext = self.tile("mnext", [512], BF16)
        NPT = 640
        PT = self.tile("PT", [NPT], F32)
        MOD = self.tile("MOD", [DEPTH, 2, 24], F32)
        MA = self.tile("MA", [DEPTH, 2, 8], F32)
        C1 = self.tile("C1", [64], F32)
        ES = self.tile("ES", [DEPTH * 8], F32)
        GKV = self.tile("GKV", [DEPTH, 256], F32)
        SK = self.tile("SK", [2, 512], F32)
        ZR = self.tile("ZR", [128], F32)
        CST = self.tile("CST", [4], F32)
        self.memset("pool", CST[:, 0:1], EPS)
        self.memset("pool", CST[:, 1:2], 1.0)
        self.dma(ident, c_ident)
        self.dma(pm_s, c_pm_s)
        self.dma(pm_m, c_pm_m)
        self.dma(sel, c_sel)
        self.dma(mprev, c_mprev)
        self.dma(mnext, c_mnext)
        self.memset("pool", ones, 1.0)
        self.memset("pool", ZR, 0.0)
        self.memset("pool", SK, 0.0)
        for l in range(DEPTH):
            self.S.add("sp", lambda e, l=l: e.dma_start(out=GKV.ap[:, l, :], in_=mla_kv_norm.ap[l].partition_broadcast(128)),
                       reads=[mla_kv_norm.reg], writes=[GKV.reg], dma=True)
        self.S.add("sp", lambda e: e.dma_start(out=ES.ap, in_=swa_sink.ap.rearrange("l h -> (l h)").partition_broadcast(128)),
                   reads=[swa_sink.reg], writes=[ES.reg], dma=True)
        self.act(ES, ES, AF.Exp)

        rows = []
        rows.append(("b_mod", b_mod.rr("l (n p) -> (l n) p", p=128)))
        rows.append(("g_norm", g_norm.rr("l (n p) -> (l n) p", p=128)))
        rows.append(("conv_w", conv_w.rr("l k (n p) -> (l k n) p", p=128)))
        rows.append(("conv_b", conv_b.rr("l (n p) -> (l n) p", p=128)))
        rows.append(("lru_ba", lru_ba.rr("l d (n p) -> (l d n) p", p=128)))
        rows.append(("q_norm", mla_q_norm.rr("l (n p) -> (l n) p", p=128)))
        rows.append(("kv_norm", mla_kv_norm.rr("l (n p) -> (l n) p", p=128)))
        rows.append(("final", final_norm.rr("(n p) -> n p", p=128)))
        rows.append(("lru_bi", lru_bi.rr("l d (n p) -> (l d n) p", p=128)))
        rows.append(("lru_lam", lru_lam.rr("l d (n p) -> (l d n) p", p=128)))
        rows.append(("state", state.rr("l d (n p) -> (l d n) p", p=128)))
        rows.append(("cond", cond.rr("c (n p) -> (c n) p", p=128)))
        self.pcol = {}
        col = 0
        m0 = self.mark()
        stg = [self.tile(f"pstg{i}", [128], F32) for i in range(2)]
        blocks = []
        cur = []
        curn = 0
        for name, view in rows:
            R = view.ap.shape[0]
            if curn + R > 128:
                blocks.append(cur)
                col += 128 - curn
                cur, curn = [], 0
            self.pcol[name] = col
            cur.append((curn, R, view))
            curn += R
            col += R
            if curn == 128:
                blocks.append(cur)
                cur, curn = [], 0
        if cur:
            blocks.append(cur)
        assert col <= NPT, col
        for bi, blk in enumerate(blocks):
            st = stg[bi % 2]
            nr = 0
            for (r0, n, view) in blk:
                self.dma(st[r0:r0 + n, :], view)
                nr = r0 + n
            pt = ps[bi % 2]
            self.transpose(pt[:, 0:128], st, ident)
            self.copy("dve", PT[:, bi * 128:bi * 128 + nr], pt[:, 0:nr])

        def pc(name, idx):
            c = self.pcol[name] + idx
            return PT[:, c:c + 1]

        lam0 = self.pcol["lru_lam"]
        self.act(C1, PT[:, lam0:lam0 + 64], AF.Exp, scale=-1.0)
        self.act(C1, C1, AF.Ln, bias=1.0)
        self.ts("dve", C1, C1, -8.0, None, ALU.mult)

        m0 = self.mark()
        sc = self.tile("sc", [8, 2], F32)
        c0 = self.pcol["cond"]
        for c in range(2):
            self.act(sc[:, :, c], PT[:, c0 + c * 8:c0 + c * 8 + 8], AF.Silu)
        wmb = [self.tile(f"wm{i}", [8, 512], F32) for i in range(2)]
        modrow = self.tile("modrow", [3 * D], F32)
        gi = 0
        for l in range(nlayers):
            for cg in range(6):
                wb = wmb[gi % 2]
                self.dma(wb, w_mod[l].rr("(k p) c -> p k c", p=128)[:, :, cg * 512:(cg + 1) * 512])
                pb = ps[2 + gi % 2]
                for k in range(8):
                    self.mm(pb[0:2, :], sc[:, k, :], wb[:, k, :], start=(k == 0), stop=(k == 7))
                self.copy("act", modrow[0:2, cg * 512:(cg + 1) * 512], pb[0:2, :])
                gi += 1
            pmod = ps[4 + l % 2]
            for j in range(24):
                self.transpose(pmod[:, 2 * j:2 * j + 2], modrow[0:2, j * 128:(j + 1) * 128], ident[0:2, 0:2])
            bm0 = self.pcol["b_mod"] + l * 24
            for c in range(2):
                self.tt("dve", MOD[:, l, c, :], pmod[:, 0:48].rr("p (j c) -> p j c", c=2)[:, :, c], PT[:, bm0:bm0 + 24], ALU.add)
                g0 = self.pcol["g_norm"] + l * 8
                self.stt("dve", MA[:, l, c, :], MOD[:, l, c, 8:16], 1.0, PT[:, g0:g0 + 8], ALU.add, ALU.mult)
        self.barrier()
        self.reset(m0)
        layer_mark = self.mark()
        if dbg.get("dump_pre"):
            dpre = self.dout("dbg_pre", [128, NPT + 192 + 64 + 64 + 32])
            self.dma(dpre[:, 0:NPT], PT)
            self.dma(dpre[:, NPT:NPT + 192], MOD.rr("p l c j -> p (l c j)"))
            self.dma(dpre[:, NPT + 192:NPT + 256], MA.rr("p l c j -> p (l c j)"))
            self.dma(dpre[:, NPT + 256:NPT + 320], C1)
            self.dma(dpre[:, NPT + 320:NPT + 352], ES)

        xin = [self.tile(f"xin{i}", [D], F32) for i in range(2)]
        xtt = [self.tile(f"xtt{i}", [8, TS], F32, dj=True) for i in range(2)]
        XTv = XT.rr("(n p) t -> p n t", p=128)
        for t in range(NT):
            xo = xtt[t % 2]
            for b in range(4):
                xi = xin[(t * 4 + b) % 2]
                r0 = t * TS + b * 128
                self.dma(xi, x_all[r0:r0 + 128, :])
                for half in range(2):
                    pt = ps[(b * 2 + half) % 4]
                    for n4 in range(4):
                        n = half * 4 + n4
                        self.transpose(pt[:, n4 * 128:(n4 + 1) * 128], xi[:, n * 128:(n + 1) * 128], ident)
                    eng = "act" if half == 0 else "dve"
                    self.copy(eng, xo[:, half * 4:half * 4 + 4, b * 128:(b + 1) * 128],
                              pt.rr("p (n t) -> p n t", n=4))
            self.dma(XTv[:, :, t * TS:(t + 1) * TS], xo)
        self.barrier()
        self.reset(layer_mark)
        if stop_after == "init":
            return self.finish()

        self.__dict__.update({k: v for k, v in locals().items() if k != "self"})
        for l in range(nlayers):
            self.layer(l)
            if self.stopped:
                return self.finish()
            self.barrier()

        self.reset(layer_mark)
        xt2 = [self.tile(f"fx{i}", [8, TS], F32) for i in range(2)]
        sq = self.tile("fsq", [8, TS], F32)
        rstd = [self.tile(f"frs{i}", [TS], F32) for i in range(2)]
        xn = [self.tile(f"fxn{i}", [8, TS], F32) for i in range(2)]
        yo = [self.tile(f"fyo{i}", [D], F32, dj=True) for i in range(2)]
        f0 = self.pcol["final"]
        for t in range(NT):
            x = xt2[t % 2]
            self.dma(x, XTv[:, :, t * TS:(t + 1) * TS])
            self.act(sq, x, AF.Square)
            pss = ps[t % 2]
            for n in range(8):
                self.mm(pss, ones, sq[:, n, :], start=(n == 0), stop=(n == 7))
            rs = rstd[t % 2]
            self.act(rs, pss, AF.Ln, bias=CST[:, 0:1], scale=1.0 / D)
            self.act(rs, rs, AF.Exp, scale=-0.5)
            xo = xn[t % 2]
            for n in range(8):
                self.stt("dve" if n % 2 == 0 else "pool", xo[:, n, :], x[:, n, :], PT[:, f0 + n:f0 + n + 1], rs, ALU.mult, ALU.mult)
            for b in range(4):
                y = yo[(t * 4 + b) % 2]
                for half in range(2):
                    pt = ps[2 + (b * 2 + half) % 4]
                    for n4 in range(4):
                        n = half * 4 + n4
                        self.transpose(pt[:, n4 * 128:(n4 + 1) * 128], xo[:, n, b * 128:(b + 1) * 128], ident)
                    self.copy("act" if half == 0 else "dve", y[:, half * 512:(half + 1) * 512], pt)
                r0 = t * TS + b * 128
                self.dma(y_out[r0:r0 + 128, :], y)
        return self.finish()

    def finish(self):
        self.S.emit(self.nc, self.es)
        self.es.close()
        return self.nc

    stopped = False

    def stop(self, l, name):
        sa = self.dbg.get("stop_after", None)
        if sa == (l, name):
            self.stopped = True
        return self.stopped

    def layer(self, l):
        self.reset(self.layer_mark)
        self.p0(l)
        if self.stop(l, "p0"):
            dh = self.dout("dbg_hT", [D, T], BF16)
            self.dma(dh.rr("(n p) t -> p n t", p=128), self.hT)
            return
        self.barrier()
        self.p1(l)
        if self.stopped or self.stop(l, "p1"):
            return
        self.barrier()
        self.reset(self.layer_mark)
        self.p2(l)
        if self.stop(l, "p2"):
            return
        self.barrier()
        self.reset(self.layer_mark)
        self.p3(l)
        if self.stopped or self.stop(l, "p3"):
            return
        self.barrier()
        self.reset(self.layer_mark)
        self.p4(l)
        if self.stop(l, "p4"):
            return
        self.barrier()
        self.reset(self.layer_mark)
        self.p5(l)
        if self.stop(l, "p5"):
            return
        self.barrier()
        self.reset(self.layer_mark)
        self.p6(l)
        if self.stop(l, "p6"):
            return

    def substop(self):
        self._sub = getattr(self, "_sub", 0) + 1
        if self.dbg.get("p1_n") == self._sub:
            self.stopped = True
        return self.stopped

    def tcols(self, t):
        return slice(t * TS, (t + 1) * TS)

    def p0(self, l):
        ps, ones, MOD, MA, XTv, CST = self.ps, self.ones, self.MOD, self.MA, self.XTv, self.CST
        self.hT = hT = self.tile("hT", [8, T], BF16, dj=True)
        self.p1_mark = self.mark()
        xt = [self.tile(f"p0x{i}", [8, TS], F32) for i in range(2)]
        sq = self.tile("p0sq", [8, TS], F32)
        rs = [self.tile(f"p0rs{i}", [TS], F32) for i in range(2)]
        tmp = [self.tile(f"p0tm{i}", [TS], F32) for i in range(4)]
        for t in range(NT):
            ci = 0 if t < 8 else 1
            x = xt[t % 2]
            self.dma(x, XTv[:, :, self.tcols(t)])
            self.act(sq, x, AF.Square)
            pss = ps[t % 2]
            for n in range(8):
                self.mm(pss, ones, sq[:, n, :], start=(n == 0), stop=(n == 7))
            r = rs[t % 2]
            self.act(r, pss, AF.Ln, bias=CST[:, 0:1], scale=1.0 / D)
            self.act(r, r, AF.Exp, scale=-0.5)
            for n in range(8):
                tm = tmp[n % 4]
                self.tt("dve" if n % 2 == 0 else "pool", tm, x[:, n, :], r, ALU.mult)
                self.act(hT[:, n, self.tcols(t)], tm, AF.Identity, bias=MOD[:, l, ci, n:n + 1], scale=MA[:, l, ci, n:n + 1])

    def p1(self, l):
        ps, hT = self.ps, self.hT
        self.reset(self.p1_mark)
        wb = [self.tile(f"p1w{i}", [8, 512], BF16) for i in range(2)]
        sf = [self.tile(f"p1sf{i}", [TS], F32) for i in range(4)]
        sb = [self.tile(f"p1sb{i}", [T], BF16, dj=True) for i in range(2)]
        cs = [self.tile(f"p1cs{i}", [TS], F32) for i in range(2)]
        sn = [self.tile(f"p1sn{i}", [TS], F32) for i in range(2)]
        qf = [self.tile(f"p1qf{i}", [TS], F32) for i in range(2)]
        t1 = [self.tile(f"p1t1{i}", [TS], F32) for i in range(2)]
        t2 = [self.tile(f"p1t2{i}", [TS], F32) for i in range(2)]
        vst = [self.tile(f"p1vs{i}", [4, 256], BF16) for i in range(2)]
        tmo = [self.tile(f"p1tm{i}", [288], F32) for i in range(2)]
        tsq = self.tile("p1tsq", [256], F32)
        tss = self.tile("p1tss", [2], F32)
        self.memset("act_ms", vst[0], 1.0)
        self.memset("dve", vst[1], 1.0)
        w_l = self.w_in[l].rr("(k p) c -> p k c", p=128)
        st = dict(pi=0, ev=0, sfi=0, sbi=0, ri=0, gi=0)

        def load_w(c0, ncol):
            w = wb[st["gi"] % 2]
            st["gi"] += 1
            self.dma(w[:, :, 0:ncol], w_l[:, :, c0:c0 + ncol], q="pool")
            return w

        def evac_eng():
            st["ev"] += 1
            return "act" if st["ev"] % 2 == 0 else "dve"

        def fm_chunk(w, wc0, M, kind, dst, dcol0=0, tables=None, pm=None):
            if kind != "f32":
                s_b = sb[st["sbi"] % 2]
                st["sbi"] += 1
            for t in range(NT):
                pb = ps[st["pi"] % 4]
                st["pi"] += 1
                for k in range(8):
                    self.mm(pb[0:M, :], w[:, k, wc0:wc0 + M], hT[:, k, self.tcols(t)], start=(k == 0), stop=(k == 7))
                if kind == "f32":
                    s = sf[st["sfi"] % 4]
                    st["sfi"] += 1
                    self.copy(evac_eng(), s[0:M, :], pb[0:M, :])
                    self.dma(dst[:, dcol0 + t * TS:dcol0 + (t + 1) * TS], s[0:M, :])
                elif kind == "bf16":
                    self.copy(evac_eng(), s_b[0:M, self.tcols(t)], pb[0:M, :])
                elif kind == "silu":
                    self.act(s_b[0:M, self.tcols(t)], pb[0:M, :], AF.Silu)
                elif kind == "sigmoid":
                    self.act(s_b[0:M, self.tcols(t)], pb[0:M, :], AF.Sigmoid)
                elif kind == "rope":
                    if t == 8:
                        self.copy(evac_eng(), s_b[0:M, self.tcols(t)], pb[0:M, :])
                    else:
                        i = st["ri"] % 2
                        st["ri"] += 1
                        self.dma(cs[i][0:M, :], tables[0][0:M, self.tcols(t)])
                        self.dma(sn[i][0:M, :], tables[1][0:M, self.tcols(t)])
                        self.copy("act", qf[i][0:M, :], pb[0:M, :])
                        pr = ps[4 + i]
                        self.mm(pr[0:M, :], pm[0:M, 0:M], qf[i][0:M, :])
                        self.tt("dve", t1[i][0:M, :], qf[i][0:M, :], cs[i][0:M, :], ALU.mult)
                        self.tt("dve", t2[i][0:M, :], pr[0:M, :], sn[i][0:M, :], ALU.mult)
                        self.tt("pool", s_b[0:M, self.tcols(t)], t1[i][0:M, :], t2[i][0:M, :], ALU.add)
            if kind != "f32":
                self.dma(dst[:, dcol0:dcol0 + T], s_b[0:M, :])

        def rows(v, r0, n=128):
            return v[r0:r0 + n, :]

        for g in range(2):
            w = load_w(C_XR + g * 512, 512)
            for j in range(4):
                n = g * 4 + j
                fm_chunk(w, j * 128, 128, "f32", rows(self.A_xr, n * 128))
        if self.substop():
            return
        for g in range(2):
            w = load_w(C_GR + g * 512, 512)
            for j in range(4):
                n = g * 4 + j
                fm_chunk(w, j * 128, 128, "silu", rows(self.A_gr, n * 128))
        if self.substop():
            return
        w = load_w(C_CQ, 384)
        for j in range(3):
            fm_chunk(w, j * 128, 128, "f32", rows(self.A_cq, j * 128))
        if self.substop():
            return
        w = load_w(C_CKV, 288)
        for j in range(2):
            fm_chunk(w, j * 128, 128, "f32", rows(self.A_ckv, j * 128))
        if self.substop():
            return
        fm_chunk(w, 256, 32, "rope", self.KR, dcol0=KOFF, tables=(self.c_cos_m, self.c_sin_m), pm=self.pm_m)
        if self.substop():
            return
        for pbk in range(4):
            seq, pos0 = pbk // 2, (pbk % 2) * 128
            tok0 = NS + pbk * 128
            pt = ps[6 + pbk % 2]
            for k in range(8):
                self.mm(pt[:, 0:288], hT[:, k, tok0:tok0 + 128], w[:, k, 0:288], start=(k == 0), stop=(k == 7))
            o = tmo[pbk % 2]
            self.act(tsq, pt[:, 0:256], AF.Square)
            self.S.add("dve", lambda e: e.reduce_sum(out=tss.ap[:, 0:1], in_=tsq.ap, axis=mybir.AxisListType.X),
                       reads=[tsq.reg], writes=[tss.reg])
            self.act(tss[:, 1:2], tss[:, 0:1], AF.Sqrt, bias=self.CST[:, 0:1], scale=1.0 / 256)
            self.recip(tss[:, 1:2], tss[:, 1:2])
            self.stt("dve", o[:, 0:256], pt[:, 0:256], tss[:, 1:2], self.GKV[:, l, :], ALU.mult, ALU.mult)
            self.copy("act", o[:, 256:288], pt[:, 256:288])
            self.dma(self.o_ckv[seq, l, pos0:pos0 + 128, :], o[:, 0:256])
            self.dma(self.o_kr[seq, l, pos0:pos0 + 128, :], o[:, 256:288])
        if self.substop():
            return
        w = load_w(C_GM, 512)
        for j in range(4):
            fm_chunk(w, j * 128, 128, "silu", rows(self.A_gm, j * 128))
        if self.substop():
            return
        w = load_w(C_QS, 512)
        for j in range(4):
            fm_chunk(w, j * 128, 128, "rope", rows(self.A_qs, j * 128), tables=(self.c_cos_s, self.c_sin_s), pm=self.pm_s)
        if self.substop():
            return
        w = load_w(C_KS, 256)
        fm_chunk(w, 0, 128, "rope", self.A_ks, dcol0=KOFF, tables=(self.c_cos_s, self.c_sin_s), pm=self.pm_s)
        if self.substop():
            return
        for t in range(NT):
            v = vst[t % 2]
            for b in range(4):
                tok0 = t * TS + b * 128
                pt = ps[6 + b % 2]
                prompt = (t == 8)
                c0 = 0 if prompt else 128
                for k in range(8):
                    self.mm(pt[:, c0:256], hT[:, k, tok0:tok0 + 128], w[:, k, c0:256], start=(k == 0), stop=(k == 7))
                if not self.dbg.get("skip_vcopy"):
                    for h2 in range(2):
                        self.copy("act" if t % 2 == 0 else "dve", v[:, b, h2 * 128:h2 * 128 + 64], pt[:, 128 + h2 * 64:192 + h2 * 64])
                if prompt:
                    seq, pos0 = b // 2, (b % 2) * 128
                    o = tmo[b % 2]
                    self.copy("act", o[:, 0:256], pt[:, 0:256])
                    self.dma(self.o_k[seq, l, pos0:pos0 + 128, :], o[:, 0:128])
                    self.dma(self.o_v[seq, l, pos0:pos0 + 128, :], o[:, 128:256])
            if not self.dbg.get("skip_vdma"):
                self.dma(self.A_vs[KOFF + t * TS:KOFF + (t + 1) * TS, :].rr("(b p) c -> p b c", p=128), v)
        if self.substop():
            return
        w = load_w(C_GS, 512)
        for j in range(4):
            fm_chunk(w, j * 128, 128, "silu", rows(self.A_gs, j * 128))
        if self.substop():
            return
        for g in range(6):
            w = load_w(C_MG + g * 512, 512)
            for j in range(4):
                fm_chunk(w, j * 128, 128, "sigmoid", rows(self.A_mg, (g * 4 + j) * 128))

    def p2(self, l):
        ps, PT = self.ps, self.PT
        segs = [(0, 0, NS, True), (NS, 1, NPR, False), (NS + NPR, 2, NPR, False)]
        XPs = [self.tile(f"p2xp{i}", [L_ + 3], F32) for i, L_ in enumerate((NS, NPR, NPR))]
        xcbs = [self.tile(f"p2xcb{i}", [T], BF16) for i in range(2)]
        RAs = [self.tile(f"p2ra{i}", [T], F32, dj=True) for i in range(2)]
        IIs = [self.tile(f"p2ii{i}", [T], BF16, dj=True) for i in range(2)]
        S2s = [self.tile(f"p2s2{i}", [T], F32) for i in range(2)]
        hf = self.tile("p2hf", [T], F32)
        hb = self.tile("p2hb", [T], F32)
        gss = [self.tile(f"p2gs{i}", [T], BF16) for i in range(2)]
        wl = [self.tile(f"p2w{i}", [4, 128], BF16, dj=True) for i in range(2)]
        HS = self.tile("p2hs", [32], F32)
        hso = self.tile("p2hso", [128], F32)
        for xp_ in XPs:
            self.memset("pool", xp_, 0.0)
        cw0, cb0 = self.pcol["conv_w"], self.pcol["conv_b"]
        ba0, bi0, st0 = self.pcol["lru_ba"], self.pcol["lru_bi"], self.pcol["state"]
        st = dict(pi=0)

        def p2loads(n):
            w_ = wl[n % 2]
            for d in range(2):
                self.dma(w_[:, d * 2 + 0, :], self.lru_wa[l, d, n], q="pool")
                self.dma(w_[:, d * 2 + 1, :], self.lru_wi[l, d, n], q="pool")
            for (tok0, xi, L, smp) in segs:
                self.dma(XPs[xi][:, 1:1 + L], self.A_xr[n * 128:(n + 1) * 128, tok0:tok0 + L])

        def p2loads_g(n):
            self.dma(gss[n % 2], self.A_gr[n * 128:(n + 1) * 128, :])

        dg = self.tile("p2dg", [4, 128], F32)

        def conv(n):
            xcb = xcbs[n % 2]
            for k in range(4):
                c = cw0 + (l * 4 + k) * 8 + n
                self.ts("dve", dg[:, k, :], self.ident, PT[:, c:c + 1], None, ALU.mult)
            bcol = PT[:, cb0 + l * 8 + n:cb0 + l * 8 + n + 1]
            for t in range(8):
                pb = ps[st["pi"] % 4]
                st["pi"] += 1
                for k in range(4):
                    self.mm(pb, dg[:, k, :], XPs[0][:, t * TS + k:t * TS + k + TS], start=(k == 0), stop=(k == 3))
                self.act(xcb[:, self.tcols(t)], pb, AF.Identity, bias=bcol)
            for si in (1, 2):
                pb = ps[st["pi"] % 4]
                st["pi"] += 1
                for k in range(4):
                    self.mm(pb[:, 0:NPR], dg[:, k, :], XPs[si][:, k:k + NPR], start=(k == 0), stop=(k == 3))
                tok0 = NS + (si - 1) * NPR
                self.act(xcb[:, tok0:tok0 + NPR], pb[:, 0:NPR], AF.Identity, bias=bcol)

        def gates(n, d):
            w = wl[n % 2]
            xcb = xcbs[n % 2]
            cc = (l * 2 + d) * 8 + n
            RA, II = RAs[d], IIs[d]
            for t in range(NT):
                pa = ps[st["pi"] % 4]
                pi_ = ps[4 + st["pi"] % 4]
                st["pi"] += 1
                self.mm(pa, w[:, d * 2 + 0, :], xcb[:, self.tcols(t)])
                self.mm(pi_, w[:, d * 2 + 1, :], xcb[:, self.tcols(t)])
                self.act(RA[:, self.tcols(t)], pa, AF.Sigmoid, bias=PT[:, ba0 + cc:ba0 + cc + 1])
                self.act(II[:, self.tcols(t)], pi_, AF.Sigmoid, bias=PT[:, bi0 + cc:bi0 + cc + 1])

        def act_part(n, d):
            cc = (l * 2 + d) * 8 + n
            RA, S2 = RAs[d], S2s[d]
            self.act(RA, RA, AF.Exp, scale=self.C1[:, cc:cc + 1])
            self.act(S2, RA, AF.Square)
            self.act(S2, S2, AF.Sqrt, bias=self.CST[:, 1:2], scale=-1.0)

        def dve_part(n, d):
            cc = (l * 2 + d) * 8 + n
            RA, II, S2, xcb = RAs[d], IIs[d], S2s[d], xcbs[n % 2]
            self.tt("dve", S2, S2, II, ALU.mult)
            self.tt("dve", S2, S2, xcb, ALU.mult)
            h = hf if d == 0 else hb
            for si, (tok0, pb, L, smp) in enumerate(segs):
                init = PT[:, st0 + cc:st0 + cc + 1] if smp else 0.0
                sl = slice(tok0, tok0 + L)
                if d == 0:
                    self.scan(h[:, sl], RA[:, sl], S2[:, sl], init)
                else:
                    self.scan(h[:, sl][:, ::-1], RA[:, sl][:, ::-1], S2[:, sl][:, ::-1], init)
                if not smp:
                    col = (si - 1) * 16 + d * 8 + n
                    src = tok0 + L - 1 if d == 0 else tok0
                    self.copy("dve", HS[:, col:col + 1], h[:, src:src + 1])

        p2loads(0)
        p2loads_g(0)
        p2loads_g(1)
        conv(0)
        p2loads(1)
        gates(0, 0)
        act_part(0, 0)
        gates(0, 1)
        for n in range(8):
            gs = gss[n % 2]
            act_part(n, 1)
            dve_part(n, 0)
            if n + 1 < 8:
                conv(n + 1)
                if n + 2 < 8:
                    p2loads(n + 2)
                gates(n + 1, 0)
                act_part(n + 1, 0)
            dve_part(n, 1)
            if n + 1 < 8:
                gates(n + 1, 1)
            self.tt("dve", hf, hf, hb, ALU.add)
            self.tt("dve", gs, hf, gs, ALU.mult)
            self.dma(self.Y_rnn[n * 128:(n + 1) * 128, :], gs)
            if n + 2 < 8:
                p2loads_g(n + 2)
        pt = ps[0]
        self.transpose(pt[0:32, 0:128], HS, self.ident)
        self.copy("dve", hso[0:32, :], pt[0:32, 0:128])
        for s_ in range(2):
            self.dma(self.o_h[s_, l].rr("d (n p) -> (d n) p", p=128), hso[s_ * 16:(s_ + 1) * 16, :])

    def p3(self, l):
        ps, PT, ident, ones = self.ps, self.PT, self.ident, self.ones
        wqn = self.tile("p3wqn", [3, 8, 64], BF16, dj=True)
        wqr = self.tile("p3wqr", [3, 8, 32], BF16, dj=True)
        wkn = self.tile("p3wkn", [2, 8, 64], BF16, dj=True)
        wv = self.tile("p3wv", [2, 8, 64], BF16, dj=True)
        wqf = self.tile("p3wqf", [3, 768], BF16)
        wkf = self.tile("p3wkf", [2, 1024], BF16)
        self.dma(wqf, self.mla_w_uq[l].rr("(k p) c -> p k c", p=128), q="pool")
        self.dma(wkf, self.mla_w_ukv[l].rr("(k p) c -> p k c", p=128), q="pool")
        ci_ = 0
        for h in range(8):
            for k in range(3):
                self.copy("dve" if ci_ % 2 else "act", wqn[:, k, h, :], wqf[:, k, h * 96:h * 96 + 64])
                self.copy("act" if ci_ % 2 else "dve", wqr[:, k, h, :], wqf[:, k, h * 96 + 64:h * 96 + 96])
                ci_ += 1
            for k in range(2):
                self.copy("dve" if ci_ % 2 else "act", wkn[:, k, h, :], wkf[:, k, h * 128:h * 128 + 64])
                self.copy("act" if ci_ % 2 else "dve", wv[:, k, h, :], wkf[:, k, h * 128 + 64:h * 128 + 128])
                ci_ += 1
        wqn2 = wqn.rr("p k h r -> p k (h r)")
        wqr2 = wqr.rr("p k h r -> p k (h r)")
        wkn2 = wkn.rr("p k h r -> p k (h r)")
        wv2 = wv.rr("p k h r -> p k (h r)")
        if self.substop():
            return
        cin = self.tile("p3cin", [2, 256], F32)
        cT = self.tile("p3cT", [2, 256], BF16)
        self.dma(cin, self.cache_ckv[l].rr("(b p) f -> p b f", p=128))
        for kf in range(2):
            pt = ps[kf]
            for b in range(2):
                self.transpose(pt[:, b * 128:(b + 1) * 128], cin[:, b, kf * 128:(kf + 1) * 128], ident)
            self.copy("dve", cT[:, kf, :], pt[:, 0:256])
        stg = [self.tile(f"p3st{i}", [TS], BF16, dj=True) for i in range(4)]
        vst = [self.tile(f"p3vs{i}", [4, 8, 128], BF16) for i in range(2)]
        self.memset("act_ms", vst[0], 1.0)
        self.memset("dve", vst[1], 1.0)
        sti = 0
        for j in range(4):
            pk = ps[2 + j % 2]
            for k in range(2):
                self.mm(pk[:, 0:256], wkn2[:, k, j * 128:(j + 1) * 128], cT[:, k, :], start=(k == 0), stop=(k == 1))
            s = stg[sti % 4]
            sti += 1
            self.copy("act", s[:, 0:256], pk[:, 0:256])
            self.dma(self.KTN[j * 128:(j + 1) * 128, 0:256], s[:, 0:256])
        v = vst[0]
        for b in range(2):
            pv = ps[4 + b]
            for k in range(2):
                self.mm(pv, cT[:, k, b * 128:(b + 1) * 128], wv2[:, k, :], start=(k == 0), stop=(k == 1))
            for h8 in range(8):
                self.copy("act", v[:, b, h8, 0:64], pv[:, h8 * 64:(h8 + 1) * 64])
        self.dma(self.VM[0:256, :].rr("(b p) c -> p b c", p=128), v[:, 0:2].rr("p b h c -> p b (h c)"))
        if self.substop():
            return
        krin = self.tile("p3kri", [2, 32], F32)
        krs = self.tile("p3krs", [256], BF16)
        self.dma(krin, self.cache_kr[l].rr("(b p) f -> p b f", p=128))
        pt = ps[6]
        for b in range(2):
            self.transpose(pt[0:32, b * 128:(b + 1) * 128], krin[:, b, :], ident)
        self.copy("act", krs[0:32, :], pt[0:32, 0:256])
        self.dma(self.KR[:, 0:256], krs[0:32, :])
        if self.substop():
            return
        skin = self.tile("p3ski", [2, 128], F32)
        sks = self.tile("p3sks", [256], BF16)
        self.dma(skin, self.cache_k[l].rr("(b p) f -> p b f", p=128))
        pt = ps[7]
        for b in range(2):
            self.transpose(pt[:, b * 128:(b + 1) * 128], skin[:, b, :], ident)
        self.copy("dve", sks, pt[:, 0:256])
        self.dma(self.A_ks[:, 0:256], sks)
        svin = self.tile("p3svi", [2, 128], F32)
        svs = self.tile("p3svs", [2, 2, 128], BF16)
        self.memset("pool", svs, 1.0)
        self.dma(svin, self.cache_v[l].rr("(b p) f -> p b f", p=128))
        for b2 in range(2):
            for h2 in range(2):
                self.copy("dve", svs[:, b2, h2, 0:64], svin[:, b2, h2 * 64:(h2 + 1) * 64])
        self.dma(self.A_vs[0:256, :].rr("(b p) c -> p b c", p=128), svs.rr("p b h c -> p b (h c)"))
        if self.substop():
            return
        cq = [self.tile(f"p3cq{i}", [3, TS], F32) for i in range(2)]
        ck = [self.tile(f"p3ck{i}", [2, TS], F32) for i in range(2)]
        sq = self.tile("p3sq", [3, TS], F32)
        rs = [self.tile(f"p3rs{i}", [TS], F32) for i in range(2)]
        cqn = [self.tile(f"p3cqn{i}", [3, TS], BF16) for i in range(2)]
        ckn = [self.tile(f"p3ckn{i}", [2, TS], BF16) for i in range(2)]
        cs = [self.tile(f"p3cs{i}", [TS], F32) for i in range(2)]
        sn = [self.tile(f"p3sn{i}", [TS], F32) for i in range(2)]
        qf = [self.tile(f"p3qf{i}", [TS], F32) for i in range(2)]
        t1 = [self.tile(f"p3t1{i}", [TS], F32) for i in range(2)]
        t2 = [self.tile(f"p3t2{i}", [TS], F32) for i in range(2)]
        A_cq3 = self.A_cq.rr("(k p) t -> p k t", p=128)
        A_ckv3 = self.A_ckv.rr("(k p) t -> p k t", p=128)
        qn0, kvn0 = self.pcol["q_norm"] + l * 3, self.pcol["kv_norm"] + l * 2
        pi = 0
        ri = 0
        def p3loads(t):
            self.dma(cq[t % 2], A_cq3[:, :, self.tcols(t)])
            self.dma(ck[t % 2], A_ckv3[:, :, self.tcols(t)])

        p3loads(0)
        for t in range(NT):
            tc_ = self.tcols(t)
            q_ = cq[t % 2]
            c_ = ck[t % 2]
            if t + 1 < NT:
                p3loads(t + 1)
            self.act(sq, q_, AF.Square)
            pss = ps[6]
            for k in range(3):
                self.mm(pss, ones, sq[:, k, :], start=(k == 0), stop=(k == 2))
            r = rs[0]
            self.act(r, pss, AF.Ln, bias=self.CST[:, 0:1], scale=1.0 / 384)
            self.act(r, r, AF.Exp, scale=-0.5)
            qn = cqn[t % 2]
            for k in range(3):
                self.stt("dve" if k != 1 else "pool", qn[:, k, :], q_[:, k, :], PT[:, qn0 + k:qn0 + k + 1], r, ALU.mult, ALU.mult)
            self.act(sq[:, 0:2, :], c_, AF.Square)
            pss = ps[7]
            for k in range(2):
                self.mm(pss, ones, sq[:, k, :], start=(k == 0), stop=(k == 1))
            r = rs[1]
            self.act(r, pss, AF.Ln, bias=self.CST[:, 0:1], scale=1.0 / 256)
            self.act(r, r, AF.Exp, scale=-0.5)
            kn = ckn[t % 2]
            for k in range(2):
                self.stt("dve" if k == 0 else "pool", kn[:, k, :], c_[:, k, :], PT[:, kvn0 + k:kvn0 + k + 1], r, ALU.mult, ALU.mult)
            for j in range(4):
                pb = ps[pi % 4]
                pi += 1
                for k in range(3):
                    self.mm(pb, wqn2[:, k, j * 128:(j + 1) * 128], qn[:, k, :], start=(k == 0), stop=(k == 2))
                s = stg[sti % 4]
                sti += 1
                self.copy("act" if j % 2 == 0 else "dve", s, pb)
                self.dma(self.QTN[j * 128:(j + 1) * 128, tc_], s)
            for j in range(2):
                pb = ps[pi % 4]
                pi += 1
                for k in range(3):
                    self.mm(pb, wqr2[:, k, j * 128:(j + 1) * 128], qn[:, k, :], start=(k == 0), stop=(k == 2))
                s = stg[sti % 4]
                sti += 1
                if t == 8:
                    self.copy("act", s, pb)
                else:
                    i = ri % 2
                    ri += 1
                    if j == 0:
                        self.dma(cs[i], self.c_cos_m[:, tc_])
                        self.dma(sn[i], self.c_sin_m[:, tc_])
                        csn = (cs[i], sn[i])
                    self.copy("act", qf[i], pb)
                    pr = ps[4 + i]
                    self.mm(pr, self.pm_m, qf[i])
                    self.tt("dve", t1[i], qf[i], csn[0], ALU.mult)
                    self.tt("dve", t2[i], pr, csn[1], ALU.mult)
                    self.tt("pool", s, t1[i], t2[i], ALU.add)
                self.dma(self.QTR[j * 128:(j + 1) * 128, tc_], s)
            for j in range(4):
                pb = ps[pi % 4]
                pi += 1
                for k in range(2):
                    self.mm(pb, wkn2[:, k, j * 128:(j + 1) * 128], kn[:, k, :], start=(k == 0), stop=(k == 1))
                s = stg[sti % 4]
                sti += 1
                self.copy("act" if j % 2 == 0 else "dve", s, pb)
                self.dma(self.KTN[j * 128:(j + 1) * 128, KOFF + t * TS:KOFF + (t + 1) * TS], s)
            v = vst[t % 2]
            for b in range(4):
                pv = ps[pi % 4]
                pi += 1
                for k in range(2):
                    self.mm(pv, kn[:, k, b * 128:(b + 1) * 128], wv2[:, k, :], start=(k == 0), stop=(k == 1))
                for h8 in range(8):
                    self.copy("act" if t % 2 == 0 else "dve", v[:, b, h8, 0:64], pv[:, h8 * 64:(h8 + 1) * 64])
            self.dma(self.VM[KOFF + t * TS:KOFF + (t + 1) * TS, :].rr("(b p) c -> p b c", p=128), v.rr("p b h c -> p b (h c)"))

    def run_pipe(self, items, la=3, pd=2):
        n = len(items)
        pend = []
        for i in range(n + la):
            if i < n:
                items[i]["score"](i)
            j = i - la
            if j >= 0:
                items[j]["pv"](j)
                if items[j].get("post"):
                    pend.append([pd, items[j]["post"]])
            npend = []
            for p in pend:
                p[0] -= 1
                if p[0] <= 0:
                    p[1]()
                else:
                    npend.append(p)
            pend = npend
        for p in pend:
            p[1]()

    def p4(self, l):
        ps = self.ps
        seqs = [(0, NS, 0, NS + KOFF), (NS, NPR, NS + KOFF, NPR), (NS + NPR, NPR, NS + NPR + KOFF, NPR)]
        Vt = self.tile("p4vt", [(NS + KOFF) // 128, 1024], BF16)
        Kt = [self.tile(f"p4kt{i}", [NS + KOFF], BF16) for i in range(2)]
        Qt = [self.tile(f"p4qt{i}", [TS], BF16) for i in range(3)]
        gm = [self.tile(f"p4gm{i}", [NS], BF16) for i in range(4)]
        yst = [self.tile(f"p4ys{i}", [NS], BF16, dj=True) for i in range(2)]
        pt = [self.tile(f"p4pt{i}", [TS], BF16) for i in range(4)]
        osb = [self.tile(f"p4os{i}", [TS], F32) for i in range(2)]
        y32 = [self.tile(f"p4y{i}", [TS], F32) for i in range(2)]
        items = []
        load_fns = []
        hq = 0
        VtP = [self.tile(f"p4vtp{i}", [2, 1024], BF16) for i in range(2)]
        for si_, (tok0, L, kb, nk) in enumerate(seqs):
            nkc = nk // 128
            vt = Vt[:, 0:nkc, :] if si_ == 0 else VtP[si_ - 1]
            for h in range(8):
                kt = Kt[h % 2]
                g_ = gm[h % 4]
                ys = yst[h % 2]
                nqt = max(1, L // TS)
                nq = min(TS, L)
                for qt in range(nqt):
                    q_ = Qt[hq % 3]
                    po = ps[4 + hq % 2]
                    ob = osb[hq % 2]
                    yb = y32[hq % 2]
                    hq += 1
                    q0 = tok0 + qt * TS

                    def loads(first_h=(h == 0 and qt == 0), first_q=(qt == 0), vt=vt, kt=kt, g_=g_, q_=q_, h=h, kb=kb, nk=nk,
                              tok0=tok0, L=L, q0=q0, nq=nq):
                        if first_h:
                            self.dma(vt, self.VM[kb:kb + nk, :].rr("(c p) f -> p c f", p=128))
                        if first_q:
                            self.dma(kt[0:64, 0:nk].sub(kt.reg + "n"), self.KTN[h * 64:(h + 1) * 64, kb:kb + nk])
                            self.dma(kt[64:96, 0:nk].sub(kt.reg + "r"), self.KR[:, kb:kb + nk])
                            self.dma(g_[0:64, 0:L], self.A_gm[h * 64:(h + 1) * 64, tok0:tok0 + L])
                        self.dma(q_[0:64, 0:nq].sub(q_.reg + "n"), self.QTN[h * 64:(h + 1) * 64, q0:q0 + nq])
                        self.dma(q_[64:96, 0:nq].sub(q_.reg + "r"), self.QTR[h * 32:(h + 1) * 32, q0:q0 + nq])

                    load_fns.append(loads)
                    gi = len(load_fns) - 1
                    for c in range(nkc):
                        def score(i, c=c, kt=kt, q_=q_, nq=nq, gi=gi):
                            if c == 0:
                                if gi == 0:
                                    load_fns[0]()
                                if gi + 1 < len(load_fns):
                                    load_fns[gi + 1]()
                            pb = ps[i % 4]
                            self.S.add("pe", lambda e: e.matmul(pb.ap[:, 0:nq], lhsT=kt.ap[0:96, c * 128:(c + 1) * 128],
                                                                rhs=q_.ap[0:96, 0:nq], start=True, stop=True),
                                       reads=[kt.reg + "n", kt.reg + "r", q_.reg + "n", q_.reg + "r"], writes=[pb.reg])
                            self.act(pt[i % 4][:, 0:nq], pb[:, 0:nq], AF.Exp, scale=MLA_SCALE)

                        def pv(i, c=c, vt=vt, h=h, po=po, nq=nq, nkc=nkc):
                            self.mm(po[:, 0:nq], vt[:, c, h * 128:(h + 1) * 128], pt[i % 4][:, 0:nq], start=(c == 0), stop=(c == nkc - 1))

                        it = dict(score=score, pv=pv)
                        if c == nkc - 1:
                            def post(po=po, ob=ob, yb=yb, ys=ys, g_=g_, nq=nq, qt=qt, h=h, tok0=tok0, L=L, last=(qt == nqt - 1)):
                                self.copy("dve", ob[0:64, 0:nq], po[64:128, 0:nq])
                                self.recip(ob[0:64, 0:nq], ob[0:64, 0:nq])
                                self.tt("dve", yb[0:64, 0:nq], po[0:64, 0:nq], ob[0:64, 0:nq], ALU.mult)
                                self.tt("pool", ys[0:64, qt * TS:qt * TS + nq], yb[0:64, 0:nq], g_[0:64, qt * TS:qt * TS + nq], ALU.mult)
                                if last:
                                    self.dma(self.Y_mla[h * 64:(h + 1) * 64, tok0:tok0 + L], ys[0:64, 0:L])
                            it["post"] = post
                        items.append(it)
        self.run_pipe(items)

    def p5(self, l):
        ps = self.ps
        SK, ES, ZR = self.SK, self.ES, self.ZR
        for h in range(8):
            kvh, g = h // 4, h % 4
            c = l * 8 + h
            self.act(SK[64:128, kvh, g * 128:(g + 1) * 128], ZR[64:128, 0:128], AF.Identity, bias=ES[64:128, c:c + 1])
        seqs = [(0, NS, 0, NS + KOFF, True), (NS, NPR, NS + KOFF, NPR, False), (NS + NPR, NPR, NS + NPR + KOFF, NPR, False)]
        Ks = self.tile("p5ks", [NS + KOFF], BF16)
        Vs = self.tile("p5vs", [(NS + KOFF) // 128, 256], BF16)
        Qt = [self.tile(f"p5qt{i}", [4, TS], BF16, dj=True) for i in range(2)]
        gsb = [self.tile(f"p5gs{i}", [2, 4, TS], BF16, dj=True) for i in range(3)]
        KsP = self.tile("p5ksp", [2 * NPR], BF16)
        VsP = self.tile("p5vsp", [(2 * NPR) // 128, 256], BF16)
        yst = [self.tile(f"p5ys{i}", [2, 4, TS], BF16, dj=True) for i in range(2)]
        pt = [self.tile(f"p5pt{i}", [TS], BF16) for i in range(4)]
        osb = [self.tile(f"p5os{i}", [TS], F32) for i in range(2)]
        rsb = [self.tile(f"p5rs{i}", [TS], F32) for i in range(2)]
        y32 = [self.tile(f"p5y{i}", [TS], F32) for i in range(2)]
        items = []
        load_fns = []
        grp = 0
        for t in range(NT):
            q_ = Qt[t % 2]
            gs_ = gsb[t % 3]
            ys = yst[t % 2]
            Kc, Vc = (Ks, Vs) if t < 8 else (KsP, VsP)

            def loads(t=t, q_=q_, gs_=gs_):
                tc_ = self.tcols(t)
                if t == 0 or t == 8:
                    kb, nk = (0, NS + KOFF) if t == 0 else (NS + KOFF, 2 * NPR)
                    Kc_, Vc_ = (Ks, Vs) if t == 0 else (KsP, VsP)
                    self.dma(Kc_[:, 0:nk], self.A_ks[:, kb:kb + nk])
                    self.dma(Vc_[:, 0:nk // 128, :], self.A_vs[kb:kb + nk, :].rr("(c p) f -> p c f", p=128))
                for kvh in range(2):
                    self.dma(q_[kvh * 64:(kvh + 1) * 64], self.A_qs[kvh * 256:(kvh + 1) * 256, tc_].rr("(g d) t -> d g t", d=64))
                    self.dma(gs_[0:64, kvh], self.A_gs[kvh * 256:(kvh + 1) * 256, tc_].rr("(g d) t -> d g t", d=64))

            load_fns.append(loads)
            gi = len(load_fns) - 1
            first_of_tile = True
            for qb in range(4):
                if t < 8:
                    n = t * 4 + qb
                    chunks = [(0, None), (1, None)]
                    for b, mk in ((n - 1, "prev"), (n, None), (n + 1, "next")):
                        if 0 <= b < NS // 128:
                            chunks.append((2 + b, mk))
                else:
                    sq_ = qb // 2
                    chunks = [(sq_ * 2, None), (sq_ * 2 + 1, None)]
                for kvh in range(2):
                    po = ps[4 + grp % 2]
                    ob = osb[grp % 2]
                    rb = rsb[grp % 2]
                    yb = y32[grp % 2]
                    grp += 1
                    nch = len(chunks)
                    for ci, (kc, mk) in enumerate(chunks):
                        def score(i, kc=kc, mk=mk, kvh=kvh, qb=qb, q_=q_, gi=gi, Ks=Kc, first=(first_of_tile and ci == 0)):
                            if first:
                                if gi == 0:
                                    load_fns[0]()
                                if gi + 1 < len(load_fns):
                                    load_fns[gi + 1]()
                            pb = ps[i % 4]
                            p0_, p1_ = kvh * 64, (kvh + 1) * 64
                            self.S.add("pe", lambda e: e.matmul(pb.ap.rearrange("p (g i) -> p g i", g=4),
                                                                lhsT=Ks.ap[p0_:p1_, kc * 128:(kc + 1) * 128],
                                                                rhs=q_.ap[p0_:p1_, :, qb * 128:(qb + 1) * 128], start=True, stop=True),
                                       reads=[Ks.reg, q_.reg], writes=[pb.reg])
                            self.act(pt[i % 4], pb, AF.Exp, scale=SWA_SCALE)
                            if mk is not None:
                                self.tt("pool", pt[i % 4], pt[i % 4], self.mprev if mk == "prev" else self.mnext, ALU.mult)

                        def pv(i, ci=ci, kc=kc, kvh=kvh, po=po, nch=nch, Vs=Vc):
                            self.mm(po, Vs[:, kc, kvh * 128:(kvh + 1) * 128], pt[i % 4], start=(ci == 0), stop=(ci == nch - 1))

                        it = dict(score=score, pv=pv)
                        first_of_tile = False
                        if ci == nch - 1:
                            def post(po=po, ob=ob, rb=rb, yb=yb, ys=ys, gs_=gs_, kvh=kvh, qb=qb, t=t, last=(qb == 3 and kvh == 1)):
                                self.tt("dve", ob[64:128, :], po[64:128, :], SK[64:128, kvh, :], ALU.add)
                                self.act(ob[64:128, :], ob[64:128, :], AF.Ln)
                                self.act(rb[0:64, :], ob[64:128, :], AF.Exp, scale=-1.0)
                                self.tt("dve", yb[0:64, :], po[0:64, :], rb[0:64, :], ALU.mult)
                                self.tt("pool", ys[0:64, kvh, :, qb * 128:(qb + 1) * 128], yb[0:64, :].rr("p (g i) -> p g i", g=4),
                                        gs_[0:64, kvh, :, qb * 128:(qb + 1) * 128], ALU.mult)
                                if last:
                                    for kv2 in range(2):
                                        self.dma(self.Y_swa[kv2 * 256:(kv2 + 1) * 256, self.tcols(t)].rr("(g d) t -> d g t", d=64),
                                                 ys[0:64, kv2])
                            it["post"] = post
                        items.append(it)
        self.run_pipe(items)

    def p6(self, l):
        ps, MOD = self.ps, self.MOD
        wr = self.tile("p6wr", [8, D], BF16)
        wm = self.tile("p6wm", [4, D], BF16)
        ws = self.tile("p6ws", [4, D], BF16)
        wo = self.tile("p6wo", [8, D], BF16)
        self.dma(wr, self.w_br_rnn[l].rr("(k p) c -> p k c", p=128), q="pool")
        self.dma(wm, self.w_br_mla[l].rr("(k p) c -> p k c", p=128), q="pool")
        self.dma(ws, self.w_br_swa[l].rr("(k p) c -> p k c", p=128), q="pool")
        self.dma(wo, self.w_out[l].rr("(k p) c -> p k c", p=128), q="pool")
        yr = [self.tile(f"p6yr{i}", [8, TS], BF16) for i in range(2)]
        ym = [self.tile(f"p6ym{i}", [4, TS], BF16) for i in range(2)]
        ysw = [self.tile(f"p6ys{i}", [4, TS], BF16) for i in range(2)]
        xt = [self.tile(f"p6xt{i}", [8, TS], F32) for i in range(2)]
        mg = [self.tile(f"p6mg{i}", [3, TS], BF16) for i in range(3)]
        U = [self.tile(f"p6u{i}", [8, TS], BF16) for i in range(2)]
        u1 = [self.tile(f"p6a{i}", [TS], BF16) for i in range(2)]
        u2 = [self.tile(f"p6b{i}", [TS], BF16) for i in range(2)]
        u3 = [self.tile(f"p6c{i}", [TS], BF16) for i in range(2)]
        e1 = [self.tile(f"p6e{i}", [TS], BF16) for i in range(2)]
        e2 = [self.tile(f"p6f{i}", [TS], BF16) for i in range(2)]
        e3 = [self.tile(f"p6g{i}", [TS], BF16) for i in range(2)]
        Yr3 = self.Y_rnn.rr("(k p) t -> p k t", p=128)
        Ym3 = self.Y_mla.rr("(k p) t -> p k t", p=128)
        Ys3 = self.Y_swa.rr("(k p) t -> p k t", p=128)
        Mg4 = self.A_mg.rr("(i j p) t -> p j i t", p=128, i=3)
        pi = 0
        mi = 0
        def p6loads(t):
            tc2 = self.tcols(t)
            self.dma(yr[t % 2], Yr3[:, :, tc2])
            self.dma(ym[t % 2], Ym3[:, :, tc2])
            self.dma(ysw[t % 2], Ys3[:, :, tc2])
            self.dma(xt[t % 2], self.XTv[:, :, tc2])

        p6loads(0)
        for t in range(NT):
            ci = 0 if t < 8 else 1
            tc_ = self.tcols(t)
            a, b, c, x, u = yr[t % 2], ym[t % 2], ysw[t % 2], xt[t % 2], U[t % 2]
            if t + 1 < NT:
                p6loads(t + 1)
            for j in range(8):
                m = mg[mi % 3]
                mi += 1
                self.dma(m, Mg4[:, j, :, tc_])
                jc = slice(j * 128, (j + 1) * 128)
                p1_, p2_, p3_ = ps[pi % 6], ps[(pi + 1) % 6], ps[(pi + 2) % 6]
                pi += 3
                for k in range(8):
                    self.mm(p1_, wr[:, k, jc], a[:, k, :], start=(k == 0), stop=(k == 7))
                for k in range(4):
                    self.mm(p2_, wm[:, k, jc], b[:, k, :], start=(k == 0), stop=(k == 3))
                for k in range(4):
                    self.mm(p3_, ws[:, k, jc], c[:, k, :], start=(k == 0), stop=(k == 3))
                i2 = j % 2
                self.copy("act", e1[i2], p1_)
                self.copy("act", e2[i2], p2_)
                self.copy("act", e3[i2], p3_)
                self.tt("dve", u1[i2], e1[i2], m[:, 0, :], ALU.mult)
                self.tt("dve", u2[i2], e2[i2], m[:, 1, :], ALU.mult)
                self.tt("dve", u3[i2], e3[i2], m[:, 2, :], ALU.mult)
                self.tt("pool", u1[i2], u1[i2], u2[i2], ALU.add)
                self.tt("pool", u[:, j, :], u1[i2], u3[i2], ALU.add)
            for j in range(8):
                jc = slice(j * 128, (j + 1) * 128)
                po = ps[6 + j % 2]
                for k in range(8):
                    self.mm(po, wo[:, k, jc], u[:, k, :], start=(k == 0), stop=(k == 7))
                self.stt("dve", x[:, j, :], po, MOD[:, l, ci, 16 + j:17 + j], x[:, j, :], ALU.mult, ALU.add)
            self.dma(self.XTv[:, :, tc_], x)


_CACHE = {}


def _get_program(dbg=None):
    key = repr(sorted((dbg or {}).items()))
    if key not in _CACHE:
        b = Builder(dbg)
        b.build()
        _CACHE[key] = b
    return _CACHE[key]


W_NAMES = ["w_mod", "b_mod", "g_norm", "w_in", "conv_w", "conv_b", "lru_wa", "lru_ba", "lru_wi", "lru_bi", "lru_lam",
           "mla_q_norm", "mla_w_uq", "mla_kv_norm", "mla_w_ukv", "swa_sink", "w_br_rnn", "w_br_mla", "w_br_swa",
           "w_out", "final_norm"]


def make_in_maps(inp, cores):
    consts = _consts()
    shared = {k: np.ascontiguousarray(np.asarray(inp[k], dtype=np.float32)) for k in W_NAMES}
    shared.update(consts)
    maps = []
    for b in cores:
        m = dict(shared)
        m["x_all"] = np.ascontiguousarray(np.concatenate(
            [np.asarray(inp["x_sample"][b]), np.asarray(inp["x_prompt"][2 * b]), np.asarray(inp["x_prompt"][2 * b + 1])], axis=0),
            dtype=np.float32)
        m["cond"] = np.ascontiguousarray(np.stack([np.asarray(inp["c"][b]), np.asarray(inp["c_ctx"])], axis=0), dtype=np.float32)
        m["cache_ckv"] = np.ascontiguousarray(np.asarray(inp["cache_mla_ckv"][b]), dtype=np.float32)
        m["cache_kr"] = np.ascontiguousarray(np.asarray(inp["cache_mla_krope"][b]), dtype=np.float32)
        m["cache_k"] = np.ascontiguousarray(np.asarray(inp["cache_swa_k"][b]).reshape(DEPTH, 256, 128), dtype=np.float32)
        m["cache_v"] = np.ascontiguousarray(np.asarray(inp["cache_swa_v"][b]).reshape(DEPTH, 256, 128), dtype=np.float32)
        m["state"] = np.ascontiguousarray(np.asarray(inp["state_rglru"][b]), dtype=np.float32)
        maps.append(m)
    return maps


def kernel(**inputs):
    prog = _get_program()
    cores = list(range(8))
    in_maps = make_in_maps(inputs, cores)
    res = run_bass_kernel_spmd(prog.nc, in_maps, core_ids=cores)
    rs = res.results
    y_sample = np.stack([rs[b]["y"][0:NS] for b in cores], axis=0)
    y_prompt = np.concatenate([rs[b]["y"][NS:].reshape(2, NPR, D) for b in cores], axis=0)
    new_ckv = np.concatenate([rs[b]["o_ckv"] for b in cores], axis=0)
    new_kr = np.concatenate([rs[b]["o_kr"] for b in cores], axis=0)
    new_k = np.concatenate([rs[b]["o_k"] for b in cores], axis=0).reshape(16, DEPTH, 256, 2, 64)
    new_v = np.concatenate([rs[b]["o_v"] for b in cores], axis=0).reshape(16, DEPTH, 256, 2, 64)
    new_h = np.concatenate([rs[b]["o_h"] for b in cores], axis=0)
    f = np.float32
    return (y_prompt.astype(f), y_sample.astype(f), new_ckv.astype(f), new_kr.astype(f), new_k.astype(f),
            new_v.astype(f), new_h.astype(f))
```

```python
import contextlib
import numpy as np
import ml_dtypes
import concourse.bass as bass
import concourse.mybir as mybir
from concourse.bass_utils import run_bass_kernel_spmd

F32 = mybir.dt.float32
BF16 = mybir.dt.bfloat16
AF = mybir.ActivationFunctionType
ALU = mybir.AluOpType

D = 1024
DEPTH = 4
NS = 4096
NPR = 256
T = NS + 2 * NPR
TS = 512
NT = T // TS
KOFF = 256
NKEY = T + KOFF
EPS = 1e-6
D_IN = 7584
C_XR, C_GR, C_CQ, C_CKV, C_KR, C_GM, C_QS, C_KS, C_VS, C_GS, C_MG = (
    0, 1024, 2048, 2432, 2688, 2720, 3232, 3744, 3872, 4000, 4512)
MLA_SCALE = 96 ** -0.5
SWA_SCALE = 64 ** -0.5

COMPUTE = ("pe", "act", "dve", "pool")
ENGS = ("pe", "act", "dve", "pool", "sp")


class Region:
    __slots__ = ("name", "writers", "readers", "prev_readers")

    def __init__(self, name):
        self.name = name
        self.writers = []
        self.readers = []
        self.prev_readers = []


class Op:
    __slots__ = ("id", "eng", "fn", "is_dma", "signal", "token", "slot", "idx", "waits", "barrier")


class Sched:
    def __init__(self, nslots=90):
        self.ops = []
        self.per_eng = {e: [] for e in ENGS}
        self.known = {e: {} for e in ENGS}
        self.regions = {}
        self.nslots = nslots
        self.slot_count = [0] * nslots
        self.slot_of = {}
        self.nsw = 12
        self.free_slots = list(range(self.nsw, nslots))
        self.free_sw = list(range(self.nsw))
        self.last = {e: None for e in COMPUTE}
        self.max_slots_used = 0
        self.dj_names = set()

    def region(self, name):
        r = self.regions.get(name)
        if r is None:
            r = Region(name)
            self.regions[name] = r
        return r

    def add(self, eng, fn, reads=(), writes=(), dma=False):
        op = Op()
        op.id = len(self.ops)
        op.eng = eng
        op.fn = fn
        op.is_dma = dma
        op.signal = False
        op.token = None
        op.slot = None
        op.waits = []
        op.barrier = None
        reads = [self.region(r) for r in dict.fromkeys(reads)]
        writes = [self.region(r) for r in dict.fromkeys(writes)]
        if dma:
            assert len(writes) == 1, "DMA op must write exactly one region"
            nm = writes[0].name
            if nm.startswith("d_") and reads and not reads[0].name.startswith("d_"):
                nm = "src:" + reads[0].name
            if nm not in self.slot_of:
                fl = self.free_sw if eng == "pool" else self.free_slots
                assert fl, "out of DMA semaphore slots"
                self.slot_of[nm] = fl.pop(0)
                self.max_slots_used = max(self.max_slots_used, len(self.slot_of))
            op.slot = self.slot_of[nm]
            self.slot_count[op.slot] += 1
            op.idx = self.slot_count[op.slot]
        else:
            op.idx = len(self.per_eng[eng]) + 1
        deps = set()
        raw = set()
        for r in reads:
            deps.update(r.writers)
            raw.update(r.writers)
        for r in writes:
            if r.name not in self.dj_names:
                deps.update(r.writers)
            deps.update(r.readers)
            deps.update(r.prev_readers)
        kn = self.known[eng]
        for d in sorted(deps, reverse=True):
            dop = self.ops[d]
            if (not dop.is_dma) and (not dma) and dop.eng == eng:
                if eng == "pe" or d not in raw:
                    continue
            if dop.is_dma:
                key = ("slot", dop.slot)
                idx = self.slot_count[dop.slot]
                if dma and op.slot == dop.slot:
                    idx -= 1
                if kn.get(key, 0) >= idx:
                    continue
                kn[key] = idx
                op.waits.append(("slot", dop.slot, idx))
                continue
            key = ("eng", dop.eng)
            if kn.get(key, 0) >= dop.idx:
                continue
            kn[key] = dop.idx
            dop.signal = True
            op.waits.append(dop)
        for r in reads:
            r.readers.append(op.id)
        for r in writes:
            if r.name in self.dj_names:
                if r.readers:
                    r.prev_readers = r.readers
                    r.readers = []
                    r.writers = [op.id]
                else:
                    r.writers.append(op.id)
            else:
                r.writers = [op.id]
                r.readers = []
                r.prev_readers = []
        self.ops.append(op)
        self.per_eng[eng].append(op)
        if not dma:
            self.last[eng] = op
        return op

    def barrier(self):
        b = Op()
        b.id = -1
        b.barrier = ([self.last[e] for e in COMPUTE if self.last[e] is not None], list(self.slot_count))
        for o in b.barrier[0]:
            o.signal = True
        for e in ENGS:
            self.per_eng[e].append(b)
            kn = self.known[e]
            for o in b.barrier[0]:
                kn[("eng", o.eng)] = o.idx
            for k, c in enumerate(self.slot_count):
                kn[("slot", k)] = c
        self.regions = {}
        self.slot_of = {}
        self.free_slots = list(range(self.nsw, self.nslots))
        self.free_sw = list(range(self.nsw))

    def emit(self, nc, es, final_slots=True):
        eng_sem = {e: es.enter_context(nc.semaphore(f"sem_{e}")) for e in COMPUTE}
        slot_sem = [es.enter_context(nc.semaphore(f"dsem{k}")) for k in range(self.nslots)]
        cnt = {e: 0 for e in COMPUTE}
        for op in self.ops:
            if op.is_dma:
                op.token = (slot_sem[op.slot], 16 * op.idx)
            elif op.signal:
                cnt[op.eng] += 1
                op.token = (eng_sem[op.eng], cnt[op.eng])
        self.stats = dict(signals=dict(cnt), nops={e: len(v) for e, v in self.per_eng.items()},
                          max_slots=self.max_slots_used, max_dma_cnt=max(self.slot_count))
        engobj = {"pe": nc.tensor, "act": nc.scalar, "dve": nc.vector, "pool": nc.gpsimd, "sp": nc.sync}
        final_counts = list(self.slot_count)
        nw = {e: 0 for e in ENGS}

        def run(engname, e):
            ek = {}

            def wait(s, v):
                if ek.get(s.num, 0) >= v:
                    return
                ek[s.num] = v
                e.wait_ge(s, v)
                nw[engname] += 1

            for op in self.per_eng[engname]:
                if op.barrier is not None:
                    lasts, slots = op.barrier
                    for o in lasts:
                        wait(*o.token)
                    for k, c in enumerate(slots):
                        if c:
                            wait(slot_sem[k], 16 * c)
                    continue
                for d in op.waits:
                    if isinstance(d, tuple):
                        wait(slot_sem[d[1]], 16 * d[2])
                    else:
                        wait(*d.token)
                ins = op.fn(e)
                if op.is_dma:
                    ins.then_inc(op.token[0], 16)
                elif op.signal:
                    ins.then_inc(op.token[0], 1)
            if engname == "sp":
                for k, c in enumerate(final_counts):
                    if c:
                        wait(slot_sem[k], 16 * c)

        with nc.Block() as block:
            @block.sync
            def _(e):
                run("sp", e)

            @block.scalar
            def _(e):
                run("act", e)

            @block.vector
            def _(e):
                run("dve", e)

            @block.gpsimd
            def _(e):
                run("pool", e)

            @block.tensor
            def _(e):
                run("pe", e)
        self.stats["nwaits"] = nw


class V:
    __slots__ = ("ap", "reg")

    def __init__(self, ap, reg):
        self.ap = ap
        self.reg = reg

    def __getitem__(self, k):
        return V(self.ap[k], self.reg)

    def rr(self, s, **kw):
        return V(self.ap.rearrange(s, **kw), self.reg)

    def sub(self, reg):
        return V(self.ap, reg)


def _esz(dt):
    return 4 if dt == F32 else 2


def _rope_tables(rot_dim, nrep):
    rows = NS // 64
    row = np.repeat(np.arange(rows, dtype=np.float32), 64)
    col = np.tile(np.arange(64, dtype=np.float32), rows)
    half = rot_dim // 2
    freqs = (np.float32(10000.0) ** (-np.arange(0, half, 2, dtype=np.float32) / np.float32(half))).astype(np.float32)
    ang_r = row[:, None] * freqs[None, :]
    ang_c = col[:, None] * freqs[None, :]
    ang = np.concatenate([ang_r, ang_r, ang_c, ang_c], axis=-1).astype(np.float32)
    cos = np.cos(ang).astype(np.float32)
    sin = np.sin(ang).astype(np.float32)
    q = rot_dim // 4
    sign = np.ones(rot_dim, np.float32)
    sign[0:q] = -1.0
    sign[2 * q:3 * q] = -1.0
    sin_s = sin * sign[None, :]
    cosT = np.tile(cos.T, (nrep, 1))
    sinT = np.tile(sin_s.T, (nrep, 1))
    perm = np.zeros(rot_dim, np.int64)
    for m in range(rot_dim):
        blk = m // q
        perm[m] = m + q if blk % 2 == 0 else m - q
    pm = np.zeros((128, 128), np.float32)
    for rep in range(nrep):
        for m in range(rot_dim):
            pm[rep * rot_dim + perm[m], rep * rot_dim + m] = 1.0
    return np.ascontiguousarray(cosT), np.ascontiguousarray(sinT), pm


def _consts():
    cos_s, sin_s, pm_s = _rope_tables(64, 2)
    cos_m, sin_m, pm_m = _rope_tables(32, 4)
    j = np.arange(128)[:, None]
    i = np.arange(128)[None, :]
    m_prev = np.tile((j >= i).astype(np.float32), (1, 4)).astype(ml_dtypes.bfloat16)
    m_next = np.tile((j <= i).astype(np.float32), (1, 4)).astype(ml_dtypes.bfloat16)
    sel = np.zeros((128, 64), np.float32)
    sel[64, :] = 1.0
    return dict(c_cos_s=cos_s, c_sin_s=sin_s, c_pm_s=pm_s, c_cos_m=cos_m, c_sin_m=sin_m, c_pm_m=pm_m,
                c_mprev=m_prev, c_mnext=m_next, c_ident=np.eye(128, dtype=np.float32), c_sel=sel)


class Builder:
    def __init__(self, dbg=None):
        self.dbg = dbg or {}
        self.nc = bass.Bass("TRN2", target_bir_lowering=False)
        self.S = Sched()
        self.es = contextlib.ExitStack()
        self.dram_in = {}
        self.dram_out = {}
        self.uid = 0

    def din(self, name, shape, dt=F32):
        t = self.nc.dram_tensor(name, list(shape), dt, kind="ExternalInput")
        self.dram_in[name] = t
        return V(t.ap(), "d_" + name)

    def dout(self, name, shape, dt=F32):
        t = self.nc.dram_tensor(name, list(shape), dt, kind="ExternalOutput")
        self.dram_out[name] = t
        self.S.dj_names.add("d_" + name)
        return V(t.ap(), "d_" + name)

    def dscr(self, name, shape, dt):
        if name in self.dbg.get("dump", ()):
            return self.dout(name, shape, dt)
        t = self.nc.dram_tensor(name, list(shape), dt, kind="Internal")
        self.S.dj_names.add("d_" + name)
        return V(t.ap(), "d_" + name)

    def arena_init(self, nbytes):
        self.arena_elems = nbytes // 2
        self.arena = self.es.enter_context(self.nc.sbuf_tensor("arena", [128, self.arena_elems], BF16))
        self.aoff = 0

    def tile(self, name, shape, dt=F32, dj=False):
        n = int(np.prod(shape))
        ne = n * (_esz(dt) // 2)
        ne = (ne + 15) // 16 * 16
        assert self.aoff + ne <= self.arena_elems, f"arena overflow at {name}: {self.aoff}+{ne}>{self.arena_elems}"
        ap = self.arena[:, self.aoff:self.aoff + n * (_esz(dt) // 2)]
        self.aoff += ne
        if dt == F32:
            ap = ap.bitcast(F32)
        if len(shape) == 2:
            ap = ap.rearrange("p (a b) -> p a b", a=shape[0])
        elif len(shape) == 3:
            ap = ap.rearrange("p (a b c) -> p a b c", a=shape[0], b=shape[1])
        self.uid += 1
        if dj:
            self.S.dj_names.add(f"{name}#{self.uid}")
        return V(ap, f"{name}#{self.uid}")

    def mark(self):
        return self.aoff

    def reset(self, m):
        self.aoff = m

    def _regs(self, *vs):
        return [v.reg for v in vs if isinstance(v, V)]

    def dma(self, out, in_, q="sp"):
        self.S.add(q, lambda e: e.dma_start(out=out.ap, in_=in_.ap), reads=[in_.reg], writes=[out.reg], dma=True)

    def mm(self, out, lhsT, rhs, start=True, stop=True):
        self.S.add("pe", lambda e: e.matmul(out.ap, lhsT=lhsT.ap, rhs=rhs.ap, start=start, stop=stop),
                   reads=[lhsT.reg, rhs.reg], writes=[out.reg])

    def transpose(self, out, in_, ident):
        self.S.add("pe", lambda e: e.transpose(out=out.ap, in_=in_.ap, identity=ident.ap),
                   reads=[in_.reg, ident.reg], writes=[out.reg])

    def act(self, out, in_, func, bias=None, scale=None):
        kw = {}
        rd = [in_.reg]
        if bias is not None:
            kw["bias"] = bias.ap if isinstance(bias, V) else bias
            rd += self._regs(bias)
        if scale is not None:
            kw["scale"] = scale.ap if isinstance(scale, V) else scale
            rd += self._regs(scale)
        self.S.add("act", lambda e: e.activation(out=out.ap, in_=in_.ap, func=func, **kw), reads=rd, writes=[out.reg])

    def _e(self, eng):
        if eng == "POOL":
            return "pool"
        if eng == "pool" and self.dbg.get("nopool", True):
            return "dve"
        return eng

    def tt(self, eng, out, in0, in1, op):
        eng = self._e(eng)
        self.S.add(eng, lambda e: e.tensor_tensor(out=out.ap, in0=in0.ap, in1=in1.ap, op=op),
                   reads=[in0.reg, in1.reg], writes=[out.reg])

    def ts(self, eng, out, in0, s1, s2, op0, op1=None):
        eng = self._e(eng)
        rd = [in0.reg] + self._regs(s1, s2)
        a1 = s1.ap if isinstance(s1, V) else s1
        a2 = s2.ap if isinstance(s2, V) else s2
        if op1 is None:
            self.S.add(eng, lambda e: e.tensor_scalar(out=out.ap, in0=in0.ap, scalar1=a1, scalar2=None, op0=op0),
                       reads=rd, writes=[out.reg])
        else:
            self.S.add(eng, lambda e: e.tensor_scalar(out=out.ap, in0=in0.ap, scalar1=a1, scalar2=a2, op0=op0, op1=op1),
                       reads=rd, writes=[out.reg])

    def stt(self, eng, out, in0, scalar, in1, op0, op1):
        eng = self._e(eng)
        rd = [in0.reg, in1.reg] + self._regs(scalar)
        sa = scalar.ap if isinstance(scalar, V) else scalar
        self.S.add(eng, lambda e: e.scalar_tensor_tensor(out=out.ap, in0=in0.ap, scalar=sa, in1=in1.ap, op0=op0, op1=op1),
                   reads=rd, writes=[out.reg])

    def copy(self, eng, out, in_):
        eng = self._e(eng)
        if eng == "act":
            self.S.add("act", lambda e: e.activation(out=out.ap, in_=in_.ap, func=AF.Copy), reads=[in_.reg], writes=[out.reg])
        else:
            self.S.add(eng, lambda e: e.tensor_copy(out=out.ap, in_=in_.ap), reads=[in_.reg], writes=[out.reg])

    def recip(self, out, in_):
        self.S.add("dve", lambda e: e.reciprocal(out=out.ap, in_=in_.ap), reads=[in_.reg], writes=[out.reg])

    def memset(self, eng, out, val):
        if eng == "act_ms":
            self.S.add("dve", lambda e: e.memset(out.ap, val), reads=[], writes=[out.reg])
            self.S.add("act", lambda e: e.activation(out=out.ap, in_=out.ap, func=AF.Copy), reads=[out.reg], writes=[out.reg])
            return
        self.S.add(eng, lambda e: e.memset(out.ap, val), reads=[], writes=[out.reg])

    def scan(self, out, d0, d1, init):
        rd = [d0.reg, d1.reg] + self._regs(init)
        ia = init.ap if isinstance(init, V) else init
        self.S.add("dve", lambda e: e.tensor_tensor_scan(out=out.ap, data0=d0.ap, data1=d1.ap, initial=ia,
                                                         op0=ALU.mult, op1=ALU.add), reads=rd, writes=[out.reg])

    def barrier(self):
        self.S.barrier()

    def build(self):
        nc = self.nc
        dbg = self.dbg
        nlayers = dbg.get("nlayers", DEPTH)
        stop_after = dbg.get("stop_after", None)
        x_all = self.din("x_all", [T, D])
        cond = self.din("cond", [2, D])
        cache_ckv = self.din("cache_ckv", [DEPTH, 256, 256])
        cache_kr = self.din("cache_kr", [DEPTH, 256, 32])
        cache_k = self.din("cache_k", [DEPTH, 256, 128])
        cache_v = self.din("cache_v", [DEPTH, 256, 128])
        state = self.din("state", [DEPTH, 2, D])
        w_mod = self.din("w_mod", [DEPTH, D, 3 * D])
        b_mod = self.din("b_mod", [DEPTH, 3 * D])
        g_norm = self.din("g_norm", [DEPTH, D])
        w_in = self.din("w_in", [DEPTH, D, D_IN])
        conv_w = self.din("conv_w", [DEPTH, 4, D])
        conv_b = self.din("conv_b", [DEPTH, D])
        lru_wa = self.din("lru_wa", [DEPTH, 2, 8, 128, 128])
        lru_ba = self.din("lru_ba", [DEPTH, 2, D])
        lru_wi = self.din("lru_wi", [DEPTH, 2, 8, 128, 128])
        lru_bi = self.din("lru_bi", [DEPTH, 2, D])
        lru_lam = self.din("lru_lam", [DEPTH, 2, D])
        mla_q_norm = self.din("mla_q_norm", [DEPTH, 384])
        mla_w_uq = self.din("mla_w_uq", [DEPTH, 384, 768])
        mla_kv_norm = self.din("mla_kv_norm", [DEPTH, 256])
        mla_w_ukv = self.din("mla_w_ukv", [DEPTH, 256, 1024])
        swa_sink = self.din("swa_sink", [DEPTH, 8])
        w_br_rnn = self.din("w_br_rnn", [DEPTH, D, D])
        w_br_mla = self.din("w_br_mla", [DEPTH, 512, D])
        w_br_swa = self.din("w_br_swa", [DEPTH, 512, D])
        w_out = self.din("w_out", [DEPTH, D, D])
        final_norm = self.din("final_norm", [D])
        c_cos_s = self.din("c_cos_s", [128, NS])
        c_sin_s = self.din("c_sin_s", [128, NS])
        c_pm_s = self.din("c_pm_s", [128, 128])
        c_cos_m = self.din("c_cos_m", [128, NS])
        c_sin_m = self.din("c_sin_m", [128, NS])
        c_pm_m = self.din("c_pm_m", [128, 128])
        c_mprev = self.din("c_mprev", [128, 512], BF16)
        c_mnext = self.din("c_mnext", [128, 512], BF16)
        c_ident = self.din("c_ident", [128, 128])
        c_sel = self.din("c_sel", [128, 64])
        y_out = self.dout("y", [T, D])
        o_ckv = self.dout("o_ckv", [2, DEPTH, 256, 256])
        o_kr = self.dout("o_kr", [2, DEPTH, 256, 32])
        o_k = self.dout("o_k", [2, DEPTH, 256, 128])
        o_v = self.dout("o_v", [2, DEPTH, 256, 128])
        o_h = self.dout("o_h", [2, DEPTH, 2, D])
        XT = self.dscr("XT", [D, T], F32)
        A_xr = self.dscr("A_xr", [1024, T], F32)
        A_gr = self.dscr("A_gr", [1024, T], BF16)
        A_cq = self.dscr("A_cq", [384, T], F32)
        A_ckv = self.dscr("A_ckv", [256, T], F32)
        KR = self.dscr("KR", [32, NKEY], BF16)
        A_gm = self.dscr("A_gm", [512, T], BF16)
        A_qs = self.dscr("A_qs", [512, T], BF16)
        A_ks = self.dscr("A_ks", [128, NKEY], BF16)
        A_vs = self.dscr("A_vs", [NKEY, 256], BF16)
        A_gs = self.dscr("A_gs", [512, T], BF16)
        A_mg = self.dscr("A_mg", [3072, T], BF16)
        Y_rnn = self.dscr("Y_rnn", [1024, T], BF16)
        Y_mla = self.dscr("Y_mla", [512, T], BF16)
        Y_swa = self.dscr("Y_swa", [512, T], BF16)
        QTN = self.dscr("QTN", [512, T], BF16)
        QTR = self.dscr("QTR", [256, T], BF16)
        KTN = self.dscr("KTN", [512, NKEY], BF16)
        VM = self.dscr("VM", [NKEY, 1024], BF16)

        self.arena_init(self.dbg.get("arena_bytes", 204 * 1024))
        ps = [V(self.es.enter_context(nc.psum_tensor(f"psum{i}", [128, 512], F32))[:], f"ps{i}") for i in range(8)]

        ident = self.tile("ident", [128], F32)
        ones = self.tile("ones", [128], F32)
        pm_s = self.tile("pm_s", [128], F32)
        pm_m = self.tile("pm_m", [128], F32)
        sel = self.tile("sel", [64], F32)
        mprev = self.tile("mprev", [512], BF16)
        mnext = self.tile("mnext", [512], BF16)
        NPT = 640
        PT = self.tile("PT", [NPT], F32)
        MOD = self.tile("MOD", [DEPTH, 2, 24], F32)
        MA = self.tile("MA", [DEPTH, 2, 8], F32)
        C1 = self.tile("C1", [64], F32)
        ES = self.tile("ES", [DEPTH * 8], F32)
        GKV = self.tile("GKV", [DEPTH, 256], F32)
        SK = self.tile("SK", [2, 512], F32)
        ZR = self.tile("ZR", [128], F32)
        CST = self.tile("CST", [4], F32)
        self.memset("pool", CST[:, 0:1], EPS)
        self.memset("pool", CST[:, 1:2], 1.0)
        self.dma(ident, c_ident)
        self.dma(pm_s, c_pm_s)
        self.dma(pm_m, c_pm_m)
        self.dma(sel, c_sel)
        self.dma(mprev, c_mprev)
        self.dma(mnext, c_mnext)
        self.memset("pool", ones, 1.0)
        self.memset("pool", ZR, 0.0)
        self.memset("pool", SK, 0.0)
        for l in range(DEPTH):
            self.S.add("sp", lambda e, l=l: e.dma_start(out=GKV.ap[:, l, :], in_=mla_kv_norm.ap[l].partition_broadcast(128)),
                       reads=[mla_kv_norm.reg], writes=[GKV.reg], dma=True)
        self.S.add("sp", lambda e: e.dma_start(out=ES.ap, in_=swa_sink.ap.rearrange("l h -> (l h)").partition_broadcast(128)),
                   reads=[swa_sink.reg], writes=[ES.reg], dma=True)
        self.act(ES, ES, AF.Exp)

        rows = []
        rows.append(("b_mod", b_mod.rr("l (n p) -> (l n) p", p=128)))
        rows.append(("g_norm", g_norm.rr("l (n p) -> (l n) p", p=128)))
        rows.append(("conv_w", conv_w.rr("l k (n p) -> (l k n) p", p=128)))
        rows.append(("conv_b", conv_b.rr("l (n p) -> (l n) p", p=128)))
        rows.append(("lru_ba", lru_ba.rr("l d (n p) -> (l d n) p", p=128)))
        rows.append(("q_norm", mla_q_norm.rr("l (n p) -> (l n) p", p=128)))
        rows.append(("kv_norm", mla_kv_norm.rr("l (n p) -> (l n) p", p=128)))
        rows.append(("final", final_norm.rr("(n p) -> n p", p=128)))
        rows.append(("lru_bi", lru_bi.rr("l d (n p) -> (l d n) p", p=128)))
        rows.append(("lru_lam", lru_lam.rr("l d (n p) -> (l d n) p", p=128)))
        rows.append(("state", state.rr("l d (n p) -> (l d n) p", p=128)))
        rows.append(("cond", cond.rr("c (n p) -> (c n) p", p=128)))
        self.pcol = {}
        col = 0
        m0 = self.mark()
        stg = [self.tile(f"pstg{i}", [128], F32) for i in range(2)]
        blocks = []
        cur = []
        curn = 0
        for name, view in rows:
            R = view.ap.shape[0]
            if curn + R > 128:
                blocks.append(cur)
                col += 128 - curn
                cur, curn = [], 0
            self.pcol[name] = col
            cur.append((curn, R, view))
            curn += R
            col += R
            if curn == 128:
                blocks.append(cur)
                cur, curn = [], 0
        if cur:
            blocks.append(cur)
        assert col <= NPT, col
        for bi, blk in enumerate(blocks):
            st = stg[bi % 2]
            nr = 0
            for (r0, n, view) in blk:
                self.dma(st[r0:r0 + n, :], view)
                nr = r0 + n
            pt = ps[bi % 2]
            self.transpose(pt[:, 0:128], st, ident)
            self.copy("dve", PT[:, bi * 128:bi * 128 + nr], pt[:, 0:nr])

        def pc(name, idx):
            c = self.pcol[name] + idx
            return PT[:, c:c + 1]

        lam0 = self.pcol["lru_lam"]
        self.act(C1, PT[:, lam0:lam0 + 64], AF.Exp, scale=-1.0)
        self.act(C1, C1, AF.Ln, bias=1.0)
        self.ts("dve", C1, C1, -8.0, None, ALU.mult)

        m0 = self.mark()
        sc = self.tile("sc", [8, 2], F32)
        c0 = self.pcol["cond"]
        for c in range(2):
            self.act(sc[:, :, c], PT[:, c0 + c * 8:c0 + c * 8 + 8], AF.Silu)
        wmb = [self.tile(f"wm{i}", [8, 512], F32) for i in range(2)]
        modrow = self.tile("modrow", [3 * D], F32)
        gi = 0
        for l in range(nlayers):
            for cg in range(6):
                wb = wmb[gi % 2]
                self.dma(wb, w_mod[l].rr("(k p) c -> p k c", p=128)[:, :, cg * 512:(cg + 1) * 512])
                pb = ps[2 + gi % 2]
                for k in range(8):
                    self.mm(pb[0:2, :], sc[:, k, :], wb[:, k, :], start=(k == 0), stop=(k == 7))
                self.copy("act", modrow[0:2, cg * 512:(cg + 1) * 512], pb[0:2, :])
                gi += 1
            pmod = ps[4 + l % 2]
            for j in range(24):
                self.transpose(pmod[:, 2 * j:2 * j + 2], modrow[0:2, j * 128:(j + 1) * 128], ident[0:2, 0:2])
            bm0 = self.pcol["b_mod"] + l * 24
            for c in range(2):
                self.tt("dve", MOD[:, l, c, :], pmod[:, 0:48].rr("p (j c) -> p j c", c=2)[:, :, c], PT[:, bm0:bm0 + 24], ALU.add)
                g0 = self.pcol["g_norm"] + l * 8
                self.stt("dve", MA[:, l, c, :], MOD[:, l, c, 8:16], 1.0, PT[:, g0:g0 + 8], ALU.add, ALU.mult)
        self.barrier()
        self.reset(m0)
        layer_mark = self.mark()
        if dbg.get("dump_pre"):
            dpre = self.dout("dbg_pre", [128, NPT + 192 + 64 + 64 + 32])
            self.dma(dpre[:, 0:NPT], PT)
            self.dma(dpre[:, NPT:NPT + 192], MOD.rr("p l c j -> p (l c j)"))
            self.dma(dpre[:, NPT + 192:NPT + 256], MA.rr("p l c j -> p (l c j)"))
            self.dma(dpre[:, NPT + 256:NPT + 320], C1)
            self.dma(dpre[:, NPT + 320:NPT + 352], ES)

        xin = [self.tile(f"xin{i}", [D], F32) for i in range(2)]
        xtt = [self.tile(f"xtt{i}", [8, TS], F32, dj=True) for i in range(2)]
        XTv = XT.rr("(n p) t -> p n t", p=128)
        for t in range(NT):
            xo = xtt[t % 2]
            for b in range(4):
                xi = xin[(t * 4 + b) % 2]
                r0 = t * TS + b * 128
                self.dma(xi, x_all[r0:r0 + 128, :])
                for half in range(2):
                    pt = ps[(b * 2 + half) % 4]
                    for n4 in range(4):
                        n = half * 4 + n4
                        self.transpose(pt[:, n4 * 128:(n4 + 1) * 128], xi[:, n * 128:(n + 1) * 128], ident)
                    eng = "act" if half == 0 else "dve"
                    self.copy(eng, xo[:, half * 4:half * 4 + 4, b * 128:(b + 1) * 128],
                              pt.rr("p (n t) -> p n t", n=4))
            self.dma(XTv[:, :, t * TS:(t + 1) * TS], xo)
        self.barrier()
        self.reset(layer_mark)
        if stop_after == "init":
            return self.finish()

        self.__dict__.update({k: v for k, v in locals().items() if k != "self"})
        for l in range(nlayers):
            self.layer(l)
            if self.stopped:
                return self.finish()
            self.barrier()

        self.reset(layer_mark)
        xt2 = [self.tile(f"fx{i}", [8, TS], F32) for i in range(2)]
        sq = self.tile("fsq", [8, TS], F32)
        rstd = [self.tile(f"frs{i}", [TS], F32) for i in range(2)]
        xn = [self.tile(f"fxn{i}", [8, TS], F32) for i in range(2)]
        yo = [self.tile(f"fyo{i}", [D], F32, dj=True) for i in range(2)]
        f0 = self.pcol["final"]
        for t in range(NT):
            x = xt2[t % 2]
            self.dma(x, XTv[:, :, t * TS:(t + 1) * TS])
            self.act(sq, x, AF.Square)
            pss = ps[t % 2]
            for n in range(8):
                self.mm(pss, ones, sq[:, n, :], start=(n == 0), stop=(n == 7))
            rs = rstd[t % 2]
            self.act(rs, pss, AF.Ln, bias=CST[:, 0:1], scale=1.0 / D)
            self.act(rs, rs, AF.Exp, scale=-0.5)
            xo = xn[t % 2]
            for n in range(8):
                self.stt("dve" if n % 2 == 0 else "pool", xo[:, n, :], x[:, n, :], PT[:, f0 + n:f0 + n + 1], rs, ALU.mult, ALU.mult)
            for b in range(4):
                y = yo[(t * 4 + b) % 2]
                for half in range(2):
                    pt = ps[2 + (b * 2 + half) % 4]
                    for n4 in range(4):
                        n = half * 4 + n4
                        self.transpose(pt[:, n4 * 128:(n4 + 1) * 128], xo[:, n, b * 128:(b + 1) * 128], ident)
                    self.copy("act" if half == 0 else "dve", y[:, half * 512:(half + 1) * 512], pt)
                r0 = t * TS + b * 128
                self.dma(y_out[r0:r0 + 128, :], y)
        return self.finish()

    def finish(self):
        self.S.emit(self.nc, self.es)
        self.es.close()
        return self.nc

    stopped = False

    def stop(self, l, name):
        sa = self.dbg.get("stop_after", None)
        if sa == (l, name):
            self.stopped = True
        return self.stopped

    def layer(self, l):
        self.reset(self.layer_mark)
        self.p0(l)
        if self.stop(l, "p0"):
            dh = self.dout("dbg_hT", [D, T], BF16)
            self.dma(dh.rr("(n p) t -> p n t", p=128), self.hT)
            return
        self.barrier()
        self.p1(l)
        if self.stopped or self.stop(l, "p1"):
            return
        self.barrier()
        self.reset(self.layer_mark)
        self.p2(l)
        if self.stop(l, "p2"):
            return
        self.barrier()
        self.reset(self.layer_mark)
        self.p3(l)
        if self.stopped or self.stop(l, "p3"):
            return
        self.barrier()
        self.reset(self.layer_mark)
        self.p4(l)
        if self.stop(l, "p4"):
            return
        self.barrier()
        self.reset(self.layer_mark)
        self.p5(l)
        if self.stop(l, "p5"):
            return
        self.barrier()
        self.reset(self.layer_mark)
        self.p6(l)
        if self.stop(l, "p6"):
            return

    def substop(self):
        self._sub = getattr(self, "_sub", 0) + 1
        if self.dbg.get("p1_n") == self._sub:
            self.stopped = True
        return self.stopped

    def tcols(self, t):
        return slice(t * TS, (t + 1) * TS)

    def p0(self, l):
        ps, ones, MOD, MA, XTv, CST = self.ps, self.ones, self.MOD, self.MA, self.XTv, self.CST
        self.hT = hT = self.tile("hT", [8, T], BF16, dj=True)
        self.p1_mark = self.mark()
        xt = [self.tile(f"p0x{i}", [8, TS], F32) for i in range(2)]
        sq = self.tile("p0sq", [8, TS], F32)
        rs = [self.tile(f"p0rs{i}", [TS], F32) for i in range(2)]
        tmp = [self.tile(f"p0tm{i}", [TS], F32) for i in range(4)]
        for t in range(NT):
            ci = 0 if t < 8 else 1
            x = xt[t % 2]
            self.dma(x, XTv[:, :, self.tcols(t)])
            self.act(sq, x, AF.Square)
            pss = ps[t % 2]
            for n in range(8):
                self.mm(pss, ones, sq[:, n, :], start=(n == 0), stop=(n == 7))
            r = rs[t % 2]
            self.act(r, pss, AF.Ln, bias=CST[:, 0:1], scale=1.0 / D)
            self.act(r, r, AF.Exp, scale=-0.5)
            for n in range(8):
                tm = tmp[n % 4]
                self.tt("dve" if n % 2 == 0 else "pool", tm, x[:, n, :], r, ALU.mult)
                self.act(hT[:, n, self.tcols(t)], tm, AF.Identity, bias=MOD[:, l, ci, n:n + 1], scale=MA[:, l, ci, n:n + 1])

    def p1(self, l):
        ps, hT = self.ps, self.hT
        self.reset(self.p1_mark)
        wb = [self.tile(f"p1w{i}", [8, 512], BF16) for i in range(2)]
        sf = [self.tile(f"p1sf{i}", [TS], F32) for i in range(4)]
        sb = [self.tile(f"p1sb{i}", [T], BF16, dj=True) for i in range(2)]
        cs = [self.tile(f"p1cs{i}", [TS], F32) for i in range(2)]
        sn = [self.tile(f"p1sn{i}", [TS], F32) for i in range(2)]
        qf = [self.tile(f"p1qf{i}", [TS], F32) for i in range(2)]
        t1 = [self.tile(f"p1t1{i}", [TS], F32) for i in range(2)]
        t2 = [self.tile(f"p1t2{i}", [TS], F32) for i in range(2)]
        vst = [self.tile(f"p1vs{i}", [4, 256], BF16) for i in range(2)]
        tmo = [self.tile(f"p1tm{i}", [288], F32) for i in range(2)]
        tsq = self.tile("p1tsq", [256], F32)
        tss = self.tile("p1tss", [2], F32)
        self.memset("act_ms", vst[0], 1.0)
        self.memset("dve", vst[1], 1.0)
        w_l = self.w_in[l].rr("(k p) c -> p k c", p=128)
        st = dict(pi=0, ev=0, sfi=0, sbi=0, ri=0, gi=0)

        def load_w(c0, ncol):
            w = wb[st["gi"] % 2]
            st["gi"] += 1
            self.dma(w[:, :, 0:ncol], w_l[:, :, c0:c0 + ncol], q="pool")
            return w

        def evac_eng():
            st["ev"] += 1
            return "act" if st["ev"] % 2 == 0 else "dve"

        def fm_chunk(w, wc0, M, kind, dst, dcol0=0, tables=None, pm=None):
            if kind != "f32":
                s_b = sb[st["sbi"] % 2]
                st["sbi"] += 1
            for t in range(NT):
                pb = ps[st["pi"] % 4]
                st["pi"] += 1
                for k in range(8):
                    self.mm(pb[0:M, :], w[:, k, wc0:wc0 + M], hT[:, k, self.tcols(t)], start=(k == 0), stop=(k == 7))
                if kind == "f32":
                    s = sf[st["sfi"] % 4]
                    st["sfi"] += 1
                    self.copy(evac_eng(), s[0:M, :], pb[0:M, :])
                    self.dma(dst[:, dcol0 + t * TS:dcol0 + (t + 1) * TS], s[0:M, :])
                elif kind == "bf16":
                    self.copy(evac_eng(), s_b[0:M, self.tcols(t)], pb[0:M, :])
                elif kind == "silu":
                    self.act(s_b[0:M, self.tcols(t)], pb[0:M, :], AF.Silu)
                elif kind == "sigmoid":
                    self.act(s_b[0:M, self.tcols(t)], pb[0:M, :], AF.Sigmoid)
                elif kind == "rope":
                    if t == 8:
                        self.copy(evac_eng(), s_b[0:M, self.tcols(t)], pb[0:M, :])
                    else:
                        i = st["ri"] % 2
                        st["ri"] += 1
                        self.dma(cs[i][0:M, :], tables[0][0:M, self.tcols(t)])
                        self.dma(sn[i][0:M, :], tables[1][0:M, self.tcols(t)])
                        self.copy("act", qf[i][0:M, :], pb[0:M, :])
                        pr = ps[4 + i]
                        self.mm(pr[0:M, :], pm[0:M, 0:M], qf[i][0:M, :])
                        self.tt("dve", t1[i][0:M, :], qf[i][0:M, :], cs[i][0:M, :], ALU.mult)
                        self.tt("dve", t2[i][0:M, :], pr[0:M, :], sn[i][0:M, :], ALU.mult)
                        self.tt("pool", s_b[0:M, self.tcols(t)], t1[i][0:M, :], t2[i][0:M, :], ALU.add)
            if kind != "f32":
                self.dma(dst[:, dcol0:dcol0 + T], s_b[0:M, :])

        def rows(v, r0, n=128):
            return v[r0:r0 + n, :]

        for g in range(2):
            w = load_w(C_XR + g * 512, 512)
            for j in range(4):
                n = g * 4 + j
                fm_chunk(w, j * 128, 128, "f32", rows(self.A_xr, n * 128))
        if self.substop():
            return
        for g in range(2):
            w = load_w(C_GR + g * 512, 512)
            for j in range(4):
                n = g * 4 + j
                fm_chunk(w, j * 128, 128, "silu", rows(self.A_gr, n * 128))
        if self.substop():
            return
        w = load_w(C_CQ, 384)
        for j in range(3):
            fm_chunk(w, j * 128, 128, "f32", rows(self.A_cq, j * 128))
        if self.substop():
            return
        w = load_w(C_CKV, 288)
        for j in range(2):
            fm_chunk(w, j * 128, 128, "f32", rows(self.A_ckv, j * 128))
        if self.substop():
            return
        fm_chunk(w, 256, 32, "rope", self.KR, dcol0=KOFF, tables=(self.c_cos_m, self.c_sin_m), pm=self.pm_m)
        if self.substop():
            return
        for pbk in range(4):
            seq, pos0 = pbk // 2, (pbk % 2) * 128
            tok0 = NS + pbk * 128
            pt = ps[6 + pbk % 2]
            for k in range(8):
                self.mm(pt[:, 0:288], hT[:, k, tok0:tok0 + 128], w[:, k, 0:288], start=(k == 0), stop=(k == 7))
            o = tmo[pbk % 2]
            self.act(tsq, pt[:, 0:256], AF.Square)
            self.S.add("dve", lambda e: e.reduce_sum(out=tss.ap[:, 0:1], in_=tsq.ap, axis=mybir.AxisListType.X),
                       reads=[tsq.reg], writes=[tss.reg])
            self.act(tss[:, 1:2], tss[:, 0:1], AF.Sqrt, bias=self.CST[:, 0:1], scale=1.0 / 256)
            self.recip(tss[:, 1:2], tss[:, 1:2])
            self.stt("dve", o[:, 0:256], pt[:, 0:256], tss[:, 1:2], self.GKV[:, l, :], ALU.mult, ALU.mult)
            self.copy("act", o[:, 256:288], pt[:, 256:288])
            self.dma(self.o_ckv[seq, l, pos0:pos0 + 128, :], o[:, 0:256])
            self.dma(self.o_kr[seq, l, pos0:pos0 + 128, :], o[:, 256:288])
        if self.substop():
            return
        w = load_w(C_GM, 512)
        for j in range(4):
            fm_chunk(w, j * 128, 128, "silu", rows(self.A_gm, j * 128))
        if self.substop():
            return
        w = load_w(C_QS, 512)
        for j in range(4):
            fm_chunk(w, j * 128, 128, "rope", rows(self.A_qs, j * 128), tables=(self.c_cos_s, self.c_sin_s), pm=self.pm_s)
        if self.substop():
            return
        w = load_w(C_KS, 256)
        fm_chunk(w, 0, 128, "rope", self.A_ks, dcol0=KOFF, tables=(self.c_cos_s, self.c_sin_s), pm=self.pm_s)
        if self.substop():
            return
        for t in range(NT):
            v = vst[t % 2]
            for b in range(4):
                tok0 = t * TS + b * 128
                pt = ps[6 + b % 2]
                prompt = (t == 8)
                c0 = 0 if prompt else 128
                for k in range(8):
                    self.mm(pt[:, c0:256], hT[:, k, tok0:tok0 + 128], w[:, k, c0:256], start=(k == 0), stop=(k == 7))
                if not self.dbg.get("skip_vcopy"):
                    for h2 in range(2):
                        self.copy("act" if t % 2 == 0 else "dve", v[:, b, h2 * 128:h2 * 128 + 64], pt[:, 128 + h2 * 64:192 + h2 * 64])
                if prompt:
                    seq, pos0 = b // 2, (b % 2) * 128
                    o = tmo[b % 2]
                    self.copy("act", o[:, 0:256], pt[:, 0:256])
                    self.dma(self.o_k[seq, l, pos0:pos0 + 128, :], o[:, 0:128])
                    self.dma(self.o_v[seq, l, pos0:pos0 + 128, :], o[:, 128:256])
            if not self.dbg.get("skip_vdma"):
                self.dma(self.A_vs[KOFF + t * TS:KOFF + (t + 1) * TS, :].rr("(b p) c -> p b c", p=128), v)
        if self.substop():
            return
        w = load_w(C_GS, 512)
        for j in range(4):
            fm_chunk(w, j * 128, 128, "silu", rows(self.A_gs, j * 128))
        if self.substop():
            return
        for g in range(6):
            w = load_w(C_MG + g * 512, 512)
            for j in range(4):
                fm_chunk(w, j * 128, 128, "sigmoid", rows(self.A_mg, (g * 4 + j) * 128))

    def p2(self, l):
        ps, PT = self.ps, self.PT
        segs = [(0, 0, NS, True), (NS, 1, NPR, False), (NS + NPR, 2, NPR, False)]
        XPs = [self.tile(f"p2xp{i}", [L_ + 3], F32) for i, L_ in enumerate((NS, NPR, NPR))]
        xcbs = [self.tile(f"p2xcb{i}", [T], BF16) for i in range(2)]
        RAs = [self.tile(f"p2ra{i}", [T], F32, dj=True) for i in range(2)]
        IIs = [self.tile(f"p2ii{i}", [T], BF16, dj=True) for i in range(2)]
        S2s = [self.tile(f"p2s2{i}", [T], F32) for i in range(2)]
        hf = self.tile("p2hf", [T], F32)
        hb = self.tile("p2hb", [T], F32)
        gss = [self.tile(f"p2gs{i}", [T], BF16) for i in range(2)]
        wl = [self.tile(f"p2w{i}", [4, 128], BF16, dj=True) for i in range(2)]
        HS = self.tile("p2hs", [32], F32)
        hso = self.tile("p2hso", [128], F32)
        for xp_ in XPs:
            self.memset("pool", xp_, 0.0)
        cw0, cb0 = self.pcol["conv_w"], self.pcol["conv_b"]
        ba0, bi0, st0 = self.pcol["lru_ba"], self.pcol["lru_bi"], self.pcol["state"]
        st = dict(pi=0)

        def p2loads(n):
            w_ = wl[n % 2]
            for d in range(2):
                self.dma(w_[:, d * 2 + 0, :], self.lru_wa[l, d, n], q="pool")
                self.dma(w_[:, d * 2 + 1, :], self.lru_wi[l, d, n], q="pool")
            for (tok0, xi, L, smp) in segs:
                self.dma(XPs[xi][:, 1:1 + L], self.A_xr[n * 128:(n + 1) * 128, tok0:tok0 + L])

        def p2loads_g(n):
            self.dma(gss[n % 2], self.A_gr[n * 128:(n + 1) * 128, :])

        def conv(n):
            xcb = xcbs[n % 2]
            for (tok0, xi, L, smp) in segs:
                XP = XPs[xi]
                o = xcb[:, tok0:tok0 + L]
                self.ts("dve", o, XP[:, 0:L], PT[:, cw0 + (l * 4 + 0) * 8 + n:cw0 + (l * 4 + 0) * 8 + n + 1],
                        PT[:, cb0 + l * 8 + n:cb0 + l * 8 + n + 1], ALU.mult, ALU.add)
                for k in range(1, 4):
                    c = cw0 + (l * 4 + k) * 8 + n
                    self.stt("dve", o, XP[:, k:k + L], PT[:, c:c + 1], o, ALU.mult, ALU.add)

        def gates(n, d):
            w = wl[n % 2]
            xcb = xcbs[n % 2]
            cc = (l * 2 + d) * 8 + n
            RA, II = RAs[d], IIs[d]
            for t in range(NT):
                pa = ps[st["pi"] % 4]
                pi_ = ps[4 + st["pi"] % 4]
                st["pi"] += 1
                self.mm(pa, w[:, d * 2 + 0, :], xcb[:, self.tcols(t)])
                self.mm(pi_, w[:, d * 2 + 1, :], xcb[:, self.tcols(t)])
                self.act(RA[:, self.tcols(t)], pa, AF.Sigmoid, bias=PT[:, ba0 + cc:ba0 + cc + 1])
                self.act(II[:, self.tcols(t)], pi_, AF.Sigmoid, bias=PT[:, bi0 + cc:bi0 + cc + 1])

        def act_part(n, d):
            cc = (l * 2 + d) * 8 + n
            RA, S2 = RAs[d], S2s[d]
            self.act(RA, RA, AF.Exp, scale=self.C1[:, cc:cc + 1])
            self.act(S2, RA, AF.Square)
            self.act(S2, S2, AF.Sqrt, bias=self.CST[:, 1:2], scale=-1.0)

        def dve_part(n, d):
            cc = (l * 2 + d) * 8 + n
            RA, II, S2, xcb = RAs[d], IIs[d], S2s[d], xcbs[n % 2]
            self.tt("dve", II, II, xcb, ALU.mult)
            self.tt("dve", S2, S2, II, ALU.mult)
            h = hf if d == 0 else hb
            for si, (tok0, pb, L, smp) in enumerate(segs):
                init = PT[:, st0 + cc:st0 + cc + 1] if smp else 0.0
                sl = slice(tok0, tok0 + L)
                if d == 0:
                    self.scan(h[:, sl], RA[:, sl], S2[:, sl], init)
                else:
                    self.scan(h[:, sl][:, ::-1], RA[:, sl][:, ::-1], S2[:, sl][:, ::-1], init)
                if not smp:
                    col = (si - 1) * 16 + d * 8 + n
                    src = tok0 + L - 1 if d == 0 else tok0
                    self.copy("dve", HS[:, col:col + 1], h[:, src:src + 1])

        p2loads(0)
        p2loads_g(0)
        p2loads_g(1)
        conv(0)
        p2loads(1)
        gates(0, 0)
        act_part(0, 0)
        gates(0, 1)
        for n in range(8):
            gs = gss[n % 2]
            act_part(n, 1)
            dve_part(n, 0)
            if n + 1 < 8:
                conv(n + 1)
                if n + 2 < 8:
                    p2loads(n + 2)
                gates(n + 1, 0)
                act_part(n + 1, 0)
            dve_part(n, 1)
            if n + 1 < 8:
                gates(n + 1, 1)
            self.tt("dve", hf, hf, hb, ALU.add)
            self.tt("dve", gs, hf, gs, ALU.mult)
            self.dma(self.Y_rnn[n * 128:(n + 1) * 128, :], gs)
            if n + 2 < 8:
                p2loads_g(n + 2)
        pt = ps[0]
        self.transpose(pt[0:32, 0:128], HS, self.ident)
        self.copy("dve", hso[0:32, :], pt[0:32, 0:128])
        for s_ in range(2):
            self.dma(self.o_h[s_, l].rr("d (n p) -> (d n) p", p=128), hso[s_ * 16:(s_ + 1) * 16, :])

    def p3(self, l):
        ps, PT, ident, ones = self.ps, self.PT, self.ident, self.ones
        wqn = self.tile("p3wqn", [3, 8, 64], BF16, dj=True)
        wqr = self.tile("p3wqr", [3, 8, 32], BF16, dj=True)
        wkn = self.tile("p3wkn", [2, 8, 64], BF16, dj=True)
        wv = self.tile("p3wv", [2, 8, 64], BF16, dj=True)
        wqf = self.tile("p3wqf", [3, 768], BF16)
        wkf = self.tile("p3wkf", [2, 1024], BF16)
        self.dma(wqf, self.mla_w_uq[l].rr("(k p) c -> p k c", p=128), q="pool")
        self.dma(wkf, self.mla_w_ukv[l].rr("(k p) c -> p k c", p=128), q="pool")
        ci_ = 0
        for h in range(8):
            for k in range(3):
                self.copy("dve" if ci_ % 2 else "act", wqn[:, k, h, :], wqf[:, k, h * 96:h * 96 + 64])
                self.copy("act" if ci_ % 2 else "dve", wqr[:, k, h, :], wqf[:, k, h * 96 + 64:h * 96 + 96])
                ci_ += 1
            for k in range(2):
                self.copy("dve" if ci_ % 2 else "act", wkn[:, k, h, :], wkf[:, k, h * 128:h * 128 + 64])
                self.copy("act" if ci_ % 2 else "dve", wv[:, k, h, :], wkf[:, k, h * 128 + 64:h * 128 + 128])
                ci_ += 1
        wqn2 = wqn.rr("p k h r -> p k (h r)")
        wqr2 = wqr.rr("p k h r -> p k (h r)")
        wkn2 = wkn.rr("p k h r -> p k (h r)")
        wv2 = wv.rr("p k h r -> p k (h r)")
        if self.substop():
            return
        cin = self.tile("p3cin", [2, 256], F32)
        cT = self.tile("p3cT", [2, 256], BF16)
        self.dma(cin, self.cache_ckv[l].rr("(b p) f -> p b f", p=128))
        for kf in range(2):
            pt = ps[kf]
            for b in range(2):
                self.transpose(pt[:, b * 128:(b + 1) * 128], cin[:, b, kf * 128:(kf + 1) * 128], ident)
            self.copy("dve", cT[:, kf, :], pt[:, 0:256])
        stg = [self.tile(f"p3st{i}", [TS], BF16, dj=True) for i in range(4)]
        vst = [self.tile(f"p3vs{i}", [4, 8, 128], BF16) for i in range(2)]
        self.memset("act_ms", vst[0], 1.0)
        self.memset("dve", vst[1], 1.0)
        sti = 0
        for j in range(4):
            pk = ps[2 + j % 2]
            for k in range(2):
                self.mm(pk[:, 0:256], wkn2[:, k, j * 128:(j + 1) * 128], cT[:, k, :], start=(k == 0), stop=(k == 1))
            s = stg[sti % 4]
            sti += 1
            self.copy("act", s[:, 0:256], pk[:, 0:256])
            self.dma(self.KTN[j * 128:(j + 1) * 128, 0:256], s[:, 0:256])
        v = vst[0]
        for b in range(2):
            pv = ps[4 + b]
            for k in range(2):
                self.mm(pv, cT[:, k, b * 128:(b + 1) * 128], wv2[:, k, :], start=(k == 0), stop=(k == 1))
            for h8 in range(8):
                self.copy("act", v[:, b, h8, 0:64], pv[:, h8 * 64:(h8 + 1) * 64])
        self.dma(self.VM[0:256, :].rr("(b p) c -> p b c", p=128), v[:, 0:2].rr("p b h c -> p b (h c)"))
        if self.substop():
            return
        krin = self.tile("p3kri", [2, 32], F32)
        krs = self.tile("p3krs", [256], BF16)
        self.dma(krin, self.cache_kr[l].rr("(b p) f -> p b f", p=128))
        pt = ps[6]
        for b in range(2):
            self.transpose(pt[0:32, b * 128:(b + 1) * 128], krin[:, b, :], ident)
        self.copy("act", krs[0:32, :], pt[0:32, 0:256])
        self.dma(self.KR[:, 0:256], krs[0:32, :])
        if self.substop():
            return
        skin = self.tile("p3ski", [2, 128], F32)
        sks = self.tile("p3sks", [256], BF16)
        self.dma(skin, self.cache_k[l].rr("(b p) f -> p b f", p=128))
        pt = ps[7]
        for b in range(2):
            self.transpose(pt[:, b * 128:(b + 1) * 128], skin[:, b, :], ident)
        self.copy("dve", sks, pt[:, 0:256])
        self.dma(self.A_ks[:, 0:256], sks)
        svin = self.tile("p3svi", [2, 128], F32)
        svs = self.tile("p3svs", [2, 2, 128], BF16)
        self.memset("pool", svs, 1.0)
        self.dma(svin, self.cache_v[l].rr("(b p) f -> p b f", p=128))
        for b2 in range(2):
            for h2 in range(2):
                self.copy("dve", svs[:, b2, h2, 0:64], svin[:, b2, h2 * 64:(h2 + 1) * 64])
        self.dma(self.A_vs[0:256, :].rr("(b p) c -> p b c", p=128), svs.rr("p b h c -> p b (h c)"))
        if self.substop():
            return
        cq = [self.tile(f"p3cq{i}", [3, TS], F32) for i in range(2)]
        ck = [self.tile(f"p3ck{i}", [2, TS], F32) for i in range(2)]
        sq = self.tile("p3sq", [3, TS], F32)
        rs = [self.tile(f"p3rs{i}", [TS], F32) for i in range(2)]
        cqn = [self.tile(f"p3cqn{i}", [3, TS], BF16) for i in range(2)]
        ckn = [self.tile(f"p3ckn{i}", [2, TS], BF16) for i in range(2)]
        cs = [self.tile(f"p3cs{i}", [TS], F32) for i in range(2)]
        sn = [self.tile(f"p3sn{i}", [TS], F32) for i in range(2)]
        qf = [self.tile(f"p3qf{i}", [TS], F32) for i in range(2)]
        t1 = [self.tile(f"p3t1{i}", [TS], F32) for i in range(2)]
        t2 = [self.tile(f"p3t2{i}", [TS], F32) for i in range(2)]
        A_cq3 = self.A_cq.rr("(k p) t -> p k t", p=128)
        A_ckv3 = self.A_ckv.rr("(k p) t -> p k t", p=128)
        qn0, kvn0 = self.pcol["q_norm"] + l * 3, self.pcol["kv_norm"] + l * 2
        pi = 0
        ri = 0
        def p3loads(t):
            self.dma(cq[t % 2], A_cq3[:, :, self.tcols(t)])
            self.dma(ck[t % 2], A_ckv3[:, :, self.tcols(t)])

        p3loads(0)
        for t in range(NT):
            tc_ = self.tcols(t)
            q_ = cq[t % 2]
            c_ = ck[t % 2]
            if t + 1 < NT:
                p3loads(t + 1)
            self.act(sq, q_, AF.Square)
            pss = ps[6]
            for k in range(3):
                self.mm(pss, ones, sq[:, k, :], start=(k == 0), stop=(k == 2))
            r = rs[0]
            self.act(r, pss, AF.Ln, bias=self.CST[:, 0:1], scale=1.0 / 384)
            self.act(r, r, AF.Exp, scale=-0.5)
            qn = cqn[t % 2]
            for k in range(3):
                self.stt("dve" if k != 1 else "pool", qn[:, k, :], q_[:, k, :], PT[:, qn0 + k:qn0 + k + 1], r, ALU.mult, ALU.mult)
            self.act(sq[:, 0:2, :], c_, AF.Square)
            pss = ps[7]
            for k in range(2):
                self.mm(pss, ones, sq[:, k, :], start=(k == 0), stop=(k == 1))
            r = rs[1]
            self.act(r, pss, AF.Ln, bias=self.CST[:, 0:1], scale=1.0 / 256)
            self.act(r, r, AF.Exp, scale=-0.5)
            kn = ckn[t % 2]
            for k in range(2):
                self.stt("dve" if k == 0 else "pool", kn[:, k, :], c_[:, k, :], PT[:, kvn0 + k:kvn0 + k + 1], r, ALU.mult, ALU.mult)
            for j in range(4):
                pb = ps[pi % 4]
                pi += 1
                for k in range(3):
                    self.mm(pb, wqn2[:, k, j * 128:(j + 1) * 128], qn[:, k, :], start=(k == 0), stop=(k == 2))
                s = stg[sti % 4]
                sti += 1
                self.copy("act" if j % 2 == 0 else "dve", s, pb)
                self.dma(self.QTN[j * 128:(j + 1) * 128, tc_], s)
            for j in range(2):
                pb = ps[pi % 4]
                pi += 1
                for k in range(3):
                    self.mm(pb, wqr2[:, k, j * 128:(j + 1) * 128], qn[:, k, :], start=(k == 0), stop=(k == 2))
                s = stg[sti % 4]
                sti += 1
                if t == 8:
                    self.copy("act", s, pb)
                else:
                    i = ri % 2
                    ri += 1
                    if j == 0:
                        self.dma(cs[i], self.c_cos_m[:, tc_])
                        self.dma(sn[i], self.c_sin_m[:, tc_])
                        csn = (cs[i], sn[i])
                    self.copy("act", qf[i], pb)
                    pr = ps[4 + i]
                    self.mm(pr, self.pm_m, qf[i])
                    self.tt("dve", t1[i], qf[i], csn[0], ALU.mult)
                    self.tt("dve", t2[i], pr, csn[1], ALU.mult)
                    self.tt("pool", s, t1[i], t2[i], ALU.add)
                self.dma(self.QTR[j * 128:(j + 1) * 128, tc_], s)
            for j in range(4):
                pb = ps[pi % 4]
                pi += 1
                for k in range(2):
                    self.mm(pb, wkn2[:, k, j * 128:(j + 1) * 128], kn[:, k, :], start=(k == 0), stop=(k == 1))
                s = stg[sti % 4]
                sti += 1
                self.copy("act" if j % 2 == 0 else "dve", s, pb)
                self.dma(self.KTN[j * 128:(j + 1) * 128, KOFF + t * TS:KOFF + (t + 1) * TS], s)
            v = vst[t % 2]
            for b in range(4):
                pv = ps[pi % 4]
                pi += 1
                for k in range(2):
                    self.mm(pv, kn[:, k, b * 128:(b + 1) * 128], wv2[:, k, :], start=(k == 0), stop=(k == 1))
                for h8 in range(8):
                    self.copy("act" if t % 2 == 0 else "dve", v[:, b, h8, 0:64], pv[:, h8 * 64:(h8 + 1) * 64])
            self.dma(self.VM[KOFF + t * TS:KOFF + (t + 1) * TS, :].rr("(b p) c -> p b c", p=128), v.rr("p b h c -> p b (h c)"))

    def run_pipe(self, items, la=3, pd=2):
        n = len(items)
        pend = []
        for i in range(n + la):
            if i < n:
                items[i]["score"](i)
            j = i - la
            if j >= 0:
                items[j]["pv"](j)
                if items[j].get("post"):
                    pend.append([pd, items[j]["post"]])
            npend = []
            for p in pend:
                p[0] -= 1
                if p[0] <= 0:
                    p[1]()
                else:
                    npend.append(p)
            pend = npend
        for p in pend:
            p[1]()

    def p4(self, l):
        ps = self.ps
        seqs = [(0, NS, 0, NS + KOFF), (NS, NPR, NS + KOFF, NPR), (NS + NPR, NPR, NS + NPR + KOFF, NPR)]
        Vt = self.tile("p4vt", [(NS + KOFF) // 128, 1024], BF16)
        Kt = [self.tile(f"p4kt{i}", [NS + KOFF], BF16) for i in range(2)]
        Qt = [self.tile(f"p4qt{i}", [TS], BF16) for i in range(3)]
        gm = [self.tile(f"p4gm{i}", [NS], BF16) for i in range(4)]
        yst = [self.tile(f"p4ys{i}", [NS], BF16, dj=True) for i in range(2)]
        pt = [self.tile(f"p4pt{i}", [TS], BF16) for i in range(4)]
        osb = [self.tile(f"p4os{i}", [TS], F32) for i in range(2)]
        y32 = [self.tile(f"p4y{i}", [TS], F32) for i in range(2)]
        items = []
        load_fns = []
        hq = 0
        VtP = [self.tile(f"p4vtp{i}", [2, 1024], BF16) for i in range(2)]
        for si_, (tok0, L, kb, nk) in enumerate(seqs):
            nkc = nk // 128
            vt = Vt[:, 0:nkc, :] if si_ == 0 else VtP[si_ - 1]
            for h in range(8):
                kt = Kt[h % 2]
                g_ = gm[h % 4]
                ys = yst[h % 2]
                nqt = max(1, L // TS)
                nq = min(TS, L)
                for qt in range(nqt):
                    q_ = Qt[hq % 3]
                    po = ps[4 + hq % 2]
                    ob = osb[hq % 2]
                    yb = y32[hq % 2]
                    hq += 1
                    q0 = tok0 + qt * TS

                    def loads(first_h=(h == 0 and qt == 0), first_q=(qt == 0), vt=vt, kt=kt, g_=g_, q_=q_, h=h, kb=kb, nk=nk,
                              tok0=tok0, L=L, q0=q0, nq=nq):
                        if first_h:
                            self.dma(vt, self.VM[kb:kb + nk, :].rr("(c p) f -> p c f", p=128))
                        if first_q:
                            self.dma(kt[0:64, 0:nk].sub(kt.reg + "n"), self.KTN[h * 64:(h + 1) * 64, kb:kb + nk])
                            self.dma(kt[64:96, 0:nk].sub(kt.reg + "r"), self.KR[:, kb:kb + nk])
                            self.dma(g_[0:64, 0:L], self.A_gm[h * 64:(h + 1) * 64, tok0:tok0 + L])
                        self.dma(q_[0:64, 0:nq].sub(q_.reg + "n"), self.QTN[h * 64:(h + 1) * 64, q0:q0 + nq])
                        self.dma(q_[64:96, 0:nq].sub(q_.reg + "r"), self.QTR[h * 32:(h + 1) * 32, q0:q0 + nq])

                    load_fns.append(loads)
                    gi = len(load_fns) - 1
                    for c in range(nkc):
                        def score(i, c=c, kt=kt, q_=q_, nq=nq, gi=gi):
                            if c == 0:
                                if gi == 0:
                                    load_fns[0]()
                                if gi + 1 < len(load_fns):
                                    load_fns[gi + 1]()
                            pb = ps[i % 4]
                            self.S.add("pe", lambda e: e.matmul(pb.ap[:, 0:nq], lhsT=kt.ap[0:96, c * 128:(c + 1) * 128],
                                                                rhs=q_.ap[0:96, 0:nq], start=True, stop=True),
                                       reads=[kt.reg + "n", kt.reg + "r", q_.reg + "n", q_.reg + "r"], writes=[pb.reg])
                            self.act(pt[i % 4][:, 0:nq], pb[:, 0:nq], AF.Exp, scale=MLA_SCALE)

                        def pv(i, c=c, vt=vt, h=h, po=po, nq=nq, nkc=nkc):
                            self.mm(po[:, 0:nq], vt[:, c, h * 128:(h + 1) * 128], pt[i % 4][:, 0:nq], start=(c == 0), stop=(c == nkc - 1))

                        it = dict(score=score, pv=pv)
                        if c == nkc - 1:
                            def post(po=po, ob=ob, yb=yb, ys=ys, g_=g_, nq=nq, qt=qt, h=h, tok0=tok0, L=L, last=(qt == nqt - 1)):
                                self.copy("dve", ob[0:64, 0:nq], po[64:128, 0:nq])
                                self.recip(ob[0:64, 0:nq], ob[0:64, 0:nq])
                                self.tt("dve", yb[0:64, 0:nq], po[0:64, 0:nq], ob[0:64, 0:nq], ALU.mult)
                                self.tt("pool", ys[0:64, qt * TS:qt * TS + nq], yb[0:64, 0:nq], g_[0:64, qt * TS:qt * TS + nq], ALU.mult)
                                if last:
                                    self.dma(self.Y_mla[h * 64:(h + 1) * 64, tok0:tok0 + L], ys[0:64, 0:L])
                            it["post"] = post
                        items.append(it)
        self.run_pipe(items)

    def p5(self, l):
        ps = self.ps
        SK, ES, ZR = self.SK, self.ES, self.ZR
        for h in range(8):
            kvh, g = h // 4, h % 4
            c = l * 8 + h
            self.act(SK[64:128, kvh, g * 128:(g + 1) * 128], ZR[64:128, 0:128], AF.Identity, bias=ES[64:128, c:c + 1])
        seqs = [(0, NS, 0, NS + KOFF, True), (NS, NPR, NS + KOFF, NPR, False), (NS + NPR, NPR, NS + NPR + KOFF, NPR, False)]
        Ks = self.tile("p5ks", [NS + KOFF], BF16)
        Vs = self.tile("p5vs", [(NS + KOFF) // 128, 256], BF16)
        Qt = [self.tile(f"p5qt{i}", [4, TS], BF16, dj=True) for i in range(2)]
        gsb = [self.tile(f"p5gs{i}", [2, 4, TS], BF16, dj=True) for i in range(3)]
        KsP = self.tile("p5ksp", [2 * NPR], BF16)
        VsP = self.tile("p5vsp", [(2 * NPR) // 128, 256], BF16)
        yst = [self.tile(f"p5ys{i}", [2, 4, TS], BF16, dj=True) for i in range(2)]
        pt = [self.tile(f"p5pt{i}", [TS], BF16) for i in range(4)]
        osb = [self.tile(f"p5os{i}", [TS], F32) for i in range(2)]
        rsb = [self.tile(f"p5rs{i}", [TS], F32) for i in range(2)]
        y32 = [self.tile(f"p5y{i}", [TS], F32) for i in range(2)]
        items = []
        load_fns = []
        grp = 0
        for t in range(NT):
            q_ = Qt[t % 2]
            gs_ = gsb[t % 3]
            ys = yst[t % 2]
            Kc, Vc = (Ks, Vs) if t < 8 else (KsP, VsP)

            def loads(t=t, q_=q_, gs_=gs_):
                tc_ = self.tcols(t)
                if t == 0 or t == 8:
                    kb, nk = (0, NS + KOFF) if t == 0 else (NS + KOFF, 2 * NPR)
                    Kc_, Vc_ = (Ks, Vs) if t == 0 else (KsP, VsP)
                    self.dma(Kc_[:, 0:nk], self.A_ks[:, kb:kb + nk])
                    self.dma(Vc_[:, 0:nk // 128, :], self.A_vs[kb:kb + nk, :].rr("(c p) f -> p c f", p=128))
                for kvh in range(2):
                    self.dma(q_[kvh * 64:(kvh + 1) * 64], self.A_qs[kvh * 256:(kvh + 1) * 256, tc_].rr("(g d) t -> d g t", d=64))
                    self.dma(gs_[0:64, kvh], self.A_gs[kvh * 256:(kvh + 1) * 256, tc_].rr("(g d) t -> d g t", d=64))

            load_fns.append(loads)
            gi = len(load_fns) - 1
            first_of_tile = True
            for qb in range(4):
                if t < 8:
                    n = t * 4 + qb
                    chunks = [(0, None), (1, None)]
                    for b, mk in ((n - 1, "prev"), (n, None), (n + 1, "next")):
                        if 0 <= b < NS // 128:
                            chunks.append((2 + b, mk))
                else:
                    sq_ = qb // 2
                    chunks = [(sq_ * 2, None), (sq_ * 2 + 1, None)]
                for kvh in range(2):
                    po = ps[4 + grp % 2]
                    ob = osb[grp % 2]
                    rb = rsb[grp % 2]
                    yb = y32[grp % 2]
                    grp += 1
                    nch = len(chunks)
                    for ci, (kc, mk) in enumerate(chunks):
                        def score(i, kc=kc, mk=mk, kvh=kvh, qb=qb, q_=q_, gi=gi, Ks=Kc, first=(first_of_tile and ci == 0)):
                            if first:
                                if gi == 0:
                                    load_fns[0]()
                                if gi + 1 < len(load_fns):
                                    load_fns[gi + 1]()
                            pb = ps[i % 4]
                            p0_, p1_ = kvh * 64, (kvh + 1) * 64
                            self.S.add("pe", lambda e: e.matmul(pb.ap.rearrange("p (g i) -> p g i", g=4),
                                                                lhsT=Ks.ap[p0_:p1_, kc * 128:(kc + 1) * 128],
                                                                rhs=q_.ap[p0_:p1_, :, qb * 128:(qb + 1) * 128], start=True, stop=True),
                                       reads=[Ks.reg, q_.reg], writes=[pb.reg])
                            self.act(pt[i % 4], pb, AF.Exp, scale=SWA_SCALE)
                            if mk is not None:
                                self.tt("pool", pt[i % 4], pt[i % 4], self.mprev if mk == "prev" else self.mnext, ALU.mult)

                        def pv(i, ci=ci, kc=kc, kvh=kvh, po=po, nch=nch, Vs=Vc):
                            self.mm(po, Vs[:, kc, kvh * 128:(kvh + 1) * 128], pt[i % 4], start=(ci == 0), stop=(ci == nch - 1))

                        it = dict(score=score, pv=pv)
                        first_of_tile = False
                        if ci == nch - 1:
                            def post(po=po, ob=ob, rb=rb, yb=yb, ys=ys, gs_=gs_, kvh=kvh, qb=qb, t=t, last=(qb == 3 and kvh == 1)):
                                self.tt("dve", ob[64:128, :], po[64:128, :], SK[64:128, kvh, :], ALU.add)
                                self.act(ob[64:128, :], ob[64:128, :], AF.Ln)
                                self.act(rb[0:64, :], ob[64:128, :], AF.Exp, scale=-1.0)
                                self.tt("dve", yb[0:64, :], po[0:64, :], rb[0:64, :], ALU.mult)
                                self.tt("pool", ys[0:64, kvh, :, qb * 128:(qb + 1) * 128], yb[0:64, :].rr("p (g i) -> p g i", g=4),
                                        gs_[0:64, kvh, :, qb * 128:(qb + 1) * 128], ALU.mult)
                                if last:
                                    for kv2 in range(2):
                                        self.dma(self.Y_swa[kv2 * 256:(kv2 + 1) * 256, self.tcols(t)].rr("(g d) t -> d g t", d=64),
                                                 ys[0:64, kv2])
                            it["post"] = post
                        items.append(it)
        self.run_pipe(items)

    def p6(self, l):
        ps, MOD = self.ps, self.MOD
        wr = self.tile("p6wr", [8, D], BF16)
        wm = self.tile("p6wm", [4, D], BF16)
        ws = self.tile("p6ws", [4, D], BF16)
        wo = self.tile("p6wo", [8, D], BF16)
        self.dma(wr, self.w_br_rnn[l].rr("(k p) c -> p k c", p=128), q="pool")
        self.dma(wm, self.w_br_mla[l].rr("(k p) c -> p k c", p=128), q="pool")
        self.dma(ws, self.w_br_swa[l].rr("(k p) c -> p k c", p=128), q="pool")
        self.dma(wo, self.w_out[l].rr("(k p) c -> p k c", p=128), q="pool")
        yr = [self.tile(f"p6yr{i}", [8, TS], BF16) for i in range(2)]
        ym = [self.tile(f"p6ym{i}", [4, TS], BF16) for i in range(2)]
        ysw = [self.tile(f"p6ys{i}", [4, TS], BF16) for i in range(2)]
        xt = [self.tile(f"p6xt{i}", [8, TS], F32) for i in range(2)]
        mg = [self.tile(f"p6mg{i}", [3, TS], BF16) for i in range(3)]
        U = [self.tile(f"p6u{i}", [8, TS], BF16) for i in range(2)]
        u1 = [self.tile(f"p6a{i}", [TS], BF16) for i in range(2)]
        u2 = [self.tile(f"p6b{i}", [TS], BF16) for i in range(2)]
        u3 = [self.tile(f"p6c{i}", [TS], BF16) for i in range(2)]
        e1 = [self.tile(f"p6e{i}", [TS], BF16) for i in range(2)]
        e2 = [self.tile(f"p6f{i}", [TS], BF16) for i in range(2)]
        e3 = [self.tile(f"p6g{i}", [TS], BF16) for i in range(2)]
        Yr3 = self.Y_rnn.rr("(k p) t -> p k t", p=128)
        Ym3 = self.Y_mla.rr("(k p) t -> p k t", p=128)
        Ys3 = self.Y_swa.rr("(k p) t -> p k t", p=128)
        Mg4 = self.A_mg.rr("(i j p) t -> p j i t", p=128, i=3)
        pi = 0
        mi = 0
        def p6loads(t):
            tc2 = self.tcols(t)
            self.dma(yr[t % 2], Yr3[:, :, tc2])
            self.dma(ym[t % 2], Ym3[:, :, tc2])
            self.dma(ysw[t % 2], Ys3[:, :, tc2])
            self.dma(xt[t % 2], self.XTv[:, :, tc2])

        p6loads(0)
        for t in range(NT):
            ci = 0 if t < 8 else 1
            tc_ = self.tcols(t)
            a, b, c, x, u = yr[t % 2], ym[t % 2], ysw[t % 2], xt[t % 2], U[t % 2]
            if t + 1 < NT:
                p6loads(t + 1)
            for j in range(8):
                m = mg[mi % 3]
                mi += 1
                self.dma(m, Mg4[:, j, :, tc_])
                jc = slice(j * 128, (j + 1) * 128)
                p1_, p2_, p3_ = ps[pi % 6], ps[(pi + 1) % 6], ps[(pi + 2) % 6]
                pi += 3
                for k in range(8):
                    self.mm(p1_, wr[:, k, jc], a[:, k, :], start=(k == 0), stop=(k == 7))
                for k in range(4):
                    self.mm(p2_, wm[:, k, jc], b[:, k, :], start=(k == 0), stop=(k == 3))
                for k in range(4):
                    self.mm(p3_, ws[:, k, jc], c[:, k, :], start=(k == 0), stop=(k == 3))
                i2 = j % 2
                self.copy("act", e1[i2], p1_)
                self.copy("act", e2[i2], p2_)
                self.copy("act", e3[i2], p3_)
                self.tt("dve", u1[i2], e1[i2], m[:, 0, :], ALU.mult)
                self.tt("dve", u2[i2], e2[i2], m[:, 1, :], ALU.mult)
                self.tt("dve", u3[i2], e3[i2], m[:, 2, :], ALU.mult)
                self.tt("pool", u1[i2], u1[i2], u2[i2], ALU.add)
                self.tt("pool", u[:, j, :], u1[i2], u3[i2], ALU.add)
            for j in range(8):
                jc = slice(j * 128, (j + 1) * 128)
                po = ps[6 + j % 2]
                for k in range(8):
                    self.mm(po, wo[:, k, jc], u[:, k, :], start=(k == 0), stop=(k == 7))
                self.stt("dve", x[:, j, :], po, MOD[:, l, ci, 16 + j:17 + j], x[:, j, :], ALU.mult, ALU.add)
            self.dma(self.XTv[:, :, tc_], x)


_CACHE = {}


def _get_program(dbg=None):
    key = repr(sorted((dbg or {}).items()))
    if key not in _CACHE:
        b = Builder(dbg)
        b.build()
        _CACHE[key] = b
    return _CACHE[key]


W_NAMES = ["w_mod", "b_mod", "g_norm", "w_in", "conv_w", "conv_b", "lru_wa", "lru_ba", "lru_wi", "lru_bi", "lru_lam",
           "mla_q_norm", "mla_w_uq", "mla_kv_norm", "mla_w_ukv", "swa_sink", "w_br_rnn", "w_br_mla", "w_br_swa",
           "w_out", "final_norm"]


def make_in_maps(inp, cores):
    consts = _consts()
    shared = {k: np.ascontiguousarray(np.asarray(inp[k], dtype=np.float32)) for k in W_NAMES}
    shared.update(consts)
    maps = []
    for b in cores:
        m = dict(shared)
        m["x_all"] = np.ascontiguousarray(np.concatenate(
            [np.asarray(inp["x_sample"][b]), np.asarray(inp["x_prompt"][2 * b]), np.asarray(inp["x_prompt"][2 * b + 1])], axis=0),
            dtype=np.float32)
        m["cond"] = np.ascontiguousarray(np.stack([np.asarray(inp["c"][b]), np.asarray(inp["c_ctx"])], axis=0), dtype=np.float32)
        m["cache_ckv"] = np.ascontiguousarray(np.asarray(inp["cache_mla_ckv"][b]), dtype=np.float32)
        m["cache_kr"] = np.ascontiguousarray(np.asarray(inp["cache_mla_krope"][b]), dtype=np.float32)
        m["cache_k"] = np.ascontiguousarray(np.asarray(inp["cache_swa_k"][b]).reshape(DEPTH, 256, 128), dtype=np.float32)
        m["cache_v"] = np.ascontiguousarray(np.asarray(inp["cache_swa_v"][b]).reshape(DEPTH, 256, 128), dtype=np.float32)
        m["state"] = np.ascontiguousarray(np.asarray(inp["state_rglru"][b]), dtype=np.float32)
        maps.append(m)
    return maps


def kernel(**inputs):
    prog = _get_program()
    cores = list(range(8))
    in_maps = make_in_maps(inputs, cores)
    res = run_bass_kernel_spmd(prog.nc, in_maps, core_ids=cores)
    rs = res.results
    y_sample = np.stack([rs[b]["y"][0:NS] for b in cores], axis=0)
    y_prompt = np.concatenate([rs[b]["y"][NS:].reshape(2, NPR, D) for b in cores], axis=0)
    new_ckv = np.concatenate([rs[b]["o_ckv"] for b in cores], axis=0)
    new_kr = np.concatenate([rs[b]["o_kr"] for b in cores], axis=0)
    new_k = np.concatenate([rs[b]["o_k"] for b in cores], axis=0).reshape(16, DEPTH, 256, 2, 64)
    new_v = np.concatenate([rs[b]["o_v"] for b in cores], axis=0).reshape(16, DEPTH, 256, 2, 64)
    new_h = np.concatenate([rs[b]["o_h"] for b in cores], axis=0)
    f = np.float32
    return (y_prompt.astype(f), y_sample.astype(f), new_ckv.astype(f), new_kr.astype(f), new_k.astype(f),
            new_v.astype(f), new_h.astype(f))
```
